# Optimizing a Trainium2 kernel written in Bass

```python
import jax, jax.numpy as jnp
from jax import lax
import numpy as np

D_MODEL = 1024
BATCH = 16
SEQ = 256
DEPTH = 4
DEC_BATCH = 4
DEC_SEQ = 4096
PAST_LEN = 512

GRID_W = 64
N_MIXERS = 3
N_A = (DEPTH + 2) // 3
N_B = (DEPTH + 1) // 3
N_C = DEPTH // 3
N_MOD = 9
D_FF = 2816
CHUNK = 128
D_A = 2 * D_MODEL
G_A = 8
C_A = D_A // G_A
HQ_B = 8
KV_B = 2
HD_B = D_MODEL // HQ_B
HQ_C = 16
KV_C = 2
HD_C = D_MODEL // HQ_C
WINDOW = 128
QBLK = 128
ROPE_THETA = 10000.0
EPS = 1e-6
NEG_INF = -1e30

kernel_name = 'hybrid_flow_prefix_trunk_step'


def rmsnorm(x, g):
    xf = x.astype(jnp.float32)
    y = xf * lax.rsqrt(jnp.mean(xf * xf, axis=-1, keepdims=True) + EPS)
    return (y * g.astype(jnp.float32)).astype(x.dtype)


def modulate(h, shift, scale):
    return h * (1 + scale[:, None, :]) + shift[:, None, :]


def ada_params(cond, w_mod, b_mod):
    m = jax.nn.silu(cond) @ w_mod + b_mod
    return m.reshape(cond.shape[0], N_MOD, D_MODEL)


def swiglu(h, w_gate, w_up, w_down):
    return (jax.nn.silu(h @ w_gate) * (h @ w_up)) @ w_down


def ffn_half_step(x, mod, k, g, w_gate, w_up, w_down):
    h = modulate(rmsnorm(x, g), mod[:, k], mod[:, k + 1])
    return x + 0.5 * mod[:, k + 2][:, None, :] * swiglu(h, w_gate, w_up, w_down)


def chunk_gating_mlp(h, w_in, norm_g, w_s, b_s, w_out):
    n, t, _ = h.shape
    u, v = jnp.split(h @ w_in, 2, axis=-1)
    v = rmsnorm(v, norm_g).reshape(n, t // CHUNK, CHUNK, G_A, C_A)
    v = jnp.einsum('gpq,bnqgc->bnpgc', w_s, v) + b_s.T[None, None, :, :, None]
    return (u * v.reshape(n, t, D_A)) @ w_out


def axial_rope_tables(n, hd):
    rows = n // GRID_W
    row = jnp.repeat(jnp.arange(rows), GRID_W).astype(jnp.float32)
    col = (jnp.arange(n) % GRID_W).astype(jnp.float32)
    quarter = hd // 4
    inv = ROPE_THETA ** (-jnp.arange(quarter, dtype=jnp.float32) / quarter)
    ang_r = row[:, None] * inv[None, :]
    ang_c = col[:, None] * inv[None, :]
    return jnp.cos(ang_r), jnp.sin(ang_r), jnp.cos(ang_c), jnp.sin(ang_c)


def _rotate(z, cos, sin):
    z1, z2 = jnp.split(z, 2, axis=-1)
    cos = cos[None, :, None, :]
    sin = sin[None, :, None, :]
    return jnp.concatenate([z1 * cos - z2 * sin, z2 * cos + z1 * sin], axis=-1)


def axial_rope(x, cos_r, sin_r, cos_c, sin_c):
    xf = x.astype(jnp.float32)
    half = x.shape[-1] // 2
    out = jnp.concatenate([_rotate(xf[..., :half], cos_r, sin_r), _rotate(xf[..., half:], cos_c, sin_c)], axis=-1)
    return out.astype(x.dtype)


def gqa_softmax(q, k, v, mask, sink):
    kv, g, hd = q.shape[2], q.shape[3], q.shape[4]
    s = jnp.einsum('bqkgd,bskd->bkgqs', q, k).astype(jnp.float32) * (hd ** -0.5)
    if mask is not None:
        s = jnp.where(mask, s, NEG_INF)
    if sink is None:
        p = jax.nn.softmax(s, axis=-1)
    else:
        sk = sink.astype(jnp.float32).reshape(kv, g)[None, :, :, None, None]
        m = jnp.maximum(jnp.max(s, axis=-1, keepdims=True), sk)
        e = jnp.exp(s - m)
        p = e / (jnp.sum(e, axis=-1, keepdims=True) + jnp.exp(sk - m))
    return jnp.einsum('bkgqs,bskd->bqkgd', p.astype(v.dtype), v)


def dense_blocked_attention(q, k, v, sink):
    n, t, hq, hd = q.shape
    kv = k.shape[2]
    nb = t // QBLK
    qb = q.reshape(n, nb, QBLK, kv, hq // kv, hd).swapaxes(0, 1)
    out = lax.map(lambda qi: gqa_softmax(qi, k, v, None, sink), qb)
    return out.swapaxes(0, 1).reshape(n, t, hq * hd)


def banded_attention(q, k, v, ck, cv, sink):
    n, t, hq, hd = q.shape
    kv = k.shape[2]
    g = hq // kv
    nb = t // QBLK
    sc = ck.shape[1]
    band = QBLK + 2 * WINDOW
    pad = ((0, 0), (WINDOW, WINDOW), (0, 0), (0, 0))
    kpad = jnp.pad(k, pad)
    vpad = jnp.pad(v, pad)
    ctx_mask = jnp.ones((QBLK, sc), dtype=bool)

    def block(j):
        start = j * QBLK
        qi = lax.dynamic_slice_in_dim(q, start, QBLK, axis=1).reshape(n, QBLK, kv, g, hd)
        kb = lax.dynamic_slice_in_dim(kpad, start, band, axis=1)
        vb = lax.dynamic_slice_in_dim(vpad, start, band, axis=1)
        qpos = start + jnp.arange(QBLK)
        kpos = start - WINDOW + jnp.arange(band)
        near = (jnp.abs(qpos[:, None] - kpos[None, :]) <= WINDOW) & (kpos[None, :] >= 0) & (kpos[None, :] < t)
        mask = jnp.concatenate([ctx_mask, near], axis=1)
        return gqa_softmax(qi, jnp.concatenate([ck, kb], axis=1), jnp.concatenate([cv, vb], axis=1), mask, sink)

    out = lax.map(block, jnp.arange(nb))
    return out.swapaxes(0, 1).reshape(n, t, hq * hd)


def split_qkv(h, w_qkv, hq, kv, hd):
    n, t, _ = h.shape
    q, k, v = jnp.split(h @ w_qkv, [hq * hd, (hq + kv) * hd], axis=-1)
    return q.reshape(n, t, hq, hd), k.reshape(n, t, kv, hd), v.reshape(n, t, kv, hd)


def full_attn_context(h, w_qkv, q_g, k_g, w_out):
    q, k, v = split_qkv(h, w_qkv, HQ_B, KV_B, HD_B)
    q = rmsnorm(q, q_g)
    k = rmsnorm(k, k_g)
    return dense_blocked_attention(q, k, v, None) @ w_out, k, v


def full_attn_latent(h, ck, cv, w_qkv, q_g, k_g, w_out):
    q, k, v = split_qkv(h, w_qkv, HQ_B, KV_B, HD_B)
    tables = axial_rope_tables(h.shape[1], HD_B)
    q = axial_rope(rmsnorm(q, q_g), *tables)
    k = axial_rope(rmsnorm(k, k_g), *tables)
    o = dense_blocked_attention(q, jnp.concatenate([ck, k], axis=1), jnp.concatenate([cv, v], axis=1), None)
    return o @ w_out


def window_attn_context(h, w_qkv, sink, w_out):
    q, k, v = split_qkv(h, w_qkv, HQ_C, KV_C, HD_C)
    return dense_blocked_attention(q, k, v, sink) @ w_out, k, v


def window_attn_latent(h, ck, cv, w_qkv, sink, w_out):
    q, k, v = split_qkv(h, w_qkv, HQ_C, KV_C, HD_C)
    tables = axial_rope_tables(h.shape[1], HD_C)
    q = axial_rope(q, *tables)
    k = axial_rope(k, *tables)
    return banded_attention(q, k, v, ck, cv, sink) @ w_out


def setup_inputs(seed: int = 0) -> dict:
    key = jax.random.key(seed)
    ks = jax.random.split(key, 32)

    def nrm(k, shape, scale=1.0):
        return scale * jax.random.normal(k, shape, jnp.float32)

    qkv_b = (HQ_B + 2 * KV_B) * HD_B
    qkv_c = (HQ_C + 2 * KV_C) * HD_C
    return {
        'x_prompt': nrm(ks[0], (BATCH, SEQ, D_MODEL)),
        'x_sample': nrm(ks[1], (DEC_BATCH, DEC_SEQ, D_MODEL)),
        'cache_b_k': nrm(ks[2], (DEC_BATCH, N_B, PAST_LEN, KV_B, HD_B)),
        'cache_b_v': nrm(ks[3], (DEC_BATCH, N_B, PAST_LEN, KV_B, HD_B)),
        'cache_c_k': nrm(ks[4], (DEC_BATCH, N_C, PAST_LEN, KV_C, HD_C)),
        'cache_c_v': nrm(ks[5], (DEC_BATCH, N_C, PAST_LEN, KV_C, HD_C)),
        'c': nrm(ks[6], (DEC_BATCH, D_MODEL)),
        'c_ctx': nrm(ks[7], (D_MODEL,)),
        'w_mod': nrm(ks[8], (DEPTH, D_MODEL, N_MOD * D_MODEL), 0.5 * D_MODEL ** -0.5),
        'b_mod': nrm(ks[9], (DEPTH, N_MOD * D_MODEL), 0.01),
        'norm_g': 1.0 + nrm(ks[10], (DEPTH, 3, D_MODEL), 0.02),
        'ffn_w_gate': nrm(ks[11], (DEPTH, 2, D_MODEL, D_FF), D_MODEL ** -0.5),
        'ffn_w_up': nrm(ks[12], (DEPTH, 2, D_MODEL, D_FF), D_MODEL ** -0.5),
        'ffn_w_down': nrm(ks[13], (DEPTH, 2, D_FF, D_MODEL), D_FF ** -0.5),
        'a_w_in': nrm(ks[14], (N_A, D_MODEL, 2 * D_A), D_MODEL ** -0.5),
        'a_norm_g': 1.0 + nrm(ks[15], (N_A, D_A), 0.02),
        'a_w_s': nrm(ks[16], (N_A, G_A, CHUNK, CHUNK), CHUNK ** -0.5),
        'a_b_s': 1.0 + nrm(ks[17], (N_A, G_A, CHUNK), 0.1),
        'a_w_out': nrm(ks[18], (N_A, D_A, D_MODEL), D_A ** -0.5),
        'b_w_qkv': nrm(ks[19], (N_B, D_MODEL, qkv_b), D_MODEL ** -0.5),
        'b_q_g': 1.0 + nrm(ks[20], (N_B, HD_B), 0.02),
        'b_k_g': 1.0 + nrm(ks[21], (N_B, HD_B), 0.02),
        'b_w_out': nrm(ks[22], (N_B, HQ_B * HD_B, D_MODEL), (HQ_B * HD_B) ** -0.5),
        'c_w_qkv': nrm(ks[23], (N_C, D_MODEL, qkv_c), D_MODEL ** -0.5),
        'c_sink': nrm(ks[24], (N_C, HQ_C), 0.5),
        'c_w_out': nrm(ks[25], (N_C, HQ_C * HD_C, D_MODEL), (HQ_C * HD_C) ** -0.5),
        'final_g': 1.0 + nrm(ks[26], (D_MODEL,), 0.02),
    }


def reference(x_prompt, x_sample, cache_b_k, cache_b_v, cache_c_k, cache_c_v, c, c_ctx,
              w_mod, b_mod, norm_g, ffn_w_gate, ffn_w_up, ffn_w_down,
              a_w_in, a_norm_g, a_w_s, a_b_s, a_w_out,
              b_w_qkv, b_q_g, b_k_g, b_w_out,
              c_w_qkv, c_sink, c_w_out, final_g):
    xp, xs = x_prompt, x_sample
    nbk, nbv, nck, ncv = [], [], [], []
    for l in range(DEPTH):
        kind, idx = l % N_MIXERS, l // N_MIXERS
        mp = ada_params(c_ctx[None, :], w_mod[l], b_mod[l])
        ms = ada_params(c, w_mod[l], b_mod[l])
        f0 = (ffn_w_gate[l, 0], ffn_w_up[l, 0], ffn_w_down[l, 0])
        xp = ffn_half_step(xp, mp, 0, norm_g[l, 0], *f0)
        xs = ffn_half_step(xs, ms, 0, norm_g[l, 0], *f0)
        hp = modulate(rmsnorm(xp, norm_g[l, 1]), mp[:, 3], mp[:, 4])
        hs = modulate(rmsnorm(xs, norm_g[l, 1]), ms[:, 3], ms[:, 4])
        if kind == 0:
            a = (a_w_in[idx], a_norm_g[idx], a_w_s[idx], a_b_s[idx], a_w_out[idx])
            op = chunk_gating_mlp(hp, *a)
            os_ = chunk_gating_mlp(hs, *a)
        elif kind == 1:
            op, kp, vp = full_attn_context(hp, b_w_qkv[idx], b_q_g[idx], b_k_g[idx], b_w_out[idx])
            nbk.append(kp)
            nbv.append(vp)
            os_ = full_attn_latent(hs, cache_b_k[:, idx], cache_b_v[:, idx], b_w_qkv[idx], b_q_g[idx], b_k_g[idx], b_w_out[idx])
        else:
            op, kp, vp = window_attn_context(hp, c_w_qkv[idx], c_sink[idx], c_w_out[idx])
            nck.append(kp)
            ncv.append(vp)
            os_ = window_attn_latent(hs, cache_c_k[:, idx], cache_c_v[:, idx], c_w_qkv[idx], c_sink[idx], c_w_out[idx])
        xp = xp + mp[:, 5][:, None, :] * op
        xs = xs + ms[:, 5][:, None, :] * os_
        f1 = (ffn_w_gate[l, 1], ffn_w_up[l, 1], ffn_w_down[l, 1])
        xp = ffn_half_step(xp, mp, 6, norm_g[l, 2], *f1)
        xs = ffn_half_step(xs, ms, 6, norm_g[l, 2], *f1)
    y_prompt = rmsnorm(xp, final_g)
    y_sample = rmsnorm(xs, final_g)
    new_b_k = jnp.stack(nbk, axis=1)
    new_b_v = jnp.stack(nbv, axis=1)
    new_c_k = jnp.stack(nck, axis=1)
    new_c_v = jnp.stack(ncv, axis=1)
    return (y_prompt, y_sample, new_b_k, new_b_v, new_c_k, new_c_v)
```

```python
import numpy as np
import ml_dtypes
import concourse.bass as bass
import concourse.mybir as mybir
from concourse.bass_utils import run_bass_kernel_spmd
from contextlib import ExitStack

F32 = mybir.dt.float32
BF16 = mybir.dt.bfloat16
AF = mybir.ActivationFunctionType
ALU = mybir.AluOpType

D = 1024
NCH = 8
NTOK = 2560
NTG = 5
TG = 512
DFF = 2816
NFF = 22
EPS = 1e-6
DEPTH = 4
FB = 4
NEG = -30000.0


class Buf:
    __slots__ = ("name", "lo", "hi", "w", "r", "aliases")

    def __init__(self, name, lo=None, hi=None):
        self.name = name
        self.lo = lo
        self.hi = hi
        self.w = None
        self.r = {}
        self.aliases = []


class Op:
    __slots__ = ("eng", "fn", "deps", "is_dma", "needs_sig", "sigsem", "sigval")

    def __init__(self, eng, fn, is_dma):
        self.eng = eng
        self.fn = fn
        self.is_dma = is_dma
        self.deps = []
        self.needs_sig = False
        self.sigsem = None
        self.sigval = 0


ENGS = ("pe", "act", "dve", "pool", "sp")
NPOOLSEM = 8


class Prog:
    def __init__(self):
        self.ops = {e: [] for e in ENGS}
        self.arena = []
        self.acache = {}
        self.dma_cnt = {"sp": 0, "pool": 0, "act": 0}
        self.dma_last = {}
        self.cc_cnt = 0

    def abuf(self, name, lo, hi):
        key = (name, lo, hi)
        if key in self.acache:
            return self.acache[key]
        b = Buf(name, lo, hi)
        self.acache[key] = b
        for o in self.arena:
            if o.lo < hi and lo < o.hi:
                o.aliases.append(b)
                b.aliases.append(o)
        self.arena.append(b)
        return b

    def add(self, eng, fn, reads=(), writes=(), dma=False, cc=False, force=()):
        op = Op(eng, fn, dma)
        for d in force:
            d.needs_sig = True
            op.deps.append(d)
        if cc:
            self.cc_cnt += 1
            op.sigsem = ("cc",)
            op.sigval = self.cc_cnt
            op.needs_sig = True
        deps = set()
        for b in reads:
            for s in [b] + b.aliases:
                if s.w is not None:
                    deps.add(s.w)
        for b in writes:
            for s in [b] + b.aliases:
                if s.w is not None:
                    deps.add(s.w)
                deps.update(s.r.values())
        for b in reads:
            b.r[id(op) if dma else eng] = op
        for b in writes:
            b.w = op
            b.r = {}
        if dma:
            i = self.dma_cnt[eng]
            self.dma_cnt[eng] = i + 1
            slot = i % NPOOLSEM
            op.sigsem = ("dma", eng, slot)
            op.sigval = 16 * (i // NPOOLSEM + 1)
            prev = self.dma_last.get((eng, slot))
            if prev is not None:
                deps.add(prev)
            self.dma_last[(eng, slot)] = op
        for d in deps:
            if d is op:
                continue
            if (not d.is_dma) and (not dma) and d.eng == "pe" and eng == "pe":
                continue
            if d.sigsem == ("cc",):
                op.deps.append(d)
                continue
            d.needs_sig = True
            op.deps.append(d)
        self.ops[eng].append(op)
        return op

    def finalize(self):
        for e in ENGS:
            c = 0
            for op in self.ops[e]:
                if not op.is_dma and op.needs_sig and op.sigsem != ("cc",):
                    c += 1
                    op.sigsem = ("eng", e)
                    op.sigval = c

    def emit(self, eng, handle, sems):
        known = {}
        for op in self.ops[eng]:
            need = {}
            for d in op.deps:
                if need.get(d.sigsem, 0) < d.sigval:
                    need[d.sigsem] = d.sigval
            for k, v in need.items():
                if known.get(k, 0) < v:
                    handle.wait_ge(sems[k], v)
                    known[k] = v
            ins = op.fn(handle)
            if op.is_dma:
                ins.then_inc(sems[op.sigsem], 16)
            elif op.sigsem == ("cc",):
                ins.then_inc(sems[op.sigsem])
            elif op.needs_sig:
                ins.then_inc(sems[op.sigsem], 1)
        if eng in self.dma_cnt:
            n = self.dma_cnt[eng]
            for slot in range(min(n, NPOOLSEM)):
                cnt = (n - slot + NPOOLSEM - 1) // NPOOLSEM
                k = ("dma", eng, slot)
                if known.get(k, 0) < 16 * cnt:
                    handle.wait_ge(sems[k], 16 * cnt)


class Builder:
    def __init__(self, stage=99, ncores=8):
        self.stage = stage
        self.ncores = ncores
        self.nc = bass.Bass("TRN2", target_bir_lowering=False)
        self.P = Prog()
        self.es = ExitStack()
        self.din = {}
        self.dout = {}

    def inp(self, name, shape, dt=F32):
        t = self.nc.dram_tensor(name, list(shape), dt, kind="ExternalInput")
        self.din[name] = t
        return t.ap()

    def outp(self, name, shape, dt=F32):
        t = self.nc.dram_tensor(name, list(shape), dt, kind="ExternalOutput")
        self.dout[name] = t
        return t.ap()

    def sb(self, name, shape, dt):
        return self.es.enter_context(self.nc.sbuf_tensor(name, list(shape), dt))

    def mm(self, out, lhsT, rhs, start, stop, reads, writes, force=()):
        return self.P.add("pe", lambda e, o=out, l=lhsT, r=rhs, s=start, t=stop:
                          e.matmul(o, l, r, start=s, stop=t), reads, writes, force=force)

    def act(self, out, in_, func, reads, writes, scale=None, bias=None):
        kw = {}
        if scale is not None:
            kw["scale"] = scale
        if bias is not None:
            kw["bias"] = bias
        self.P.add("act", lambda e, o=out, i=in_, f=func, kw=kw:
                   e.activation(o, i, f, **kw), reads, writes)

    def tt(self, out, in0, in1, op, reads, writes, eng="dve"):
        self.P.add(eng, lambda e, o=out, a=in0, b=in1, p=op:
                   e.tensor_tensor(o, a, b, p), reads, writes)

    def stt(self, out, in0, scalar, in1, op0, op1, reads, writes):
        self.P.add("dve", lambda e, o=out, a=in0, s=scalar, b=in1, p0=op0, p1=op1:
                   e.scalar_tensor_tensor(o, a, s, b, p0, p1), reads, writes)

    def ts(self, out, in0, s1, s2, op0, op1, reads, writes, eng="dve"):
        if op1 is None:
            self.P.add(eng, lambda e, o=out, a=in0, x=s1, p0=op0:
                       e.tensor_scalar(o, a, x, None, p0), reads, writes)
        else:
            self.P.add(eng, lambda e, o=out, a=in0, x=s1, y=s2, p0=op0, p1=op1:
                       e.tensor_scalar(o, a, x, y, p0, p1), reads, writes)

    def copy(self, out, in_, reads, writes, eng="dve"):
        self.P.add(eng, lambda e, o=out, i=in_: e.tensor_copy(o, i), reads, writes)

    def recip(self, out, in_, reads, writes):
        self.P.add("dve", lambda e, o=out, i=in_: e.reciprocal(o, i), reads, writes)

    def memset(self, ap, val, writes, eng="dve"):
        self.P.add(eng, lambda e, a=ap, v=val: e.memset(a, v), (), writes)

    def dma(self, out, in_, reads, writes, eng="sp"):
        self.P.add(eng, lambda e, o=out, i=in_: e.dma_start(out=o, in_=i),
                   reads, writes, dma=True)

    def build(self):
        nc = self.nc
        P = self.P
        xT_d = self.inp("xT", [128, NCH * NTOK])
        cT_d = self.inp("cT", [128, NCH * 2])
        fg_d = self.inp("fg", [128, NCH])
        yT_d = self.outp("yT", [128, NCH * NTOK])

        x_t = self.sb("x", [128, NCH, NTOK], F32)
        h_t = self.sb("h", [128, NCH, NTOK], BF16)
        UBYTES = 84992
        U_t = self.sb("U", [128, UBYTES // 2], BF16)
        ones_t = self.sb("ones", [128, 128], BF16)
        cT_t = self.sb("cTs", [128, NCH * 2], F32)
        sc_t = self.sb("scs", [128, NCH, 2], BF16)
        self.modT2 = [self.sb(f"modT{i}", [128, 72, 2], F32) for i in range(2)]
        self.bm2 = [self.sb(f"bms{i}", [128, 72], F32) for i in range(2)]
        self.gn2 = [self.sb(f"gns{i}", [128, 3, NCH], F32) for i in range(2)]
        fg_t = self.sb("fgs", [128, NCH], F32)
        self.A2 = [self.sb(f"As{i}", [128, 3, NCH, 2], F32) for i in range(2)]
        self.G2 = [self.sb(f"Gs{i}", [128, 3, NCH, 2], F32) for i in range(2)]
        self.bm_b2 = [Buf(f"bm{i}") for i in range(2)]
        self.gn_b2 = [Buf(f"gn{i}") for i in range(2)]
        self.modT_b2 = [Buf(f"modT{i}") for i in range(2)]
        self.A_b2 = [Buf(f"A{i}") for i in range(2)]
        self.G_b2 = [Buf(f"G{i}") for i in range(2)]
        self.sc_t = sc_t
        eps_t = self.sb("epss", [128, 1], F32)
        self.onef_t = self.sb("onef", [128, 1], F32)
        self.b_onef = Buf("onef")
        self.x_t, self.h_t, self.U_t, self.ones_t = x_t, h_t, U_t, ones_t
        self.eps_t = eps_t

        self.ps = [self.es.enter_context(nc.psum_tensor(f"ps{i}", [128, 512], F32)) for i in range(8)]
        self.psb = [Buf(f"psb{i}") for i in range(8)]

        self.xb = [[Buf(f"x{c}_{t}") for t in range(NTG)] for c in range(NCH)]
        self.hb = [[Buf(f"h{c}_{t}") for t in range(NTG)] for c in range(NCH)]
        b_ones = Buf("ones")
        b_cT, b_sc, b_fg = (Buf(n) for n in ("cT", "sc", "fg"))
        b_eps = Buf("eps")
        self.b_ones, self.b_eps, self.b_sc = b_ones, b_eps, b_sc
        self.mod_items = []

        def U(lo, nbytes, dt, shape_rest):
            es = 2 if dt == BF16 else 4
            n = nbytes // es
            if dt == BF16:
                ap = U_t[:, lo // 2: lo // 2 + n]
            else:
                ap = U_t[:, lo // 2: lo // 2 + 2 * n].bitcast(F32)
            return ap
        self.U = U

        self.memset(ones_t[:], 1.0, [b_ones])
        self.memset(eps_t[:], EPS, [b_eps])
        self.memset(self.onef_t[:], 1.0, [self.b_onef])
        for t in range(NTG):
            self.dma(x_t[:, :, t * TG:(t + 1) * TG],
                     xT_d.rearrange("p (c n) -> p c n", c=NCH)[:, :, t * TG:(t + 1) * TG],
                     [], [self.xb[c][t] for c in range(NCH)])
        self.dma(cT_t[:], cT_d, [], [b_cT])
        self.dma(fg_t[:], fg_d, [], [b_fg])
        self.act(sc_t[:].rearrange("p k c -> p (k c)"), cT_t[:], AF.Silu, [b_cT], [b_sc])

        WS = 8192
        WBASE = 40960
        self.wslot_ap = [U(WBASE + i * WS, WS, BF16, None) for i in range(4)]
        self.wslot_b = [P.abuf(f"wslot{i}", WBASE + i * WS, WBASE + (i + 1) * WS) for i in range(4)]
        self.wctr = 0

        for l in range(DEPTH):
            if l >= self.stage_layers():
                break
            self.inp(f"wgu{l}", [2, NFF, 128, 2 * NCH * 128])
            self.inp(f"wd{l}", [2, NFF, 128, D])
            if l == 0:
                self.mods_schedule(0)
                self.mod_pump(1 + 24 + 1)
            else:
                self.mod_flush()
            self.set_layer(l)
            self.norm_mod(s=0)
            self.ffn(l, 0)
            if self.stage_partial(l, 1):
                break
            self.norm_mod(s=1)
            kind = l % 3
            if kind == 0:
                self.mixer_a(l, l // 3)
            else:
                self.attn_layer(l, "B" if kind == 1 else "C")
            if self.stage_partial(l, 2):
                break
            self.norm_mod(s=2)
            if l + 1 < self.stage_layers():
                self.mods_schedule(l + 1)
            self.ffn(l, 1)

        if self.stage >= 99:
            self.final_norm(fg_t, b_fg)
        for t in range(NTG):
            self.dma(yT_d.rearrange("p (c n) -> p c n", c=NCH)[:, :, t * TG:(t + 1) * TG],
                     x_t[:, :, t * TG:(t + 1) * TG],
                     [self.xb[c][t] for c in range(NCH)], [])

        P.finalize()
        sems = {}
        for e in ENGS:
            sems[("eng", e)] = self.es.enter_context(nc.semaphore(f"s_{e}"))
        for e in ("sp", "pool"):
            for i in range(NPOOLSEM):
                sems[("dma", e, i)] = self.es.enter_context(nc.semaphore(f"d_{e}{i}"))
        sems[("cc",)] = self.es.enter_context(nc.semaphore("s_cc"))
        with nc.Block() as block:
            @block.tensor
            def _(t):
                P.emit("pe", t, sems)

            @block.scalar
            def _(a):
                P.emit("act", a, sems)

            @block.vector
            def _(v):
                P.emit("dve", v, sems)

            @block.gpsimd
            def _(g):
                P.emit("pool", g, sems)

            @block.sync
            def _(s):
                P.emit("sp", s, sems)
        self.es.close()
        return nc

    def set_layer(self, l):
        i = l % 2
        self.modT_t, self.A_t, self.G_t = self.modT2[i], self.A2[i], self.G2[i]
        self.b_modT, self.b_A, self.b_G = self.modT_b2[i], self.A_b2[i], self.G_b2[i]

    def mods_schedule(self, l):
        P, U = self.P, self.U
        i = l % 2
        wm_d = self.inp(f"wm{l}", [72, 128, NCH * 128])
        bm_d = self.inp(f"bm{l}", [128, 72])
        gn_d = self.inp(f"gn{l}", [128, 3 * NCH])
        modT_t, bm_t, gn_t, A_t, G_t = self.modT2[i], self.bm2[i], self.gn2[i], self.A2[i], self.G2[i]
        b_modT, b_bm, b_gn, b_A, b_G = self.modT_b2[i], self.bm_b2[i], self.gn_b2[i], self.A_b2[i], self.G_b2[i]
        pm = self.ps[7]
        mslot_ap = [U(77824 + j * 2048, 2048, BF16, None).rearrange("p (k c) -> p k c", k=NCH) for j in range(2)]
        mslot_b = [P.abuf(f"mslot{j}", 77824 + j * 2048, 77824 + (j + 1) * 2048) for j in range(2)]
        items = []

        def first():
            self.dma(bm_t[:], bm_d, [], [b_bm])
            self.dma(gn_t[:].rearrange("p s c -> p (s c)"), gn_d, [], [b_gn])
            load(0)
            load(1)

        def load(fc):
            j = fc % 2
            self.dma(mslot_ap[j], wm_d[fc].rearrange("p (k c) -> p k c", k=NCH), [], [mslot_b[j]], eng="pool")

        def work(fc):
            j = fc % 2
            for k in range(NCH):
                self.mm(pm[:, fc * 2:fc * 2 + 2], mslot_ap[j][:, k, :], self.sc_t[:, k, :],
                        k == 0, k == NCH - 1, [mslot_b[j], self.b_sc], [self.psb[7]])
            if fc + 2 < 72:
                load(fc + 2)

        def final(s_):
            c0, c1 = 24 * s_, 24 * (s_ + 1)
            for ctx in range(2):
                self.tt(modT_t[:, c0:c1, ctx], pm[:, 2 * c0:2 * c1].rearrange("p (f c) -> p f c", c=2)[:, :, ctx],
                        bm_t[:, c0:c1], ALU.add, [b_bm], [self.psb[7], b_modT])
            for ctx in range(2):
                self.stt(A_t[:, s_, :, ctx], modT_t[:, (3 * s_ + 1) * 8:(3 * s_ + 2) * 8, ctx], 1.0,
                         gn_t[:, s_, :], ALU.add, ALU.mult, [b_modT, b_gn], [b_A])
                self.ts(G_t[:, s_, :, ctx], modT_t[:, (3 * s_ + 2) * 8:(3 * s_ + 3) * 8, ctx],
                        0.5 if s_ != 1 else 1.0, None, ALU.mult, None, [b_modT], [b_G])
        items.append(first)
        for fc in range(72):
            items.append(lambda fc=fc: work(fc))
            if fc % 24 == 23:
                items.append(lambda s_=fc // 24: final(s_))
        self.mod_items = items

    def mod_pump(self, n=1):
        for _ in range(n):
            if self.mod_items:
                self.mod_items.pop(0)()

    def mod_flush(self):
        while self.mod_items:
            self.mod_items.pop(0)()

    def stage_layers(self):
        return {1: 1, 2: 1, 3: 2, 4: 2, 5: 3, 6: 3}.get(self.stage, DEPTH)

    def stage_partial(self, l, point):
        if self.stage == 1:
            return l == 0 and point == 1
        if self.stage == 2:
            return l == 0 and point == 2
        if self.stage == 4:
            return l == 1 and point == 2
        if self.stage == 6:
            return l == 2 and point == 2
        return False

    def final_norm(self, fg_t, b_fg):
        P, U = self.P, self.U
        x_t = self.x_t
        sq_ap = [U(i * 8192, 8192, BF16, None).rearrange("p (c n) -> p c n", c=NCH) for i in range(2)]
        sq_b = [P.abuf(f"sq{i}", i * 8192, (i + 1) * 8192) for i in range(2)]
        sd_ap = U(16384, 10240, F32, None)
        sd_b = P.abuf("sd", 16384, 26624)
        rs_ap = U(26624, 10240, F32, None)
        rs_b = P.abuf("rstd", 26624, 36864)
        for t in range(NTG):
            i = t % 2
            pb = 4 + (t % 4)
            self.act(sq_ap[i], x_t[:, :, t * TG:(t + 1) * TG], AF.Square,
                     [self.xb[c][t] for c in range(NCH)], [sq_b[i]])
            for c in range(NCH):
                self.mm(self.ps[pb][:], self.ones_t[:], sq_ap[i][:, c, :], c == 0, c == NCH - 1,
                        [self.b_ones, sq_b[i]], [self.psb[pb]])
            self.act(sd_ap[:, t * TG:(t + 1) * TG], self.ps[pb][:], AF.Sqrt,
                     [self.b_eps], [self.psb[pb], sd_b], scale=1.0 / D, bias=self.eps_t[:, 0:1])
        self.recip(rs_ap, sd_ap, [sd_b], [rs_b])
        for t in range(NTG):
            for c in range(NCH):
                self.stt(x_t[:, c, t * TG:(t + 1) * TG], x_t[:, c, t * TG:(t + 1) * TG], fg_t[:, c:c + 1],
                         rs_ap[:, t * TG:(t + 1) * TG], ALU.mult, ALU.mult,
                         [b_fg, rs_b], [self.xb[c][t]])

    def ctx_of(self, tg):
        return 0 if tg < 4 else 1

    def norm_mod(self, s):
        P, U = self.P, self.U
        x_t, h_t = self.x_t, self.h_t
        sq_ap = [U(i * 8192, 8192, BF16, None).rearrange("p (c n) -> p c n", c=NCH) for i in range(2)]
        sq_b = [P.abuf(f"sq{i}", i * 8192, (i + 1) * 8192) for i in range(2)]
        sd_ap = [U(16384 + t * 2048, 2048, F32, None) for t in range(NTG)]
        sd_b = [P.abuf(f"sd{t}", 16384 + t * 2048, 16384 + (t + 1) * 2048) for t in range(NTG)]
        rs_ap = [U(26624 + t * 2048, 2048, F32, None) for t in range(NTG)]
        rs_b = [P.abuf(f"rstd{t}", 26624 + t * 2048, 26624 + (t + 1) * 2048) for t in range(NTG)]
        tmp_ap = [U(36864 + i * 2048, 2048, F32, None) for i in range(2)]
        tmp_b = [P.abuf(f"tmp{i}", 36864 + i * 2048, 36864 + (i + 1) * 2048) for i in range(2)]
        kk = [0]

        def sq_(t):
            self.act(sq_ap[t % 2], x_t[:, :, t * TG:(t + 1) * TG], AF.Square,
                     [self.xb[c][t] for c in range(NCH)], [sq_b[t % 2]])

        def mm_(t):
            pb = 4 + (t % 4)
            for c in range(NCH):
                self.mm(self.ps[pb][:], self.ones_t[:], sq_ap[t % 2][:, c, :], c == 0, c == NCH - 1,
                        [self.b_ones, sq_b[t % 2]], [self.psb[pb]])

        def rs_(t):
            pb = 4 + (t % 4)
            self.act(sd_ap[t], self.ps[pb][:], AF.Sqrt,
                     [self.b_eps], [self.psb[pb], sd_b[t]], scale=1.0 / D, bias=self.eps_t[:, 0:1])
            self.recip(rs_ap[t], sd_ap[t], [sd_b[t]], [rs_b[t]])

        def stage_b(t):
            ctx = self.ctx_of(t)
            for c in range(NCH):
                i = kk[0] % 2
                kk[0] += 1
                self.stt(tmp_ap[i], x_t[:, c, t * TG:(t + 1) * TG], self.A_t[:, s, c, ctx:ctx + 1],
                         rs_ap[t], ALU.mult, ALU.mult,
                         [self.xb[c][t], self.b_A, rs_b[t]], [tmp_b[i]])
                self.act(h_t[:, c, t * TG:(t + 1) * TG], tmp_ap[i], AF.Identity,
                         [tmp_b[i], self.b_modT], [self.hb[c][t]],
                         bias=self.modT_t[:, (3 * s) * 8 + c, ctx:ctx + 1])

        sq_(0)
        mm_(0)
        for t in range(NTG):
            if t + 1 < NTG:
                sq_(t + 1)
            rs_(t)
            if t + 1 < NTG:
                mm_(t + 1)
        for t in range(NTG):
            stage_b(t)

    def ffn(self, l, si):
        P, U = self.P, self.U
        s = 0 if si == 0 else 2
        wgu_d, wd_d = self.din[f"wgu{l}"].ap(), self.din[f"wd{l}"].ap()
        a_ap = U(0, FB * NTOK * 2, BF16, None).rearrange("p (j n) -> p j n", j=FB)
        a_b = [[P.abuf(f"a{j}_{t}", (j * NTOK + t * TG) * 2, (j * NTOK + (t + 1) * TG) * 2)
                for t in range(NTG)] for j in range(FB)]
        sg_ap = [U(73728 + i * 2048, 2048, F32, None) for i in range(2)]
        sg_b = [P.abuf(f"sg{i}", 73728 + i * 2048, 73728 + (i + 1) * 2048) for i in range(2)]
        fs_ap = list(self.wslot_ap) + [U(20480 + i * 8192, 8192, BF16, None) for i in range(2)]
        fs_b = list(self.wslot_b) + [P.abuf(f"fslot{i}", 20480 + i * 8192, 20480 + (i + 1) * 8192) for i in range(2)]
        bsz = [NFF % FB] + [FB] * (NFF // FB) if NFF % FB else [FB] * (NFF // FB)
        bstart = [sum(bsz[:i]) for i in range(len(bsz))]
        nblk = len(bsz)
        mods_pending = len(self.mod_items) > 0
        ndb = 3 if mods_pending else 4
        it = 0
        fctr = [0]

        def load_block(b):
            j0 = bstart[b]
            nj = bsz[b]
            slots = {}
            for jj in range(0, nj, 2):
                sidx = fctr[0] % 6
                fctr[0] += 1
                n2 = min(2, nj - jj)
                wap = fs_ap[sidx].rearrange("p (j g k c) -> p j g k c", j=2, g=2, k=NCH)
                self.dma(wap[:, 0:n2], wgu_d[si, j0 + jj:j0 + jj + n2].rearrange("j p (g k c) -> p j g k c", g=2, k=NCH),
                         [], [fs_b[sidx]], eng="pool")
                for q in range(n2):
                    slots[jj + q] = (sidx, wap, q)
            sidx_d = fctr[0] % 6
            fctr[0] += 1
            wdap = fs_ap[sidx_d].rearrange("p (j d) -> p j d", j=FB)
            self.dma(wdap[:, 0:nj], wd_d[si, j0:j0 + nj].rearrange("j p d -> p j d"),
                     [], [fs_b[sidx_d]], eng="pool")
            return (j0, nj, slots, sidx_d, wdap)

        nxt = load_block(0)
        for b in range(nblk):
            j0, nj, slots, sidx_d, wdap = nxt
            if b + 1 < nblk:
                nxt = load_block(b + 1)
            for jj in range(nj):
                sidx, wap, q = slots[jj]
                for t in range(NTG):
                    pg = (it % 2) * 2
                    pu = pg + 1
                    i2 = it % 2
                    it += 1
                    for k in range(NCH):
                        self.mm(self.ps[pg][:], wap[:, q, 0, k, :], self.h_t[:, k, t * TG:(t + 1) * TG],
                                k == 0, k == NCH - 1, [fs_b[sidx], self.hb[k][t]], [self.psb[pg]])
                    for k in range(NCH):
                        self.mm(self.ps[pu][:], wap[:, q, 1, k, :], self.h_t[:, k, t * TG:(t + 1) * TG],
                                k == 0, k == NCH - 1, [fs_b[sidx], self.hb[k][t]], [self.psb[pu]])
                    self.act(sg_ap[i2], self.ps[pg][:], AF.Silu, [], [self.psb[pg], sg_b[i2]])
                    self.tt(a_ap[:, jj, t * TG:(t + 1) * TG], self.ps[pu][:], sg_ap[i2], ALU.mult,
                            [sg_b[i2]], [self.psb[pu], a_b[jj][t]])
                self.mod_pump()
            for d in range(NCH):
                for t in range(NTG):
                    ctx = self.ctx_of(t)
                    pd = 4 + (it % ndb)
                    it += 1
                    for jj in range(nj):
                        self.mm(self.ps[pd][:], wdap[:, jj, d * 128:(d + 1) * 128], a_ap[:, jj, t * TG:(t + 1) * TG],
                                jj == 0, jj == nj - 1, [fs_b[sidx_d], a_b[jj][t]], [self.psb[pd]])
                    self.stt(self.x_t[:, d, t * TG:(t + 1) * TG], self.ps[pd][:], self.G_t[:, s, d, ctx:ctx + 1],
                             self.x_t[:, d, t * TG:(t + 1) * TG], ALU.mult, ALU.add,
                             [self.b_G], [self.psb[pd], self.xb[d][t]])
                self.mod_pump()
        self.mod_flush()

    def next_slot(self):
        si = self.wctr % 4
        self.wctr += 1
        return si

    def mixer_a(self, l, idx):
        P, U = self.P, self.U
        x_t, h_t = self.x_t, self.h_t
        awuv_d = self.inp(f"awuv{idx}", [8, 128, 2 * NCH * 256])
        awout_d = self.inp(f"awout{idx}", [8, 128, 2 * D])
        awsT_d = self.inp(f"awsT{idx}", [128, 8 * 128])
        abs_d = self.inp(f"abs{idx}", [1, 8 * 128])
        ang_d = self.inp(f"ang{idx}", [128, 2048])
        sqa_ap = [U(i * 1024, 1024, BF16, None) for i in range(2)]
        sqa_b = [P.abuf(f"sqa{i}", i * 1024, (i + 1) * 1024) for i in range(2)]
        sd_ap = U(2048, 10240, F32, None)
        sd_b = P.abuf("a_sd", 2048, 12288)
        rs_ap = U(12288, 10240, F32, None)
        rs_b = P.abuf("a_rs", 12288, 22528)
        rT_ap = U(22528, 128, F32, None)
        rT_b = P.abuf("a_rT", 22528, 22656)
        ga_ap = U(22656, 8192, F32, None)
        ga_b = P.abuf("a_ga", 22656, 30848)
        ws_ap = U(30848, 2048, BF16, None).rearrange("p (g q) -> p g q", g=8)
        ws_b = P.abuf("a_ws", 30848, 32896)
        bs_ap = U(32896, 2048, BF16, None)
        bs_b = P.abuf("a_bs", 32896, 34944)
        vn_ap = U(2048, 10240, BF16, None).rearrange("p (t c) -> p t c", t=20)
        vn_b = [P.abuf(f"a_vn{t}", 2048 + t * 512, 2048 + (t + 1) * 512) for t in range(20)]
        u_ap = U(12288, 10240, BF16, None).rearrange("p (f n) -> p f n", f=2)
        u_b = [[P.abuf(f"a_u{f}_{t}", 12288 + (f * NTOK + t * TG) * 2, 12288 + (f * NTOK + (t + 1) * TG) * 2)
                for t in range(NTG)] for f in range(2)]
        self.dma(ga_ap, ang_d, [], [ga_b])
        self.dma(ws_ap.rearrange("p g q -> p (g q)"), awsT_d, [], [ws_b], eng="pool")
        self.dma(bs_ap[0:1, :], abs_d, [], [bs_b], eng="pool")
        it = 0
        pend = None
        for g in range(8):
            si = self.next_slot()
            wv = self.wslot_ap[si][:, 0:NCH * 256].rearrange("p (k c) -> p k c", k=NCH)
            self.dma(wv, awuv_d[g].rearrange("p (u k c) -> p u k c", u=2, k=NCH)[:, 1], [], [self.wslot_b[si]], eng="pool")
            for fc in range(2):
                for t in range(NTG):
                    pb = it % 3
                    i2 = it % 2
                    it += 1
                    for k in range(NCH):
                        self.mm(self.ps[pb][:], wv[:, k, fc * 128:(fc + 1) * 128], h_t[:, k, t * TG:(t + 1) * TG],
                                k == 0, k == NCH - 1, [self.wslot_b[si], self.hb[k][t]], [self.psb[pb]])
                    if pend is not None:
                        pend()
                    self.act(sqa_ap[i2], self.ps[pb][:], AF.Square, [], [self.psb[pb], sqa_b[i2]])
                    pend = (lambda t=t, i2=i2, first=(g == 0 and fc == 0), lastf=(g == 7 and fc == 1):
                            self.mm(self.ps[3 + t][:], self.ones_t[:], sqa_ap[i2], first, lastf,
                                    [self.b_ones, sqa_b[i2]], [self.psb[3 + t]]))
        pend()
        for t in range(NTG):
            self.act(sd_ap[:, t * TG:(t + 1) * TG], self.ps[3 + t][:], AF.Sqrt,
                     [self.b_eps], [self.psb[3 + t], sd_b], scale=1.0 / 2048, bias=self.eps_t[:, 0:1])
        self.recip(rs_ap, sd_ap, [sd_b], [rs_b])
        for tile in range(20):
            self.mm(self.ps[0][:, tile:tile + 1], rs_ap[0:1, tile * 128:(tile + 1) * 128], self.onef_t[0:1, 0:1],
                    True, True, [rs_b, self.b_onef], [self.psb[0]])
        self.copy(rT_ap[:, 0:20], self.ps[0][:, 0:20], [], [self.psb[0], rT_b])
        def load_g(g):
            sa = self.next_slot()
            wuv = self.wslot_ap[sa].rearrange("p (u k c) -> p u k c", u=2, k=NCH)
            self.dma(wuv, awuv_d[g].rearrange("p (u k c) -> p u k c", u=2, k=NCH), [], [self.wslot_b[sa]], eng="pool")
            sb_ = self.next_slot()
            wo = self.wslot_ap[sb_][:, 0:2 * D].rearrange("p (f d) -> p f d", f=2)
            self.dma(wo, awout_d[g].rearrange("p (f d) -> p f d", f=2), [], [self.wslot_b[sb_]], eng="pool")
            return (g, sa, wuv, sb_, wo)

        vctr = [0]

        def v_tile(G_, tile):
            g, sa, wuv, sb_, wo = G_
            pb = 2 + vctr[0] % 2
            vctr[0] += 1
            t = tile // 4
            for k in range(NCH):
                self.mm(self.ps[pb][:, 0:256], h_t[:, k, tile * 128:(tile + 1) * 128], wuv[:, 1, k, :],
                        k == 0, k == NCH - 1, [self.wslot_b[sa], self.hb[k][t]], [self.psb[pb]])
            self.stt(vn_ap[:, tile, :], self.ps[pb][:, 0:256], rT_ap[:, tile:tile + 1],
                     ga_ap[:, g * 256:(g + 1) * 256], ALU.mult, ALU.mult,
                     [rT_b, ga_b], [self.psb[pb], vn_b[tile]])

        cur = load_g(0)
        for tile in range(20):
            v_tile(cur, tile)
        for g in range(8):
            _, sa, wuv, sb_, wo = cur
            for fc in range(2):
                for t in range(NTG):
                    pb = it % 2
                    it += 1
                    for k in range(NCH):
                        self.mm(self.ps[pb][:], wuv[:, 0, k, fc * 128:(fc + 1) * 128], h_t[:, k, t * TG:(t + 1) * TG],
                                k == 0, k == NCH - 1, [self.wslot_b[sa], self.hb[k][t]], [self.psb[pb]])
                    self.act(u_ap[:, fc, t * TG:(t + 1) * TG], self.ps[pb][:], AF.Copy, [], [self.psb[pb], u_b[fc][t]])
            for fc in range(2):
                for t in range(NTG):
                    pb = 4 + it % 2
                    it += 1
                    for n in range(4):
                        tile = t * 4 + n
                        self.mm(self.ps[pb][:, n * 128:(n + 1) * 128], vn_ap[:, tile, fc * 128:(fc + 1) * 128],
                                ws_ap[:, g, :], True, False, [vn_b[tile], ws_b], [self.psb[pb]])
                        self.mm(self.ps[pb][:, n * 128:(n + 1) * 128], self.ones_t[0:1, :],
                                bs_ap[0:1, g * 128:(g + 1) * 128], False, True, [self.b_ones, bs_b], [self.psb[pb]])
                    self.tt(u_ap[:, fc, t * TG:(t + 1) * TG], self.ps[pb][:], u_ap[:, fc, t * TG:(t + 1) * TG],
                            ALU.mult, [], [self.psb[pb], u_b[fc][t]])
            nxt = load_g(g + 1) if g + 1 < 8 else None
            oi = 0
            for d in range(NCH):
                for t in range(NTG):
                    ctx = self.ctx_of(t)
                    pb = 6 + it % 2
                    it += 1
                    for fc in range(2):
                        self.mm(self.ps[pb][:], wo[:, fc, d * 128:(d + 1) * 128], u_ap[:, fc, t * TG:(t + 1) * TG],
                                fc == 0, fc == 1, [self.wslot_b[sb_], u_b[fc][t]], [self.psb[pb]])
                    self.stt(x_t[:, d, t * TG:(t + 1) * TG], self.ps[pb][:], self.G_t[:, 1, d, ctx:ctx + 1],
                             x_t[:, d, t * TG:(t + 1) * TG], ALU.mult, ALU.add,
                             [self.b_G], [self.psb[pb], self.xb[d][t]])
                    if nxt is not None and oi % 2 == 1:
                        v_tile(nxt, oi // 2)
                    oi += 1
            cur = nxt

    def attn_layer(self, l, kind):
        P, U = self.P, self.U
        nc = self.nc
        x_t, h_t = self.x_t, self.h_t
        isB = kind == "B"
        hd = 128 if isB else 64
        HQ = 8 if isB else 16
        KV = 2
        G = HQ // KV
        pre = "b_" if isB else "c_"
        scale = float(hd) ** -0.5
        voff = 0 if isB else 128
        VW = KV * hd
        HW = 128
        NQ = HQ if isB else HQ // 2
        qw_d = self.inp(pre + "qw", [NQ, 128, NCH * HW])
        kw_d = self.inp(pre + "kw", [128, KV * NCH * HW])
        tw_d = self.inp(pre + "tw", [128, NCH * 256])
        ow_d = self.inp(pre + "ow", [NCH, 128, 8 * 128])
        ck_d = self.inp(pre + "ck", [HW, KV * 512])
        cv_d = self.inp(pre + "cv", [128, 4 * VW])
        rope_d = self.inp(pre + "rope", [2, HW, 2048])
        pm_d = self.inp(pre + "pm", [HW, HW])
        if isB:
            qg_d = self.inp("b_qg", [128, 1])
            kg_d = self.inp("b_kg", [128, 1])
            ko_d = self.outp("b_ko", [128, KV * 512])
            vo_d = self.outp("b_vo", [128, 4 * 256])
            XW = 8192
        else:
            ident_d = self.inp("c_ident", [128, 128])
            mask_d = self.inp("c_mask", [128, 8 * 512])
            sink_d = self.inp("c_sink", [128, 16])
            vo_d = self.outp("c_kvo", [128, 4 * 256])
            XW = 768
        kxin = nc.dram_tensor(pre + "kxin", [128, XW], BF16)
        kxout = nc.dram_tensor(pre + "kxout", [256, XW], BF16)
        b_kxin, b_kxout = Buf(pre + "kxin"), Buf(pre + "kxout")
        pm_t = self.sb(pre + "pm_s", [128, 128], BF16)
        b_pm = Buf(pre + "pm")
        self.dma(pm_t[:, :], pm_d, [], [b_pm], eng="pool")
        if isB:
            onesf_t = self.sb("onesf_s", [128, 128], F32)
            b_onesf = Buf("onesf")
            self.memset(onesf_t[:], 1.0, [b_onesf])
            qg_t = self.sb("qg_s", [128, 1], F32)
            kg_t = self.sb("kg_s", [128, 1], F32)
            b_g = Buf("qkg")
            self.dma(qg_t[:], qg_d, [], [b_g])
            self.dma(kg_t[:], kg_d, [], [b_g])
        else:
            ident_t = self.sb("ident_s", [128, 128], BF16)
            sink_t = self.sb("sink_s", [128, 16], F32)
            esink_t = self.sb("esink_s", [128, 16], F32)
            b_id, b_sk, b_esk = Buf("ident"), Buf("sink"), Buf("esink")
            self.dma(ident_t[:], ident_d, [], [b_id], eng="pool")
            self.dma(sink_t[:], sink_d, [], [b_sk])
            self.act(esink_t[:], sink_t[:], AF.Exp, [b_sk], [b_esk])
        if isB:
            Kall_ap = U(0, 18432, BF16, None).rearrange("p (kv n) -> p kv n", kv=KV)
            b_K = P.abuf("B_Kall", 0, 18432)
            Vall_ap = U(18432, 18432, BF16, None).rearrange("p (t c) -> p t c", t=36)
            b_V = P.abuf("B_Vall", 18432, 36864)
            ktmp_ap = U(0, 8192, BF16, None).rearrange("p (kv n) -> p kv n", kv=KV)
            b_ktmp = P.abuf("B_ktmp", 0, 8192)
            vtmp_ap = U(18432, 8192, BF16, None).rearrange("p (t c) -> p t c", t=16)
            b_vtmp = P.abuf("B_vtmp", 18432, 26624)
            KP0, VP0 = 36864, 38912
            O0 = 76800
            RC0, RS0 = 68608, 69632
        else:
            Kown_ap = U(0, 9216, BF16, None).rearrange("p (kv n) -> p kv n", kv=KV)
            b_K = P.abuf("C_Kown", 0, 9216)
            Kctx_ap = U(9216, 2048, BF16, None).rearrange("p (kv n) -> p kv n", kv=KV)
            b_Kc = P.abuf("C_Kctx", 9216, 11264)
            Vowna_ap = U(11264, 9216, BF16, None).rearrange("p (t kv c) -> p t kv c", t=18, kv=KV)
            b_V = P.abuf("C_Vown", 11264, 20480)
            Vctxa_ap = U(20480, 2048, BF16, None).rearrange("p (t kv c) -> p t kv c", t=4, kv=KV)
            b_Vc = P.abuf("C_Vctx", 20480, 22528)
            mask_ap = U(22528, 8192, BF16, None).rearrange("p (m n) -> p m n", m=8)
            b_mask = P.abuf("C_mask", 22528, 30720)
            KP0, VP0 = 78848, 80896
            O0 = 30720
            RC0, RS0 = 76800, 77824
        KP_ap = U(KP0, 2048, BF16, None).rearrange("p (kv n) -> p kv n", kv=KV)
        b_KP = P.abuf(pre + "KP", KP0, KP0 + 2048)
        if isB:
            VP_ap = U(VP0, 4 * VW * 2, BF16, None).rearrange("p (t c) -> p t c", t=4)
            b_VP = P.abuf(pre + "VP", VP0, VP0 + 4 * VW * 2)
        else:
            VPa_ap = U(VP0, 2048, BF16, None).rearrange("p (t kv c) -> p t kv c", t=4, kv=KV)
            b_VP = P.abuf(pre + "VP", VP0, VP0 + 2048)
            self.memset(Vowna_ap[:, :, :, 64:128], 1.0, [b_V])
            self.memset(Vctxa_ap[:, :, :, 64:128], 1.0, [b_Vc])
            self.memset(VPa_ap[:, :, :, 64:128], 1.0, [b_VP])
        O_ap = U(O0, 8192, BF16, None).rearrange("p (h n) -> p h n", h=8)
        b_O = [P.abuf(pre + f"O{i}", O0 + i * 1024, O0 + (i + 1) * 1024) for i in range(8)]

        def f32buf(name, lo, nb=2048):
            return U(lo, nb, F32, None), P.abuf(pre + name, lo, lo + nb)
        rstd_ap, b_rstd = f32buf("rstd", 57344)
        t1_ap, b_t1 = f32buf("t1", 59392)
        t2_ap, b_t2 = f32buf("t2", 61440)
        rz_ap, b_rz = f32buf("rz", 63488)
        kf_ap, b_kf = f32buf("kf", 65536)
        vf_ap, b_vf = f32buf("vf", 67584, 1024)
        if not isB:
            zz_ap, b_zz = f32buf("zz", 68608)

        def bfbuf(name, lo, nb=1024):
            return U(lo, nb, BF16, None), P.abuf(pre + name, lo, lo + nb)
        Q_ap, b_Q = [None, None], [None, None]
        Q_ap[0], b_Q[0] = bfbuf("Q0", 70656)
        Q_ap[1], b_Q[1] = bfbuf("Q1", 71680)
        raw_ap, b_raw = bfbuf("raw", 72704)
        sq_ap, b_sq = bfbuf("sqh", 73728)
        PT_ap, b_PT = [None, None], [None, None]
        PT_ap[0], b_PT[0] = bfbuf("PT0", 74752)
        PT_ap[1], b_PT[1] = bfbuf("PT1", 75776)
        rC_ap, b_rC = bfbuf("rC", RC0)
        rS_ap, b_rS = bfbuf("rS", RS0)
        ps = self.ps
        psb = self.psb
        WSL = [0, 1]
        PQ, PS2 = 6, 7

        def load_rope(t):
            self.dma(rC_ap[:, :], rope_d[0, :, t * TG:(t + 1) * TG], [], [b_rC], eng="pool")
            self.dma(rS_ap[:, :], rope_d[1, :, t * TG:(t + 1) * TG], [], [b_rS], eng="pool")

        def qk_post_a(src, n, g_t, rope, out_ap, out_b):
            if isB:
                self.act(raw_ap[0:HW, 0:n], src, AF.Identity, [b_g], [psb[PQ], b_raw], scale=g_t[0:HW, 0:1])
                self.act(sq_ap[0:HW, 0:n], src, AF.Square, [], [psb[PQ], b_sq])
            elif not rope:
                self.act(out_ap, src, AF.Copy, [], [psb[PQ]] + out_b)
            else:
                self.act(raw_ap[0:HW, 0:n], src, AF.Copy, [], [psb[PQ], b_raw])

        def qk_post_b(n, rope, out_ap, out_b, f32_out=None):
            if isB:
                self.mm(ps[PS2][0:HW, 0:n], self.ones_t[0:HW, 0:HW], sq_ap[0:HW, 0:n], True, True,
                        [self.b_ones, b_sq], [psb[PS2]])
                self.act(rstd_ap[0:HW, 0:n], ps[PS2][0:HW, 0:n], AF.Ln, [self.b_eps], [psb[PS2], b_rstd],
                         scale=1.0 / hd, bias=self.eps_t[0:HW, 0:1])
                self.act(rstd_ap[0:HW, 0:n], rstd_ap[0:HW, 0:n], AF.Exp, [], [b_rstd], scale=-0.5)
            elif not rope:
                return
            if rope:
                self.mm(ps[PS2][0:HW, 0:n], pm_t[0:HW, 0:HW], raw_ap[0:HW, 0:n], True, True, [b_pm, b_raw], [psb[PS2]])
                self.tt(t1_ap[0:HW, 0:n], raw_ap[0:HW, 0:n], rC_ap[0:HW, 0:n], ALU.mult, [b_raw, b_rC], [b_t1])
                self.tt(t2_ap[0:HW, 0:n], ps[PS2][0:HW, 0:n], rS_ap[0:HW, 0:n], ALU.mult, [b_rS], [psb[PS2], b_t2])
                if isB:
                    self.tt(t1_ap[0:HW, 0:n], t1_ap[0:HW, 0:n], t2_ap[0:HW, 0:n], ALU.add, [b_t2], [b_t1])
                    self.tt(out_ap, t1_ap[0:HW, 0:n], rstd_ap[0:HW, 0:n], ALU.mult, [b_t1, b_rstd], out_b)
                else:
                    self.tt(out_ap, t1_ap[0:HW, 0:n], t2_ap[0:HW, 0:n], ALU.add, [b_t1, b_t2], out_b)
            else:
                if f32_out is not None:
                    self.tt(f32_out[0], raw_ap[0:HW, 0:n], rstd_ap[0:HW, 0:n], ALU.mult, [b_raw, b_rstd], [f32_out[1]])
                    self.act(out_ap, f32_out[0], AF.Copy, [f32_out[1]], out_b)
                else:
                    self.tt(out_ap, raw_ap[0:HW, 0:n], rstd_ap[0:HW, 0:n], ALU.mult, [b_raw, b_rstd], out_b)

        def qk_post(src, n, g_t, rope, out_ap, out_b, f32_out=None):
            qk_post_a(src, n, g_t, rope, out_ap, out_b)
            qk_post_b(n, rope, out_ap, out_b, f32_out)

        sk = WSL[0]
        kw = self.wslot_ap[sk][:, 0:KV * NCH * HW].rearrange("p (kv k c) -> p kv k c", kv=KV, k=NCH)
        self.dma(kw, kw_d.rearrange("p (kv k c) -> p kv k c", kv=KV, k=NCH), [], [self.wslot_b[sk]], eng="pool")
        st = WSL[1]
        tw = self.wslot_ap[st][:, 0:NCH * 256].rearrange("p (k c) -> p k c", k=NCH)
        self.dma(tw, tw_d.rearrange("p (k c) -> p k c", k=NCH), [], [self.wslot_b[st]], eng="pool")
        if not isB:
            self.dma(mask_ap.rearrange("p m n -> p (m n)"), mask_d, [], [b_mask], eng="pool")
        for t in range(NTG):
            if t < 4:
                load_rope(t)
            for kv in range(KV):
                for k in range(NCH):
                    self.mm(ps[PQ][0:HW, :], kw[:, kv, k, :], h_t[:, k, t * TG:(t + 1) * TG], k == 0, k == NCH - 1,
                            [self.wslot_b[sk], self.hb[k][t]], [psb[PQ]])
                if t < 4:
                    if isB:
                        dst, dstb = ktmp_ap[0:HW, kv, t * TG:(t + 1) * TG], [b_ktmp]
                    else:
                        dst, dstb = Kown_ap[0:HW, kv, 128 + t * TG:128 + (t + 1) * TG], [b_K]
                    qk_post(ps[PQ][0:HW, :], TG, kg_t if isB else None, True, dst, dstb)
                else:
                    qk_post(ps[PQ][0:HW, :], TG, kg_t if isB else None, False, KP_ap[0:HW, kv, :], [b_KP],
                            f32_out=(kf_ap[0:HW, :], b_kf) if isB else None)
                    if isB:
                        self.dma(ko_d.rearrange("p (kv n) -> p kv n", kv=KV)[:, kv, :], kf_ap[:, :], [b_kf], [])
        it = 0
        for tile in range(20):
            pb = it % 2
            it += 1
            t = tile // 4
            for k in range(NCH):
                self.mm(ps[pb][:, 0:256], h_t[:, k, tile * 128:(tile + 1) * 128], tw[:, k, :], k == 0, k == NCH - 1,
                        [self.wslot_b[st], self.hb[k][t]], [psb[pb]])
            if tile < 16:
                if isB:
                    self.act(vtmp_ap[:, tile, :], ps[pb][:, 0:256], AF.Copy, [], [psb[pb], b_vtmp])
                else:
                    self.act(Vowna_ap[:, tile + 1, :, 0:64], ps[pb][:, 128:256].rearrange("p (kv d) -> p kv d", kv=KV),
                             AF.Copy, [], [psb[pb], b_V])
            else:
                if isB:
                    self.act(VP_ap[:, tile - 16, :], ps[pb][:, voff:voff + VW], AF.Copy, [], [psb[pb], b_VP])
                else:
                    self.act(VPa_ap[:, tile - 16, :, 0:64], ps[pb][:, 128:256].rearrange("p (kv d) -> p kv d", kv=KV),
                             AF.Copy, [], [psb[pb], b_VP])
                self.copy(vf_ap[:, :], ps[pb][:, 0:256], [], [psb[pb], b_vf])
                self.dma(vo_d.rearrange("p (t c) -> p t c", t=4)[:, tile - 16, :], vf_ap[:, :], [b_vf], [])
        rg = [[2 * i, 2 * i + 1] for i in range(self.ncores // 2)]
        kxin_ap, kxout_ap = kxin.ap(), kxout.ap()
        if isB:
            self.dma(kxin_ap[:, 0:4096], ktmp_ap.rearrange("p kv n -> p (kv n)"), [b_ktmp], [b_kxin])
            self.dma(kxin_ap[:, 4096:8192], vtmp_ap.rearrange("p t c -> p (t c)"), [b_vtmp], [b_kxin])
        else:
            kx4 = kxin_ap[:, 0:512].rearrange("p (kv e s) -> p kv e s", kv=KV, e=2)
            self.dma(kx4[:, :, 0, :], Kown_ap[:, :, 128:256], [b_K], [b_kxin])
            self.dma(kx4[:, :, 1, :], Kown_ap[:, :, 16 * 128:17 * 128], [b_K], [b_kxin])
            self.dma(kxin_ap[:, 512:640].rearrange("p (kv d) -> p kv d", kv=KV), Vowna_ap[:, 1, :, 0:64], [b_V], [b_kxin])
            self.dma(kxin_ap[:, 640:768].rearrange("p (kv d) -> p kv d", kv=KV), Vowna_ap[:, 16, :, 0:64], [b_V], [b_kxin])
        if self.ncores >= 2:
            P.add("pool", lambda e, a=kxin, b=kxout, r=rg: e.collective_compute(
                "AllGather", ALU.bypass, replica_groups=r, ins=[a.ap().opt()], outs=[b.ap().opt()]),
                [b_kxin], [b_kxout], cc=True)
        if isB:
            for r in range(2):
                self.dma(Kall_ap[:, :, (4 + 16 * r) * 128:(4 + 16 * r) * 128 + 2048],
                         kxout_ap[r * 128:(r + 1) * 128, 0:4096].rearrange("p (kv n) -> p kv n", kv=KV),
                         [b_kxout], [b_K])
                self.dma(Vall_ap[:, 4 + 16 * r:4 + 16 * (r + 1), :],
                         kxout_ap[r * 128:(r + 1) * 128, 4096:8192].rearrange("p (t c) -> p t c", t=16),
                         [b_kxout], [b_V])
            self.dma(Kall_ap[:, :, 0:512], ck_d.rearrange("p (kv n) -> p kv n", kv=KV), [], [b_K], eng="pool")
            self.dma(Vall_ap[:, 0:4, :], cv_d.rearrange("p (t c) -> p t c", t=4), [], [b_V], eng="pool")
        else:
            ko4 = kxout_ap[:, 0:512].rearrange("p (kv e s) -> p kv e s", kv=KV, e=2)
            self.dma(Kown_ap[:, :, 0:128], ko4[0:128, :, 1, :], [b_kxout], [b_K])
            self.dma(Kown_ap[:, :, 17 * 128:18 * 128], ko4[128:256, :, 0, :], [b_kxout], [b_K])
            self.dma(Vowna_ap[:, 0, :, 0:64], kxout_ap[0:128, 640:768].rearrange("p (kv d) -> p kv d", kv=KV), [b_kxout], [b_V])
            self.dma(Vowna_ap[:, 17, :, 0:64], kxout_ap[128:256, 512:640].rearrange("p (kv d) -> p kv d", kv=KV), [b_kxout], [b_V])
            self.dma(Kctx_ap[:, :, :], ck_d.rearrange("p (kv n) -> p kv n", kv=KV), [], [b_Kc], eng="pool")
            self.dma(Vctxa_ap[:, :, :, 0:64], cv_d.rearrange("p (t kv d) -> p t kv d", t=4, kv=KV), [], [b_Vc], eng="pool")

        HPS = 4
        R0 = 0
        for t in [4, 0, 1, 2, 3]:
            ctx = self.ctx_of(t)
            if t < 4:
                load_rope(t)
            tl = []
            if t == 4:
                for sq_i in range(2):
                    for j in range(2):
                        kt = sq_i * 2 + j
                        tl.append((lambda kv, kt=kt: KP_ap[R0:R0 + hd, kv, kt * 128:(kt + 1) * 128],
                                   (lambda kv, kt=kt: VP_ap[:, kt, kv * hd:(kv + 1) * hd]) if isB
                                   else (lambda kv, kt=kt: VPa_ap[:, kt, kv, :]),
                                   [], b_KP, b_VP, sq_i * 256, 256))
            elif isB:
                for kt in range(36):
                    tl.append((lambda kv, kt=kt: Kall_ap[0:hd, kv, kt * 128:(kt + 1) * 128],
                               lambda kv, kt=kt: Vall_ap[:, kt, kv * hd:(kv + 1) * hd], [], b_K, b_V, 0, 512))
            else:
                for j in range(4):
                    tl.append((lambda kv, j=j: Kctx_ap[R0:R0 + hd, kv, j * 128:(j + 1) * 128],
                               lambda kv, j=j: Vctxa_ap[:, j, kv, :], [], b_Kc, b_Vc, 0, 512))
                for r in range(-1, 5):
                    tile = t * 4 + r + 1
                    mi = r + 1
                    if t == 0 and r == -1:
                        mi = 6
                    if t == 3 and r == 4:
                        mi = 7
                    c0 = max(0, r - 1) * 128
                    c1 = min(4, r + 2) * 128
                    masks = []
                    for blk in (r + 1, r - 1):
                        if 0 <= blk <= 3:
                            masks.append((mi, blk * 128 - c0, blk * 128))
                    tl.append((lambda kv, tile=tile: Kown_ap[R0:R0 + hd, kv, tile * 128:(tile + 1) * 128],
                               lambda kv, tile=tile: Vowna_ap[:, tile, kv, :], masks, b_K, b_V, c0, c1 - c0))
            sit = 0

            def prep_steps(u):
                n = TG
                rope = t < 4
                out_ap, out_b = Q_ap[u % 2][:, :], [b_Q[u % 2]]
                src = ps[PQ][:, :]
                st = []

                def mmk(k):
                    nonlocal qw, sq_slot
                    if k == 0 and u % HPS == 0:
                        sq_slot = WSL[(u // HPS) % 2]
                        qw = self.wslot_ap[sq_slot][:, 0:HPS * NCH * HW].rearrange("p (h k c) -> p h k c", h=HPS, k=NCH)
                        self.dma(qw, qw_d[u:u + HPS].rearrange("h p (k c) -> p h k c", k=NCH), [], [self.wslot_b[sq_slot]], eng="pool")
                    self.mm(src, qw[:, u % HPS, k, :], h_t[:, k, t * TG:(t + 1) * TG], k == 0, k == NCH - 1,
                            [self.wslot_b[sq_slot], self.hb[k][t]], [psb[PQ]])
                for k in range(NCH):
                    st.append(lambda k=k: mmk(k))
                if isB:
                    st.append(lambda: self.act(raw_ap[0:HW, 0:n], src, AF.Identity, [b_g], [psb[PQ], b_raw], scale=qg_t[0:HW, 0:1]))
                    st.append(lambda: self.act(sq_ap[0:HW, 0:n], src, AF.Square, [], [psb[PQ], b_sq]))
                    st.append(lambda: self.mm(ps[PS2][0:HW, 0:n], self.ones_t[0:HW, 0:HW], sq_ap[0:HW, 0:n], True, True,
                                              [self.b_ones, b_sq], [psb[PS2]]))
                    st.append(lambda: self.act(rstd_ap[0:HW, 0:n], ps[PS2][0:HW, 0:n], AF.Ln, [self.b_eps], [psb[PS2], b_rstd],
                                               scale=1.0 / hd, bias=self.eps_t[0:HW, 0:1]))
                    st.append(lambda: self.act(rstd_ap[0:HW, 0:n], rstd_ap[0:HW, 0:n], AF.Exp, [], [b_rstd], scale=-0.5))
                elif not rope:
                    st.append(lambda: self.act(out_ap, src, AF.Copy, [], [psb[PQ]] + out_b))
                    return st
                else:
                    st.append(lambda: self.act(raw_ap[0:HW, 0:n], src, AF.Copy, [], [psb[PQ], b_raw]))
                if rope:
                    st.append(lambda: self.mm(ps[PS2][0:HW, 0:n], pm_t[0:HW, 0:HW], raw_ap[0:HW, 0:n], True, True,
                                              [b_pm, b_raw], [psb[PS2]]))
                    st.append(lambda: self.tt(t1_ap[0:HW, 0:n], raw_ap[0:HW, 0:n], rC_ap[0:HW, 0:n], ALU.mult, [b_raw, b_rC], [b_t1]))
                    st.append(lambda: self.tt(t2_ap[0:HW, 0:n], ps[PS2][0:HW, 0:n], rS_ap[0:HW, 0:n], ALU.mult, [b_rS], [psb[PS2], b_t2]))
                    if isB:
                        st.append(lambda: self.tt(t1_ap[0:HW, 0:n], t1_ap[0:HW, 0:n], t2_ap[0:HW, 0:n], ALU.add, [b_t2], [b_t1]))
                        st.append(lambda: self.tt(out_ap, t1_ap[0:HW, 0:n], rstd_ap[0:HW, 0:n], ALU.mult, [b_t1, b_rstd], out_b))
                    else:
                        st.append(lambda: self.tt(out_ap, t1_ap[0:HW, 0:n], t2_ap[0:HW, 0:n], ALU.add, [b_t1, b_t2], out_b))
                else:
                    st.append(lambda: self.tt(out_ap, raw_ap[0:HW, 0:n], rstd_ap[0:HW, 0:n], ALU.mult, [b_raw, b_rstd], out_b))
                return st

            def prep_q(u):
                for f_ in prep_steps(u):
                    f_()

            qw, sq_slot = None, None
            steps = []
            prep_q(0)
            MO = hd if isB else 128
            for h in range(HQ):
                kv = h // G
                if isB:
                    u, R0 = h, 0
                    if h + 1 < HQ:
                        steps = prep_steps(h + 1)
                else:
                    u, R0 = h // 2, (h % 2) * 64
                    if h % 2 == 0 and u + 1 < NQ:
                        steps = prep_steps(u + 1)
                qi = u % 2
                pO, pZ = 2 + (h % 2), 4 + (h % 2)

                def emit_qk(ti):
                    Kf, Vf, masks, bK, bV, c0, n = tl[ti]
                    sb_i = (sit + ti) % 2
                    self.mm(ps[sb_i][:, 0:n], Kf(kv), Q_ap[qi][R0:R0 + hd, c0:c0 + n], True, len(masks) == 0,
                            [bK, b_Q[qi]], [psb[sb_i]])
                    for mk, (mi, off, mcol) in enumerate(masks):
                        self.mm(ps[sb_i][:, off:off + 128], ident_t[:, :], mask_ap[:, mi, mcol:mcol + 128], False,
                                mk == len(masks) - 1, [b_id, b_mask], [psb[sb_i]])
                seen_cols = set()
                emit_qk(0)
                for ti, (Kf, Vf, masks, bK, bV, c0, n) in enumerate(tl):
                    sb_i = (sit + ti) % 2
                    last = ti == len(tl) - 1
                    if not last:
                        emit_qk(ti + 1)
                    if steps and ti >= 1:
                        steps.pop(0)()
                    self.act(PT_ap[sb_i][:, 0:n], ps[sb_i][:, 0:n], AF.Exp, [], [psb[sb_i], b_PT[sb_i]], scale=scale)
                    if t == 4:
                        first = c0 not in seen_cols
                        seen_cols.add(c0)
                    else:
                        first = ti == 0
                    stopf = last or (t == 4 and ti == 1)
                    self.mm(ps[pO][0:MO, c0:c0 + n], Vf(kv), PT_ap[sb_i][:, 0:n], first, stopf,
                            [bV, b_PT[sb_i]], [psb[pO]])
                    if isB and t == 4:
                        self.mm(ps[pZ][0:hd, c0:c0 + n], self.ones_t[:, 0:hd], PT_ap[sb_i][:, 0:n], first, stopf,
                                [self.b_ones, b_PT[sb_i]], [psb[pZ]])
                    elif isB:
                        self.mm(ps[pZ][0:hd, :], self.ones_t[:, 0:hd], PT_ap[sb_i][:, :], first, stopf,
                                [self.b_ones, b_PT[sb_i]], [psb[pZ]])
                sit += len(tl)
                if isB or h % 2 == 1:
                    while steps:
                        steps.pop(0)()
                if isB:
                    self.recip(rz_ap[0:hd, :], ps[pZ][0:hd, :], [], [psb[pZ], b_rz])
                    self.tt(O_ap[:, h, :], ps[pO][0:hd, :], rz_ap[0:hd, :], ALU.mult, [b_rz], [psb[pO], b_O[h % 8]])
                else:
                    self.ts(zz_ap[64:128, :], ps[pO][64:128, :], esink_t[64:128, h:h + 1], None, ALU.add, None,
                            [b_esk], [psb[pO], b_zz])
                    self.recip(rz_ap[64:128, :], zz_ap[64:128, :], [b_zz], [b_rz])
                    p0 = (h % 2) * 64
                    self.tt(O_ap[p0:p0 + 64, h // 2, :], ps[pO][0:64, :], rz_ap[64:128, :], ALU.mult,
                            [b_rz], [psb[pO], b_O[h // 2]])

            for d in range(NCH):
                if d % 4 == 0:
                    so = WSL[(d // 4) % 2]
                    ow = self.wslot_ap[so].rearrange("p (d h c) -> p d h c", d=4, h=8)
                    self.dma(ow, ow_d[d:d + 4].rearrange("d p (h c) -> p d h c", h=8), [], [self.wslot_b[so]], eng="pool")
                pob = PQ if d % 2 == 0 else PS2
                for u in range(8):
                    self.mm(ps[pob][:], ow[:, d % 4, u, :], O_ap[:, u, :], u == 0, u == 7,
                            [self.wslot_b[so], b_O[u]], [psb[pob]])
                self.stt(x_t[:, d, t * TG:(t + 1) * TG], ps[pob][:], self.G_t[:, 1, d, ctx:ctx + 1],
                         x_t[:, d, t * TG:(t + 1) * TG], ALU.mult, ALU.add,
                         [self.b_G], [psb[pob], self.xb[d][t]])


def _prep_shared(inputs, need):
    f = np.float32
    sh = {}

    def want(n):
        return n in need

    w_mod = np.asarray(inputs["w_mod"], f)
    b_mod = np.asarray(inputs["b_mod"], f)
    norm_g = np.asarray(inputs["norm_g"], f)
    for l in range(DEPTH):
        if want(f"wm{l}"):
            sh[f"wm{l}"] = np.ascontiguousarray(
                w_mod[l].reshape(NCH, 128, 72, 128).transpose(2, 1, 0, 3)).reshape(72, 128, NCH * 128)
            sh[f"bm{l}"] = np.ascontiguousarray(b_mod[l].reshape(72, 128).T)
            sh[f"gn{l}"] = np.ascontiguousarray(norm_g[l].reshape(3, NCH, 128).transpose(2, 0, 1)).reshape(128, 3 * NCH)
            wg = np.asarray(inputs["ffn_w_gate"][l], f).reshape(2, NCH, 128, NFF, 128)
            wu = np.asarray(inputs["ffn_w_up"][l], f).reshape(2, NCH, 128, NFF, 128)
            wgu = np.stack([wg, wu], axis=1)
            sh[f"wgu{l}"] = np.ascontiguousarray(wgu.transpose(0, 4, 3, 1, 2, 5)).reshape(2, NFF, 128, 2 * NCH * 128)
            sh[f"wd{l}"] = np.ascontiguousarray(np.asarray(inputs["ffn_w_down"][l], f).reshape(2, NFF, 128, D))
    for idx in range(2):
        if want(f"awuv{idx}"):
            w_in = np.asarray(inputs["a_w_in"][idx], f).reshape(NCH, 128, 2, 8, 256)
            sh[f"awuv{idx}"] = np.ascontiguousarray(w_in.transpose(3, 1, 2, 0, 4)).reshape(8, 128, 2 * NCH * 256)
            w_out = np.asarray(inputs["a_w_out"][idx], f).reshape(8, 2, 128, D)
            sh[f"awout{idx}"] = np.ascontiguousarray(w_out.transpose(0, 2, 1, 3)).reshape(8, 128, 2 * D)
            w_s = np.asarray(inputs["a_w_s"][idx], f)
            sh[f"awsT{idx}"] = np.ascontiguousarray(w_s.transpose(2, 0, 1)).reshape(128, 8 * 128)
            sh[f"abs{idx}"] = np.ascontiguousarray(np.asarray(inputs["a_b_s"][idx], f).reshape(1, 8 * 128))
            sh[f"ang{idx}"] = np.ascontiguousarray(np.broadcast_to(np.asarray(inputs["a_norm_g"][idx], f)[None, :], (128, 2048)))
    sh["fg"] = np.ascontiguousarray(np.asarray(inputs["final_g"], f).reshape(NCH, 128).T)
    _prep_attn_shared(inputs, need, sh)
    return sh


def _rope_tables(hd, rank):
    quarter = hd // 4
    t = np.arange(2048) + rank * 2048
    row = (t // 64).astype(np.float32)
    col = (t % 64).astype(np.float32)
    inv = (10000.0 ** (-np.arange(quarter, dtype=np.float32) / quarter)).astype(np.float32)
    tab = np.zeros((2, hd, 2048), np.float32)
    pm = np.zeros((hd, hd), np.float32)
    for d in range(hd):
        region = d // quarter
        i = d % quarter
        pos = row if region < 2 else col
        ang = (pos * inv[i]).astype(np.float32)
        tab[0, d] = np.cos(ang)
        tab[1, d] = np.sin(ang) * (-1.0 if region % 2 == 0 else 1.0)
        partner = d + quarter if region % 2 == 0 else d - quarter
        pm[partner, d] = 1.0
    return tab, pm


def _c_masks(rank):
    s_ = np.arange(128)[:, None]
    q = np.arange(512)[None, :]
    qi, ql = q // 128, q % 128
    m = np.full((8, 128, 512), NEG, np.float32)
    for r in range(-1, 5):
        vis = ((r == qi - 1) & (s_ >= ql)) | (r == qi) | ((r == qi + 1) & (s_ <= ql))
        m[r + 1] = np.where(vis, 0.0, NEG)
    if rank == 1:
        m[6] = m[0]
    if rank == 0:
        m[7] = m[5]
    return np.ascontiguousarray(m.transpose(1, 0, 2)).reshape(128, 8 * 512)


def _prep_attn_shared(inputs, need, sh):
    f = np.float32
    if "b_qw" in need:
        w = np.asarray(inputs["b_w_qkv"][0], f)
        sh["b_qw"] = np.ascontiguousarray(w[:, :1024].reshape(8, 128, 8, 128).transpose(2, 1, 0, 3)).reshape(8, 128, 1024)
        sh["b_kw"] = np.ascontiguousarray(w[:, 1024:1280].reshape(8, 128, 2, 128).transpose(1, 2, 0, 3)).reshape(128, 2048)
        sh["b_tw"] = np.ascontiguousarray(w[:, 1280:1536].reshape(8, 128, 256).transpose(1, 0, 2)).reshape(128, 2048)
        wo = np.asarray(inputs["b_w_out"][0], f)
        sh["b_ow"] = np.ascontiguousarray(wo.reshape(8, 128, 8, 128).transpose(2, 1, 0, 3)).reshape(8, 128, 1024)
        sh["b_qg"] = np.ascontiguousarray(np.asarray(inputs["b_q_g"][0], f).reshape(128, 1))
        sh["b_kg"] = np.ascontiguousarray(np.asarray(inputs["b_k_g"][0], f).reshape(128, 1))
        sh["b_pm"] = _rope_tables(128, 0)[1]
    if "c_qw" in need:
        w = np.asarray(inputs["c_w_qkv"][0], f)
        sh["c_qw"] = np.ascontiguousarray(w[:, :1024].reshape(8, 128, 8, 128).transpose(2, 1, 0, 3)).reshape(8, 128, 1024)
        wk = w[:, 1024:1152].reshape(8, 128, 2, 1, 64)
        wk = np.broadcast_to(wk, (8, 128, 2, 2, 64))
        sh["c_kw"] = np.ascontiguousarray(wk.transpose(1, 2, 0, 3, 4)).reshape(128, 2048)
        sh["c_tw"] = np.ascontiguousarray(w[:, 1024:1280].reshape(8, 128, 256).transpose(1, 0, 2)).reshape(128, 2048)
        wo = np.asarray(inputs["c_w_out"][0], f)
        sh["c_ow"] = np.ascontiguousarray(wo.reshape(8, 128, 8, 128).transpose(2, 1, 0, 3)).reshape(8, 128, 1024)
        pm64 = _rope_tables(64, 0)[1]
        pm = np.zeros((128, 128), f)
        pm[:64, :64] = pm64
        pm[64:, 64:] = pm64
        sh["c_pm"] = pm
        sh["c_ident"] = np.eye(128, dtype=f)
        sh["c_sink"] = np.ascontiguousarray(np.broadcast_to(np.asarray(inputs["c_sink"][0], f)[None, :], (128, 16)))


def _prep_core(inputs, c, need=()):
    f = np.float32
    b, r = c // 2, c % 2
    xs = np.asarray(inputs["x_sample"], f)[b, r * 2048:(r + 1) * 2048]
    xp = np.asarray(inputs["x_prompt"], f)[2 * c:2 * c + 2].reshape(512, D)
    xx = np.concatenate([xs, xp], 0)
    m = {}
    m["xT"] = np.ascontiguousarray(xx.reshape(NTOK, NCH, 128).transpose(2, 1, 0)).reshape(128, NCH * NTOK)
    cc = np.stack([np.asarray(inputs["c"], f)[b], np.asarray(inputs["c_ctx"], f)], 0)
    m["cT"] = np.ascontiguousarray(cc.reshape(2, NCH, 128).transpose(2, 1, 0)).reshape(128, NCH * 2)
    if "b_ck" in need:
        ck = np.asarray(inputs["cache_b_k"], f)[b, 0]
        m["b_ck"] = np.ascontiguousarray(ck.transpose(2, 1, 0)).reshape(128, 1024)
        cv = np.asarray(inputs["cache_b_v"], f)[b, 0].reshape(4, 128, 256)
        m["b_cv"] = np.ascontiguousarray(cv.transpose(1, 0, 2)).reshape(128, 1024)
        m["b_rope"] = _rope_tables(128, r)[0]
    if "c_ck" in need:
        ck = np.asarray(inputs["cache_c_k"], f)[b, 0]
        ckT = np.ascontiguousarray(ck.transpose(2, 1, 0)).reshape(64, 1024)
        m["c_ck"] = np.ascontiguousarray(np.concatenate([ckT, ckT], 0))
        cv = np.asarray(inputs["cache_c_v"], f)[b, 0].reshape(4, 128, 128)
        m["c_cv"] = np.ascontiguousarray(cv.transpose(1, 0, 2)).reshape(128, 512)
        rt = _rope_tables(64, r)[0]
        m["c_rope"] = np.ascontiguousarray(np.concatenate([rt, rt], 1))
        m["c_mask"] = _c_masks(r)
    return m


def run(inputs, stage=99, trace=False, ncores=8):
    bld = Builder(stage, ncores)
    nc = bld.build()
    need = set(bld.din.keys())
    sh = _prep_shared(inputs, need)
    in_maps = []
    for c in range(ncores):
        m = {k: v for k, v in sh.items() if k in need}
        pc = _prep_core(inputs, c, need)
        m.update({k: v for k, v in pc.items() if k in need})
        assert set(m.keys()) == need, (need - set(m.keys()), set(m.keys()) - need)
        in_maps.append(m)
    res = run_bass_kernel_spmd(nc, in_maps, core_ids=list(range(ncores)), trace=trace)
    return res


def kernel(**inputs):
    res = run(inputs)
    f = np.float32
    y_prompt = np.zeros((16, 256, D), f)
    y_sample = np.zeros((4, 4096, D), f)
    nbk = np.zeros((16, 1, 256, 2, 128), f)
    nbv = np.zeros((16, 1, 256, 2, 128), f)
    nck = np.zeros((16, 1, 256, 2, 64), f)
    ncv = np.zeros((16, 1, 256, 2, 64), f)
    for c in range(8):
        o = res.results[c]
        b, r = c // 2, c % 2
        y = np.asarray(o["yT"], f).reshape(128, NCH, NTOK).transpose(2, 1, 0).reshape(NTOK, D)
        y_sample[b, r * 2048:(r + 1) * 2048] = y[:2048]
        y_prompt[2 * c:2 * c + 2] = y[2048:].reshape(2, 256, D)
        ko = np.asarray(o["b_ko"], f).reshape(128, 2, 2, 256)
        nbk[2 * c:2 * c + 2, 0] = ko.transpose(2, 3, 1, 0)
        vo = np.asarray(o["b_vo"], f).reshape(128, 4, 256).transpose(1, 0, 2).reshape(2, 256, 2, 128)
        nbv[2 * c:2 * c + 2, 0] = vo
        kvo = np.asarray(o["c_kvo"], f).reshape(128, 4, 256).transpose(1, 0, 2).reshape(2, 256, 256)
        nck[2 * c:2 * c + 2, 0] = kvo[:, :, :128].reshape(2, 256, 2, 64)
        ncv[2 * c:2 * c + 2, 0] = kvo[:, :, 128:].reshape(2, 256, 2, 64)
    return (y_prompt, y_sample, nbk, nbv, nck, ncv)
```

```python
import numpy as np
import ml_dtypes
import concourse.bass as bass
import concourse.mybir as mybir
from concourse.bass_utils import run_bass_kernel_spmd
from contextlib import ExitStack

F32 = mybir.dt.float32
BF16 = mybir.dt.bfloat16
AF = mybir.ActivationFunctionType
ALU = mybir.AluOpType

D = 1024
NCH = 8
NTOK = 2560
NTG = 5
TG = 512
DFF = 2816
NFF = 22
EPS = 1e-6
DEPTH = 4
FB = 4
NEG = -30000.0


class Buf:
    __slots__ = ("name", "lo", "hi", "w", "r", "aliases")

    def __init__(self, name, lo=None, hi=None):
        self.name = name
        self.lo = lo
        self.hi = hi
        self.w = None
        self.r = {}
        self.aliases = []


class Op:
    __slots__ = ("eng", "fn", "deps", "is_dma", "needs_sig", "sigsem", "sigval")

    def __init__(self, eng, fn, is_dma):
        self.eng = eng
        self.fn = fn
        self.is_dma = is_dma
        self.deps = []
        self.needs_sig = False
        self.sigsem = None
        self.sigval = 0


ENGS = ("pe", "act", "dve", "pool", "sp")
NPOOLSEM = 8


class Prog:
    def __init__(self):
        self.ops = {e: [] for e in ENGS}
        self.arena = []
        self.acache = {}
        self.dma_cnt = {"sp": 0, "pool": 0, "act": 0}
        self.dma_last = {}
        self.cc_cnt = 0

    def abuf(self, name, lo, hi):
        key = (name, lo, hi)
        if key in self.acache:
            return self.acache[key]
        b = Buf(name, lo, hi)
        self.acache[key] = b
        for o in self.arena:
            if o.lo < hi and lo < o.hi:
                o.aliases.append(b)
                b.aliases.append(o)
        self.arena.append(b)
        return b

    def add(self, eng, fn, reads=(), writes=(), dma=False, cc=False, force=()):
        op = Op(eng, fn, dma)
        for d in force:
            d.needs_sig = True
            op.deps.append(d)
        if cc:
            self.cc_cnt += 1
            op.sigsem = ("cc",)
            op.sigval = self.cc_cnt
            op.needs_sig = True
        deps = set()
        for b in reads:
            for s in [b] + b.aliases:
                if s.w is not None:
                    deps.add(s.w)
        for b in writes:
            for s in [b] + b.aliases:
                if s.w is not None:
                    deps.add(s.w)
                deps.update(s.r.values())
        for b in reads:
            b.r[id(op) if dma else eng] = op
        for b in writes:
            b.w = op
            b.r = {}
        if dma:
            i = self.dma_cnt[eng]
            self.dma_cnt[eng] = i + 1
            slot = i % NPOOLSEM
            op.sigsem = ("dma", eng, slot)
            op.sigval = 16 * (i // NPOOLSEM + 1)
            prev = self.dma_last.get((eng, slot))
            if prev is not None:
                deps.add(prev)
            self.dma_last[(eng, slot)] = op
        for d in deps:
            if d is op:
                continue
            if (not d.is_dma) and (not dma) and d.eng == "pe" and eng == "pe":
                continue
            if d.sigsem == ("cc",):
                op.deps.append(d)
                continue
            d.needs_sig = True
            op.deps.append(d)
        self.ops[eng].append(op)
        return op

    def finalize(self):
        for e in ENGS:
            c = 0
            for op in self.ops[e]:
                if not op.is_dma and op.needs_sig and op.sigsem != ("cc",):
                    c += 1
                    op.sigsem = ("eng", e)
                    op.sigval = c

    def emit(self, eng, handle, sems):
        known = {}
        for op in self.ops[eng]:
            need = {}
            for d in op.deps:
                if need.get(d.sigsem, 0) < d.sigval:
                    need[d.sigsem] = d.sigval
            for k, v in need.items():
                if known.get(k, 0) < v:
                    handle.wait_ge(sems[k], v)
                    known[k] = v
            ins = op.fn(handle)
            if op.is_dma:
                ins.then_inc(sems[op.sigsem], 16)
            elif op.sigsem == ("cc",):
                ins.then_inc(sems[op.sigsem])
            elif op.needs_sig:
                ins.then_inc(sems[op.sigsem], 1)
        if eng in self.dma_cnt:
            n = self.dma_cnt[eng]
            for slot in range(min(n, NPOOLSEM)):
                cnt = (n - slot + NPOOLSEM - 1) // NPOOLSEM
                k = ("dma", eng, slot)
                if known.get(k, 0) < 16 * cnt:
                    handle.wait_ge(sems[k], 16 * cnt)


class Builder:
    def __init__(self, stage=99, ncores=8):
        self.stage = stage
        self.ncores = ncores
        self.nc = bass.Bass("TRN2", target_bir_lowering=False)
        self.P = Prog()
        self.es = ExitStack()
        self.din = {}
        self.dout = {}

    def inp(self, name, shape, dt=F32):
        t = self.nc.dram_tensor(name, list(shape), dt, kind="ExternalInput")
        self.din[name] = t
        return t.ap()

    def outp(self, name, shape, dt=F32):
        t = self.nc.dram_tensor(name, list(shape), dt, kind="ExternalOutput")
        self.dout[name] = t
        return t.ap()

    def sb(self, name, shape, dt):
        return self.es.enter_context(self.nc.sbuf_tensor(name, list(shape), dt))

    def mm(self, out, lhsT, rhs, start, stop, reads, writes, force=()):
        return self.P.add("pe", lambda e, o=out, l=lhsT, r=rhs, s=start, t=stop:
                          e.matmul(o, l, r, start=s, stop=t), reads, writes, force=force)

    def act(self, out, in_, func, reads, writes, scale=None, bias=None):
        kw = {}
        if scale is not None:
            kw["scale"] = scale
        if bias is not None:
            kw["bias"] = bias
        self.P.add("act", lambda e, o=out, i=in_, f=func, kw=kw:
                   e.activation(o, i, f, **kw), reads, writes)

    def tt(self, out, in0, in1, op, reads, writes, eng="dve"):
        self.P.add(eng, lambda e, o=out, a=in0, b=in1, p=op:
                   e.tensor_tensor(o, a, b, p), reads, writes)

    def stt(self, out, in0, scalar, in1, op0, op1, reads, writes):
        self.P.add("dve", lambda e, o=out, a=in0, s=scalar, b=in1, p0=op0, p1=op1:
                   e.scalar_tensor_tensor(o, a, s, b, p0, p1), reads, writes)

    def ts(self, out, in0, s1, s2, op0, op1, reads, writes, eng="dve"):
        if op1 is None:
            self.P.add(eng, lambda e, o=out, a=in0, x=s1, p0=op0:
                       e.tensor_scalar(o, a, x, None, p0), reads, writes)
        else:
            self.P.add(eng, lambda e, o=out, a=in0, x=s1, y=s2, p0=op0, p1=op1:
                       e.tensor_scalar(o, a, x, y, p0, p1), reads, writes)

    def copy(self, out, in_, reads, writes, eng="dve"):
        self.P.add(eng, lambda e, o=out, i=in_: e.tensor_copy(o, i), reads, writes)

    def recip(self, out, in_, reads, writes):
        self.P.add("dve", lambda e, o=out, i=in_: e.reciprocal(o, i), reads, writes)

    def memset(self, ap, val, writes, eng="dve"):
        self.P.add(eng, lambda e, a=ap, v=val: e.memset(a, v), (), writes)

    def dma(self, out, in_, reads, writes, eng="sp"):
        self.P.add(eng, lambda e, o=out, i=in_: e.dma_start(out=o, in_=i),
                   reads, writes, dma=True)

    def build(self):
        nc = self.nc
        P = self.P
        xT_d = self.inp("xT", [128, NCH * NTOK])
        cT_d = self.inp("cT", [128, NCH * 2])
        fg_d = self.inp("fg", [128, NCH])
        yT_d = self.outp("yT", [128, NCH * NTOK])

        x_t = self.sb("x", [128, NCH, NTOK], F32)
        h_t = self.sb("h", [128, NCH, NTOK], BF16)
        UBYTES = 84992
        U_t = self.sb("U", [128, UBYTES // 2], BF16)
        ones_t = self.sb("ones", [128, 128], BF16)
        cT_t = self.sb("cTs", [128, NCH * 2], F32)
        sc_t = self.sb("scs", [128, NCH, 2], BF16)
        self.modT2 = [self.sb(f"modT{i}", [128, 72, 2], F32) for i in range(2)]
        self.bm2 = [self.sb(f"bms{i}", [128, 72], F32) for i in range(2)]
        self.gn2 = [self.sb(f"gns{i}", [128, 3, NCH], F32) for i in range(2)]
        fg_t = self.sb("fgs", [128, NCH], F32)
        self.A2 = [self.sb(f"As{i}", [128, 3, NCH, 2], F32) for i in range(2)]
        self.G2 = [self.sb(f"Gs{i}", [128, 3, NCH, 2], F32) for i in range(2)]
        self.bm_b2 = [Buf(f"bm{i}") for i in range(2)]
        self.gn_b2 = [Buf(f"gn{i}") for i in range(2)]
        self.modT_b2 = [Buf(f"modT{i}") for i in range(2)]
        self.A_b2 = [Buf(f"A{i}") for i in range(2)]
        self.G_b2 = [Buf(f"G{i}") for i in range(2)]
        self.sc_t = sc_t
        eps_t = self.sb("epss", [128, 1], F32)
        self.onef_t = self.sb("onef", [128, 1], F32)
        self.b_onef = Buf("onef")
        self.x_t, self.h_t, self.U_t, self.ones_t = x_t, h_t, U_t, ones_t
        self.eps_t = eps_t

        self.ps = [self.es.enter_context(nc.psum_tensor(f"ps{i}", [128, 512], F32)) for i in range(8)]
        self.psb = [Buf(f"psb{i}") for i in range(8)]

        self.xb = [[Buf(f"x{c}_{t}") for t in range(NTG)] for c in range(NCH)]
        self.hb = [[Buf(f"h{c}_{t}") for t in range(NTG)] for c in range(NCH)]
        b_ones = Buf("ones")
        b_cT, b_sc, b_fg = (Buf(n) for n in ("cT", "sc", "fg"))
        b_eps = Buf("eps")
        self.b_ones, self.b_eps, self.b_sc = b_ones, b_eps, b_sc
        self.mod_items = []

        def U(lo, nbytes, dt, shape_rest):
            es = 2 if dt == BF16 else 4
            n = nbytes // es
            if dt == BF16:
                ap = U_t[:, lo // 2: lo // 2 + n]
            else:
                ap = U_t[:, lo // 2: lo // 2 + 2 * n].bitcast(F32)
            return ap
        self.U = U

        self.memset(ones_t[:], 1.0, [b_ones])
        self.memset(eps_t[:], EPS, [b_eps])
        self.memset(self.onef_t[:], 1.0, [self.b_onef])
        for t in range(NTG):
            self.dma(x_t[:, :, t * TG:(t + 1) * TG],
                     xT_d.rearrange("p (c n) -> p c n", c=NCH)[:, :, t * TG:(t + 1) * TG],
                     [], [self.xb[c][t] for c in range(NCH)])
        self.dma(cT_t[:], cT_d, [], [b_cT])
        self.dma(fg_t[:], fg_d, [], [b_fg])
        self.act(sc_t[:].rearrange("p k c -> p (k c)"), cT_t[:], AF.Silu, [b_cT], [b_sc])

        WS = 8192
        WBASE = 40960
        self.wslot_ap = [U(WBASE + i * WS, WS, BF16, None) for i in range(4)]
        self.wslot_b = [P.abuf(f"wslot{i}", WBASE + i * WS, WBASE + (i + 1) * WS) for i in range(4)]
        self.wctr = 0

        for l in range(DEPTH):
            if l >= self.stage_layers():
                break
            self.inp(f"wgu{l}", [2, NFF, 128, 2 * NCH * 128])
            self.inp(f"wd{l}", [2, NFF, 128, D])
            if l == 0:
                self.mods_schedule(0)
                self.mod_pump(1 + 24 + 1)
            else:
                self.mod_flush()
            self.set_layer(l)
            self.norm_mod(s=0)
            self.ffn(l, 0)
            if self.stage_partial(l, 1):
                break
            self.norm_mod(s=1)
            kind = l % 3
            if kind == 0:
                self.mixer_a(l, l // 3)
            else:
                self.attn_layer(l, "B" if kind == 1 else "C")
            if self.stage_partial(l, 2):
                break
            self.norm_mod(s=2)
            if l + 1 < self.stage_layers():
                self.mods_schedule(l + 1)
            self.ffn(l, 1)

        if self.stage >= 99:
            self.final_norm(fg_t, b_fg)
        for t in range(NTG):
            self.dma(yT_d.rearrange("p (c n) -> p c n", c=NCH)[:, :, t * TG:(t + 1) * TG],
                     x_t[:, :, t * TG:(t + 1) * TG],
                     [self.xb[c][t] for c in range(NCH)], [])

        P.finalize()
        sems = {}
        for e in ENGS:
            sems[("eng", e)] = self.es.enter_context(nc.semaphore(f"s_{e}"))
        for e in ("sp", "pool"):
            for i in range(NPOOLSEM):
                sems[("dma", e, i)] = self.es.enter_context(nc.semaphore(f"d_{e}{i}"))
        sems[("cc",)] = self.es.enter_context(nc.semaphore("s_cc"))
        with nc.Block() as block:
            @block.tensor
            def _(t):
                P.emit("pe", t, sems)

            @block.scalar
            def _(a):
                P.emit("act", a, sems)

            @block.vector
            def _(v):
                P.emit("dve", v, sems)

            @block.gpsimd
            def _(g):
                P.emit("pool", g, sems)

            @block.sync
            def _(s):
                P.emit("sp", s, sems)
        self.es.close()
        return nc

    def set_layer(self, l):
        i = l % 2
        self.modT_t, self.A_t, self.G_t = self.modT2[i], self.A2[i], self.G2[i]
        self.b_modT, self.b_A, self.b_G = self.modT_b2[i], self.A_b2[i], self.G_b2[i]

    def mods_schedule(self, l):
        P, U = self.P, self.U
        i = l % 2
        wm_d = self.inp(f"wm{l}", [72, 128, NCH * 128])
        bm_d = self.inp(f"bm{l}", [128, 72])
        gn_d = self.inp(f"gn{l}", [128, 3 * NCH])
        modT_t, bm_t, gn_t, A_t, G_t = self.modT2[i], self.bm2[i], self.gn2[i], self.A2[i], self.G2[i]
        b_modT, b_bm, b_gn, b_A, b_G = self.modT_b2[i], self.bm_b2[i], self.gn_b2[i], self.A_b2[i], self.G_b2[i]
        pm = self.ps[7]
        mslot_ap = [U(77824 + j * 2048, 2048, BF16, None).rearrange("p (k c) -> p k c", k=NCH) for j in range(2)]
        mslot_b = [P.abuf(f"mslot{j}", 77824 + j * 2048, 77824 + (j + 1) * 2048) for j in range(2)]
        items = []

        def first():
            self.dma(bm_t[:], bm_d, [], [b_bm])
            self.dma(gn_t[:].rearrange("p s c -> p (s c)"), gn_d, [], [b_gn])
            load(0)
            load(1)

        def load(fc):
            j = fc % 2
            self.dma(mslot_ap[j], wm_d[fc].rearrange("p (k c) -> p k c", k=NCH), [], [mslot_b[j]], eng="pool")

        def work(fc):
            j = fc % 2
            for k in range(NCH):
                self.mm(pm[:, fc * 2:fc * 2 + 2], mslot_ap[j][:, k, :], self.sc_t[:, k, :],
                        k == 0, k == NCH - 1, [mslot_b[j], self.b_sc], [self.psb[7]])
            if fc + 2 < 72:
                load(fc + 2)

        def final(s_):
            c0, c1 = 24 * s_, 24 * (s_ + 1)
            for ctx in range(2):
                self.tt(modT_t[:, c0:c1, ctx], pm[:, 2 * c0:2 * c1].rearrange("p (f c) -> p f c", c=2)[:, :, ctx],
                        bm_t[:, c0:c1], ALU.add, [b_bm], [self.psb[7], b_modT])
            for ctx in range(2):
                self.stt(A_t[:, s_, :, ctx], modT_t[:, (3 * s_ + 1) * 8:(3 * s_ + 2) * 8, ctx], 1.0,
                         gn_t[:, s_, :], ALU.add, ALU.mult, [b_modT, b_gn], [b_A])
                self.ts(G_t[:, s_, :, ctx], modT_t[:, (3 * s_ + 2) * 8:(3 * s_ + 3) * 8, ctx],
                        0.5 if s_ != 1 else 1.0, None, ALU.mult, None, [b_modT], [b_G])
        items.append(first)
        for fc in range(72):
            items.append(lambda fc=fc: work(fc))
            if fc % 24 == 23:
                items.append(lambda s_=fc // 24: final(s_))
        self.mod_items = items

    def mod_pump(self, n=1):
        for _ in range(n):
            if self.mod_items:
                self.mod_items.pop(0)()

    def mod_flush(self):
        while self.mod_items:
            self.mod_items.pop(0)()

    def stage_layers(self):
        return {1: 1, 2: 1, 3: 2, 4: 2, 5: 3, 6: 3}.get(self.stage, DEPTH)

    def stage_partial(self, l, point):
        if self.stage == 1:
            return l == 0 and point == 1
        if self.stage == 2:
            return l == 0 and point == 2
        if self.stage == 4:
            return l == 1 and point == 2
        if self.stage == 6:
            return l == 2 and point == 2
        return False

    def final_norm(self, fg_t, b_fg):
        P, U = self.P, self.U
        x_t = self.x_t
        sq_ap = [U(i * 8192, 8192, BF16, None).rearrange("p (c n) -> p c n", c=NCH) for i in range(2)]
        sq_b = [P.abuf(f"sq{i}", i * 8192, (i + 1) * 8192) for i in range(2)]
        sd_ap = U(16384, 10240, F32, None)
        sd_b = P.abuf("sd", 16384, 26624)
        rs_ap = U(26624, 10240, F32, None)
        rs_b = P.abuf("rstd", 26624, 36864)
        for t in range(NTG):
            i = t % 2
            pb = 4 + (t % 4)
            self.act(sq_ap[i], x_t[:, :, t * TG:(t + 1) * TG], AF.Square,
                     [self.xb[c][t] for c in range(NCH)], [sq_b[i]])
            for c in range(NCH):
                self.mm(self.ps[pb][:], self.ones_t[:], sq_ap[i][:, c, :], c == 0, c == NCH - 1,
                        [self.b_ones, sq_b[i]], [self.psb[pb]])
            self.act(sd_ap[:, t * TG:(t + 1) * TG], self.ps[pb][:], AF.Sqrt,
                     [self.b_eps], [self.psb[pb], sd_b], scale=1.0 / D, bias=self.eps_t[:, 0:1])
        self.recip(rs_ap, sd_ap, [sd_b], [rs_b])
        for t in range(NTG):
            for c in range(NCH):
                self.stt(x_t[:, c, t * TG:(t + 1) * TG], x_t[:, c, t * TG:(t + 1) * TG], fg_t[:, c:c + 1],
                         rs_ap[:, t * TG:(t + 1) * TG], ALU.mult, ALU.mult,
                         [b_fg, rs_b], [self.xb[c][t]])

    def ctx_of(self, tg):
        return 0 if tg < 4 else 1

    def norm_mod(self, s):
        P, U = self.P, self.U
        x_t, h_t = self.x_t, self.h_t
        sq_ap = [U(i * 8192, 8192, BF16, None).rearrange("p (c n) -> p c n", c=NCH) for i in range(2)]
        sq_b = [P.abuf(f"sq{i}", i * 8192, (i + 1) * 8192) for i in range(2)]
        sd_ap = [U(16384 + t * 2048, 2048, F32, None) for t in range(NTG)]
        sd_b = [P.abuf(f"sd{t}", 16384 + t * 2048, 16384 + (t + 1) * 2048) for t in range(NTG)]
        rs_ap = [U(26624 + t * 2048, 2048, F32, None) for t in range(NTG)]
        rs_b = [P.abuf(f"rstd{t}", 26624 + t * 2048, 26624 + (t + 1) * 2048) for t in range(NTG)]
        tmp_ap = [U(36864 + i * 2048, 2048, F32, None) for i in range(2)]
        tmp_b = [P.abuf(f"tmp{i}", 36864 + i * 2048, 36864 + (i + 1) * 2048) for i in range(2)]
        kk = [0]

        def sq_(t):
            self.act(sq_ap[t % 2], x_t[:, :, t * TG:(t + 1) * TG], AF.Square,
                     [self.xb[c][t] for c in range(NCH)], [sq_b[t % 2]])

        def mm_(t):
            pb = 4 + (t % 4)
            for c in range(NCH):
                self.mm(self.ps[pb][:], self.ones_t[:], sq_ap[t % 2][:, c, :], c == 0, c == NCH - 1,
                        [self.b_ones, sq_b[t % 2]], [self.psb[pb]])

        def rs_(t):
            pb = 4 + (t % 4)
            self.act(sd_ap[t], self.ps[pb][:], AF.Sqrt,
                     [self.b_eps], [self.psb[pb], sd_b[t]], scale=1.0 / D, bias=self.eps_t[:, 0:1])
            self.recip(rs_ap[t], sd_ap[t], [sd_b[t]], [rs_b[t]])

        def stage_b(t):
            ctx = self.ctx_of(t)
            for c in range(NCH):
                i = kk[0] % 2
                kk[0] += 1
                self.stt(tmp_ap[i], x_t[:, c, t * TG:(t + 1) * TG], self.A_t[:, s, c, ctx:ctx + 1],
                         rs_ap[t], ALU.mult, ALU.mult,
                         [self.xb[c][t], self.b_A, rs_b[t]], [tmp_b[i]])
                self.act(h_t[:, c, t * TG:(t + 1) * TG], tmp_ap[i], AF.Identity,
                         [tmp_b[i], self.b_modT], [self.hb[c][t]],
                         bias=self.modT_t[:, (3 * s) * 8 + c, ctx:ctx + 1])

        sq_(0)
        mm_(0)
        for t in range(NTG):
            if t + 1 < NTG:
                sq_(t + 1)
            rs_(t)
            if t + 1 < NTG:
                mm_(t + 1)
        for t in range(NTG):
            stage_b(t)

    def ffn(self, l, si):
        P, U = self.P, self.U
        s = 0 if si == 0 else 2
        wgu_d, wd_d = self.din[f"wgu{l}"].ap(), self.din[f"wd{l}"].ap()
        a_ap = U(0, FB * NTOK * 2, BF16, None).rearrange("p (j n) -> p j n", j=FB)
        a_b = [[P.abuf(f"a{j}_{t}", (j * NTOK + t * TG) * 2, (j * NTOK + (t + 1) * TG) * 2)
                for t in range(NTG)] for j in range(FB)]
        sg_ap = [U(73728 + i * 2048, 2048, F32, None) for i in range(2)]
        sg_b = [P.abuf(f"sg{i}", 73728 + i * 2048, 73728 + (i + 1) * 2048) for i in range(2)]
        fs_ap = list(self.wslot_ap) + [U(20480 + i * 8192, 8192, BF16, None) for i in range(2)]
        fs_b = list(self.wslot_b) + [P.abuf(f"fslot{i}", 20480 + i * 8192, 20480 + (i + 1) * 8192) for i in range(2)]
        bsz = [NFF % FB] + [FB] * (NFF // FB) if NFF % FB else [FB] * (NFF // FB)
        bstart = [sum(bsz[:i]) for i in range(len(bsz))]
        nblk = len(bsz)
        mods_pending = len(self.mod_items) > 0
        ndb = 3 if mods_pending else 4
        it = 0
        fctr = [0]

        def load_block(b):
            j0 = bstart[b]
            nj = bsz[b]
            slots = {}
            for jj in range(0, nj, 2):
                sidx = fctr[0] % 6
                fctr[0] += 1
                n2 = min(2, nj - jj)
                wap = fs_ap[sidx].rearrange("p (j g k c) -> p j g k c", j=2, g=2, k=NCH)
                self.dma(wap[:, 0:n2], wgu_d[si, j0 + jj:j0 + jj + n2].rearrange("j p (g k c) -> p j g k c", g=2, k=NCH),
                         [], [fs_b[sidx]], eng="pool")
                for q in range(n2):
                    slots[jj + q] = (sidx, wap, q)
            sidx_d = fctr[0] % 6
            fctr[0] += 1
            wdap = fs_ap[sidx_d].rearrange("p (j d) -> p j d", j=FB)
            self.dma(wdap[:, 0:nj], wd_d[si, j0:j0 + nj].rearrange("j p d -> p j d"),
                     [], [fs_b[sidx_d]], eng="pool")
            return (j0, nj, slots, sidx_d, wdap)

        nxt = load_block(0)
        for b in range(nblk):
            j0, nj, slots, sidx_d, wdap = nxt
            if b + 1 < nblk:
                nxt = load_block(b + 1)
            for jj in range(nj):
                sidx, wap, q = slots[jj]
                for t in range(NTG):
                    pg = (it % 2) * 2
                    pu = pg + 1
                    i2 = it % 2
                    it += 1
                    for k in range(NCH):
                        self.mm(self.ps[pg][:], wap[:, q, 0, k, :], self.h_t[:, k, t * TG:(t + 1) * TG],
                                k == 0, k == NCH - 1, [fs_b[sidx], self.hb[k][t]], [self.psb[pg]])
                    for k in range(NCH):
                        self.mm(self.ps[pu][:], wap[:, q, 1, k, :], self.h_t[:, k, t * TG:(t + 1) * TG],
                                k == 0, k == NCH - 1, [fs_b[sidx], self.hb[k][t]], [self.psb[pu]])
                    self.act(sg_ap[i2], self.ps[pg][:], AF.Silu, [], [self.psb[pg], sg_b[i2]])
                    self.tt(a_ap[:, jj, t * TG:(t + 1) * TG], self.ps[pu][:], sg_ap[i2], ALU.mult,
                            [sg_b[i2]], [self.psb[pu], a_b[jj][t]])
                self.mod_pump()
            for d in range(NCH):
                for t in range(NTG):
                    ctx = self.ctx_of(t)
                    pd = 4 + (it % ndb)
                    it += 1
                    for jj in range(nj):
                        self.mm(self.ps[pd][:], wdap[:, jj, d * 128:(d + 1) * 128], a_ap[:, jj, t * TG:(t + 1) * TG],
                                jj == 0, jj == nj - 1, [fs_b[sidx_d], a_b[jj][t]], [self.psb[pd]])
                    self.stt(self.x_t[:, d, t * TG:(t + 1) * TG], self.ps[pd][:], self.G_t[:, s, d, ctx:ctx + 1],
                             self.x_t[:, d, t * TG:(t + 1) * TG], ALU.mult, ALU.add,
                             [self.b_G], [self.psb[pd], self.xb[d][t]])
                self.mod_pump()
        self.mod_flush()

    def next_slot(self):
        si = self.wctr % 4
        self.wctr += 1
        return si

    def mixer_a(self, l, idx):
        P, U = self.P, self.U
        x_t, h_t = self.x_t, self.h_t
        awuv_d = self.inp(f"awuv{idx}", [8, 128, 2 * NCH * 256])
        awout_d = self.inp(f"awout{idx}", [8, 128, 2 * D])
        awsT_d = self.inp(f"awsT{idx}", [128, 8 * 128])
        abs_d = self.inp(f"abs{idx}", [1, 8 * 128])
        ang_d = self.inp(f"ang{idx}", [128, 2048])
        sqa_ap = [U(i * 1024, 1024, BF16, None) for i in range(2)]
        sqa_b = [P.abuf(f"sqa{i}", i * 1024, (i + 1) * 1024) for i in range(2)]
        sd_ap = U(2048, 10240, F32, None)
        sd_b = P.abuf("a_sd", 2048, 12288)
        rs_ap = U(12288, 10240, F32, None)
        rs_b = P.abuf("a_rs", 12288, 22528)
        rT_ap = U(22528, 128, F32, None)
        rT_b = P.abuf("a_rT", 22528, 22656)
        ga_ap = U(22656, 8192, F32, None)
        ga_b = P.abuf("a_ga", 22656, 30848)
        ws_ap = U(30848, 2048, BF16, None).rearrange("p (g q) -> p g q", g=8)
        ws_b = P.abuf("a_ws", 30848, 32896)
        bs_ap = U(32896, 2048, BF16, None)
        bs_b = P.abuf("a_bs", 32896, 34944)
        vn_ap = U(2048, 10240, BF16, None).rearrange("p (t c) -> p t c", t=20)
        vn_b = [P.abuf(f"a_vn{t}", 2048 + t * 512, 2048 + (t + 1) * 512) for t in range(20)]
        u_ap = U(12288, 10240, BF16, None).rearrange("p (f n) -> p f n", f=2)
        u_b = [[P.abuf(f"a_u{f}_{t}", 12288 + (f * NTOK + t * TG) * 2, 12288 + (f * NTOK + (t + 1) * TG) * 2)
                for t in range(NTG)] for f in range(2)]
        self.dma(ga_ap, ang_d, [], [ga_b])
        self.dma(ws_ap.rearrange("p g q -> p (g q)"), awsT_d, [], [ws_b], eng="pool")
        self.dma(bs_ap[0:1, :], abs_d, [], [bs_b], eng="pool")
        it = 0
        pend = None
        for g in range(8):
            si = self.next_slot()
            wv = self.wslot_ap[si][:, 0:NCH * 256].rearrange("p (k c) -> p k c", k=NCH)
            self.dma(wv, awuv_d[g].rearrange("p (u k c) -> p u k c", u=2, k=NCH)[:, 1], [], [self.wslot_b[si]], eng="pool")
            for fc in range(2):
                for t in range(NTG):
                    pb = it % 3
                    i2 = it % 2
                    it += 1
                    for k in range(NCH):
                        self.mm(self.ps[pb][:], wv[:, k, fc * 128:(fc + 1) * 128], h_t[:, k, t * TG:(t + 1) * TG],
                                k == 0, k == NCH - 1, [self.wslot_b[si], self.hb[k][t]], [self.psb[pb]])
                    if pend is not None:
                        pend()
                    self.act(sqa_ap[i2], self.ps[pb][:], AF.Square, [], [self.psb[pb], sqa_b[i2]])
                    pend = (lambda t=t, i2=i2, first=(g == 0 and fc == 0), lastf=(g == 7 and fc == 1):
                            self.mm(self.ps[3 + t][:], self.ones_t[:], sqa_ap[i2], first, lastf,
                                    [self.b_ones, sqa_b[i2]], [self.psb[3 + t]]))
        pend()
        for t in range(NTG):
            self.act(sd_ap[:, t * TG:(t + 1) * TG], self.ps[3 + t][:], AF.Sqrt,
                     [self.b_eps], [self.psb[3 + t], sd_b], scale=1.0 / 2048, bias=self.eps_t[:, 0:1])
        self.recip(rs_ap, sd_ap, [sd_b], [rs_b])
        for tile in range(20):
            self.mm(self.ps[0][:, tile:tile + 1], rs_ap[0:1, tile * 128:(tile + 1) * 128], self.onef_t[0:1, 0:1],
                    True, True, [rs_b, self.b_onef], [self.psb[0]])
        self.copy(rT_ap[:, 0:20], self.ps[0][:, 0:20], [], [self.psb[0], rT_b])
        def load_g(g):
            sa = self.next_slot()
            wuv = self.wslot_ap[sa].rearrange("p (u k c) -> p u k c", u=2, k=NCH)
            self.dma(wuv, awuv_d[g].rearrange("p (u k c) -> p u k c", u=2, k=NCH), [], [self.wslot_b[sa]], eng="pool")
            sb_ = self.next_slot()
            wo = self.wslot_ap[sb_][:, 0:2 * D].rearrange("p (f d) -> p f d", f=2)
            self.dma(wo, awout_d[g].rearrange("p (f d) -> p f d", f=2), [], [self.wslot_b[sb_]], eng="pool")
            return (g, sa, wuv, sb_, wo)

        vctr = [0]

        def v_tile(G_, tile):
            g, sa, wuv, sb_, wo = G_
            pb = 2 + vctr[0] % 2
            vctr[0] += 1
            t = tile // 4
            for k in range(NCH):
                self.mm(self.ps[pb][:, 0:256], h_t[:, k, tile * 128:(tile + 1) * 128], wuv[:, 1, k, :],
                        k == 0, k == NCH - 1, [self.wslot_b[sa], self.hb[k][t]], [self.psb[pb]])
            self.stt(vn_ap[:, tile, :], self.ps[pb][:, 0:256], rT_ap[:, tile:tile + 1],
                     ga_ap[:, g * 256:(g + 1) * 256], ALU.mult, ALU.mult,
                     [rT_b, ga_b], [self.psb[pb], vn_b[tile]])

        cur = load_g(0)
        for tile in range(20):
            v_tile(cur, tile)
        for g in range(8):
            _, sa, wuv, sb_, wo = cur
            for fc in range(2):
                for t in range(NTG):
                    pb = it % 2
                    it += 1
                    for k in range(NCH):
                        self.mm(self.ps[pb][:], wuv[:, 0, k, fc * 128:(fc + 1) * 128], h_t[:, k, t * TG:(t + 1) * TG],
                                k == 0, k == NCH - 1, [self.wslot_b[sa], self.hb[k][t]], [self.psb[pb]])
                    self.act(u_ap[:, fc, t * TG:(t + 1) * TG], self.ps[pb][:], AF.Copy, [], [self.psb[pb], u_b[fc][t]])
            for fc in range(2):
                for t in range(NTG):
                    pb = 4 + it % 2
                    it += 1
                    for n in range(4):
                        tile = t * 4 + n
                        self.mm(self.ps[pb][:, n * 128:(n + 1) * 128], vn_ap[:, tile, fc * 128:(fc + 1) * 128],
                                ws_ap[:, g, :], True, False, [vn_b[tile], ws_b], [self.psb[pb]])
                        self.mm(self.ps[pb][:, n * 128:(n + 1) * 128], self.ones_t[0:1, :],
                                bs_ap[0:1, g * 128:(g + 1) * 128], False, True, [self.b_ones, bs_b], [self.psb[pb]])
                    self.tt(u_ap[:, fc, t * TG:(t + 1) * TG], self.ps[pb][:], u_ap[:, fc, t * TG:(t + 1) * TG],
                            ALU.mult, [], [self.psb[pb], u_b[fc][t]])
            nxt = load_g(g + 1) if g + 1 < 8 else None
            oi = 0
            for d in range(NCH):
                for t in range(NTG):
                    ctx = self.ctx_of(t)
                    pb = 6 + it % 2
                    it += 1
                    for fc in range(2):
                        self.mm(self.ps[pb][:], wo[:, fc, d * 128:(d + 1) * 128], u_ap[:, fc, t * TG:(t + 1) * TG],
                                fc == 0, fc == 1, [self.wslot_b[sb_], u_b[fc][t]], [self.psb[pb]])
                    self.stt(x_t[:, d, t * TG:(t + 1) * TG], self.ps[pb][:], self.G_t[:, 1, d, ctx:ctx + 1],
                             x_t[:, d, t * TG:(t + 1) * TG], ALU.mult, ALU.add,
                             [self.b_G], [self.psb[pb], self.xb[d][t]])
                    if nxt is not None and oi % 2 == 1:
                        v_tile(nxt, oi // 2)
                    oi += 1
            cur = nxt

    def attn_layer(self, l, kind):
        P, U = self.P, self.U
        nc = self.nc
        x_t, h_t = self.x_t, self.h_t
        isB = kind == "B"
        hd = 128 if isB else 64
        HQ = 8 if isB else 16
        KV = 2
        G = HQ // KV
        pre = "b_" if isB else "c_"
        scale = float(hd) ** -0.5
        voff = 0 if isB else 128
        VW = KV * hd
        HW = 128
        NQ = HQ if isB else HQ // 2
        qw_d = self.inp(pre + "qw", [NQ, 128, NCH * HW])
        kw_d = self.inp(pre + "kw", [128, KV * NCH * HW])
        tw_d = self.inp(pre + "tw", [128, NCH * 256])
        ow_d = self.inp(pre + "ow", [NCH, 128, 8 * 128])
        ck_d = self.inp(pre + "ck", [HW, KV * 512])
        cv_d = self.inp(pre + "cv", [128, 4 * VW])
        rope_d = self.inp(pre + "rope", [2, HW, 2048])
        pm_d = self.inp(pre + "pm", [HW, HW])
        if isB:
            qg_d = self.inp("b_qg", [128, 1])
            kg_d = self.inp("b_kg", [128, 1])
            ko_d = self.outp("b_ko", [128, KV * 512])
            vo_d = self.outp("b_vo", [128, 4 * 256])
            XW = 8192
        else:
            ident_d = self.inp("c_ident", [128, 128])
            mask_d = self.inp("c_mask", [128, 8 * 512])
            sink_d = self.inp("c_sink", [128, 16])
            vo_d = self.outp("c_kvo", [128, 4 * 256])
            XW = 768
        kxin = nc.dram_tensor(pre + "kxin", [128, XW], BF16)
        kxout = nc.dram_tensor(pre + "kxout", [256, XW], BF16)
        b_kxin, b_kxout = Buf(pre + "kxin"), Buf(pre + "kxout")
        pm_t = self.sb(pre + "pm_s", [128, 128], BF16)
        b_pm = Buf(pre + "pm")
        self.dma(pm_t[:, :], pm_d, [], [b_pm], eng="pool")
        if isB:
            onesf_t = self.sb("onesf_s", [128, 128], F32)
            b_onesf = Buf("onesf")
            self.memset(onesf_t[:], 1.0, [b_onesf])
            qg_t = self.sb("qg_s", [128, 1], F32)
            kg_t = self.sb("kg_s", [128, 1], F32)
            b_g = Buf("qkg")
            self.dma(qg_t[:], qg_d, [], [b_g])
            self.dma(kg_t[:], kg_d, [], [b_g])
        else:
            ident_t = self.sb("ident_s", [128, 128], BF16)
            sink_t = self.sb("sink_s", [128, 16], F32)
            esink_t = self.sb("esink_s", [128, 16], F32)
            b_id, b_sk, b_esk = Buf("ident"), Buf("sink"), Buf("esink")
            self.dma(ident_t[:], ident_d, [], [b_id], eng="pool")
            self.dma(sink_t[:], sink_d, [], [b_sk])
            self.act(esink_t[:], sink_t[:], AF.Exp, [b_sk], [b_esk])
        if isB:
            Kall_ap = U(0, 18432, BF16, None).rearrange("p (kv n) -> p kv n", kv=KV)
            b_K = P.abuf("B_Kall", 0, 18432)
            Vall_ap = U(18432, 18432, BF16, None).rearrange("p (t c) -> p t c", t=36)
            b_V = P.abuf("B_Vall", 18432, 36864)
            ktmp_ap = U(0, 8192, BF16, None).rearrange("p (kv n) -> p kv n", kv=KV)
            b_ktmp = P.abuf("B_ktmp", 0, 8192)
            vtmp_ap = U(18432, 8192, BF16, None).rearrange("p (t c) -> p t c", t=16)
            b_vtmp = P.abuf("B_vtmp", 18432, 26624)
            KP0, VP0 = 36864, 38912
            O0 = 76800
            RC0, RS0 = 68608, 69632
        else:
            Kown_ap = U(0, 9216, BF16, None).rearrange("p (kv n) -> p kv n", kv=KV)
            b_K = P.abuf("C_Kown", 0, 9216)
            Kctx_ap = U(9216, 2048, BF16, None).rearrange("p (kv n) -> p kv n", kv=KV)
            b_Kc = P.abuf("C_Kctx", 9216, 11264)
            Vowna_ap = U(11264, 9216, BF16, None).rearrange("p (t kv c) -> p t kv c", t=18, kv=KV)
            b_V = P.abuf("C_Vown", 11264, 20480)
            Vctxa_ap = U(20480, 2048, BF16, None).rearrange("p (t kv c) -> p t kv c", t=4, kv=KV)
            b_Vc = P.abuf("C_Vctx", 20480, 22528)
            mask_ap = U(22528, 8192, BF16, None).rearrange("p (m n) -> p m n", m=8)
            b_mask = P.abuf("C_mask", 22528, 30720)
            KP0, VP0 = 78848, 80896
            O0 = 30720
            RC0, RS0 = 76800, 77824
        KP_ap = U(KP0, 2048, BF16, None).rearrange("p (kv n) -> p kv n", kv=KV)
        b_KP = P.abuf(pre + "KP", KP0, KP0 + 2048)
        if isB:
            VP_ap = U(VP0, 4 * VW * 2, BF16, None).rearrange("p (t c) -> p t c", t=4)
            b_VP = P.abuf(pre + "VP", VP0, VP0 + 4 * VW * 2)
        else:
            VPa_ap = U(VP0, 2048, BF16, None).rearrange("p (t kv c) -> p t kv c", t=4, kv=KV)
            b_VP = P.abuf(pre + "VP", VP0, VP0 + 2048)
            self.memset(Vowna_ap[:, :, :, 64:128], 1.0, [b_V])
            self.memset(Vctxa_ap[:, :, :, 64:128], 1.0, [b_Vc])
            self.memset(VPa_ap[:, :, :, 64:128], 1.0, [b_VP])
        O_ap = U(O0, 8192, BF16, None).rearrange("p (h n) -> p h n", h=8)
        b_O = [P.abuf(pre + f"O{i}", O0 + i * 1024, O0 + (i + 1) * 1024) for i in range(8)]

        def f32buf(name, lo, nb=2048):
            return U(lo, nb, F32, None), P.abuf(pre + name, lo, lo + nb)
        rstd_ap, b_rstd = f32buf("rstd", 57344)
        t1_ap, b_t1 = f32buf("t1", 59392)
        t2_ap, b_t2 = f32buf("t2", 61440)
        rz_ap, b_rz = f32buf("rz", 63488)
        kf_ap, b_kf = f32buf("kf", 65536)
        vf_ap, b_vf = f32buf("vf", 67584, 1024)
        if not isB:
            zz_ap, b_zz = f32buf("zz", 68608)

        def bfbuf(name, lo, nb=1024):
            return U(lo, nb, BF16, None), P.abuf(pre + name, lo, lo + nb)
        Q_ap, b_Q = [None, None], [None, None]
        Q_ap[0], b_Q[0] = bfbuf("Q0", 70656)
        Q_ap[1], b_Q[1] = bfbuf("Q1", 71680)
        raw_ap, b_raw = bfbuf("raw", 72704)
        sq_ap, b_sq = bfbuf("sqh", 73728)
        PT_ap, b_PT = [None, None], [None, None]
        PT_ap[0], b_PT[0] = bfbuf("PT0", 74752)
        PT_ap[1], b_PT[1] = bfbuf("PT1", 75776)
        rC_ap, b_rC = bfbuf("rC", RC0)
        rS_ap, b_rS = bfbuf("rS", RS0)
        ps = self.ps
        psb = self.psb
        WSL = [0, 1]
        PQ, PS2 = 6, 7

        def load_rope(t):
            self.dma(rC_ap[:, :], rope_d[0, :, t * TG:(t + 1) * TG], [], [b_rC], eng="pool")
            self.dma(rS_ap[:, :], rope_d[1, :, t * TG:(t + 1) * TG], [], [b_rS], eng="pool")

        def qk_post_a(src, n, g_t, rope, out_ap, out_b):
            if isB:
                self.act(raw_ap[0:HW, 0:n], src, AF.Identity, [b_g], [psb[PQ], b_raw], scale=g_t[0:HW, 0:1])
                self.act(sq_ap[0:HW, 0:n], src, AF.Square, [], [psb[PQ], b_sq])
            elif not rope:
                self.act(out_ap, src, AF.Copy, [], [psb[PQ]] + out_b)
            else:
                self.act(raw_ap[0:HW, 0:n], src, AF.Copy, [], [psb[PQ], b_raw])

        def qk_post_b(n, rope, out_ap, out_b, f32_out=None):
            if isB:
                self.mm(ps[PS2][0:HW, 0:n], self.ones_t[0:HW, 0:HW], sq_ap[0:HW, 0:n], True, True,
                        [self.b_ones, b_sq], [psb[PS2]])
                self.act(rstd_ap[0:HW, 0:n], ps[PS2][0:HW, 0:n], AF.Ln, [self.b_eps], [psb[PS2], b_rstd],
                         scale=1.0 / hd, bias=self.eps_t[0:HW, 0:1])
                self.act(rstd_ap[0:HW, 0:n], rstd_ap[0:HW, 0:n], AF.Exp, [], [b_rstd], scale=-0.5)
            elif not rope:
                return
            if rope:
                self.mm(ps[PS2][0:HW, 0:n], pm_t[0:HW, 0:HW], raw_ap[0:HW, 0:n], True, True, [b_pm, b_raw], [psb[PS2]])
                self.tt(t1_ap[0:HW, 0:n], raw_ap[0:HW, 0:n], rC_ap[0:HW, 0:n], ALU.mult, [b_raw, b_rC], [b_t1])
                self.tt(t2_ap[0:HW, 0:n], ps[PS2][0:HW, 0:n], rS_ap[0:HW, 0:n], ALU.mult, [b_rS], [psb[PS2], b_t2])
                if isB:
                    self.tt(t1_ap[0:HW, 0:n], t1_ap[0:HW, 0:n], t2_ap[0:HW, 0:n], ALU.add, [b_t2], [b_t1])
                    self.tt(out_ap, t1_ap[0:HW, 0:n], rstd_ap[0:HW, 0:n], ALU.mult, [b_t1, b_rstd], out_b)
                else:
                    self.tt(out_ap, t1_ap[0:HW, 0:n], t2_ap[0:HW, 0:n], ALU.add, [b_t1, b_t2], out_b)
            else:
                if f32_out is not None:
                    self.tt(f32_out[0], raw_ap[0:HW, 0:n], rstd_ap[0:HW, 0:n], ALU.mult, [b_raw, b_rstd], [f32_out[1]])
                    self.act(out_ap, f32_out[0], AF.Copy, [f32_out[1]], out_b)
                else:
                    self.tt(out_ap, raw_ap[0:HW, 0:n], rstd_ap[0:HW, 0:n], ALU.mult, [b_raw, b_rstd], out_b)

        def qk_post(src, n, g_t, rope, out_ap, out_b, f32_out=None):
            qk_post_a(src, n, g_t, rope, out_ap, out_b)
            qk_post_b(n, rope, out_ap, out_b, f32_out)

        sk = WSL[0]
        kw = self.wslot_ap[sk][:, 0:KV * NCH * HW].rearrange("p (kv k c) -> p kv k c", kv=KV, k=NCH)
        self.dma(kw, kw_d.rearrange("p (kv k c) -> p kv k c", kv=KV, k=NCH), [], [self.wslot_b[sk]], eng="pool")
        st = WSL[1]
        tw = self.wslot_ap[st][:, 0:NCH * 256].rearrange("p (k c) -> p k c", k=NCH)
        self.dma(tw, tw_d.rearrange("p (k c) -> p k c", k=NCH), [], [self.wslot_b[st]], eng="pool")
        if not isB:
            self.dma(mask_ap.rearrange("p m n -> p (m n)"), mask_d, [], [b_mask], eng="pool")
        vit = [0]

        def v_tile(tile):
            pb = vit[0] % 2
            vit[0] += 1
            tt_ = tile // 4
            for k in range(NCH):
                self.mm(ps[pb][:, 0:256], h_t[:, k, tile * 128:(tile + 1) * 128], tw[:, k, :], k == 0, k == NCH - 1,
                        [self.wslot_b[st], self.hb[k][tt_]], [psb[pb]])
            if tile < 16:
                if isB:
                    self.act(vtmp_ap[:, tile, :], ps[pb][:, 0:256], AF.Copy, [], [psb[pb], b_vtmp])
                else:
                    self.act(Vowna_ap[:, tile + 1, :, 0:64], ps[pb][:, 128:256].rearrange("p (kv d) -> p kv d", kv=KV),
                             AF.Copy, [], [psb[pb], b_V])
            else:
                if isB:
                    self.act(VP_ap[:, tile - 16, :], ps[pb][:, voff:voff + VW], AF.Copy, [], [psb[pb], b_VP])
                else:
                    self.act(VPa_ap[:, tile - 16, :, 0:64], ps[pb][:, 128:256].rearrange("p (kv d) -> p kv d", kv=KV),
                             AF.Copy, [], [psb[pb], b_VP])
                self.copy(vf_ap[:, :], ps[pb][:, 0:256], [], [psb[pb], b_vf])
                self.dma(vo_d.rearrange("p (t c) -> p t c", t=4)[:, tile - 16, :], vf_ap[:, :], [b_vf], [])

        vnext = 0
        for t in range(NTG):
            if t < 4:
                load_rope(t)
            for kv in range(KV):
                for k in range(NCH):
                    self.mm(ps[PQ][0:HW, :], kw[:, kv, k, :], h_t[:, k, t * TG:(t + 1) * TG], k == 0, k == NCH - 1,
                            [self.wslot_b[sk], self.hb[k][t]], [psb[PQ]])
                for _ in range(2):
                    v_tile(vnext)
                    vnext += 1
                if t < 4:
                    if isB:
                        dst, dstb = ktmp_ap[0:HW, kv, t * TG:(t + 1) * TG], [b_ktmp]
                    else:
                        dst, dstb = Kown_ap[0:HW, kv, 128 + t * TG:128 + (t + 1) * TG], [b_K]
                    qk_post(ps[PQ][0:HW, :], TG, kg_t if isB else None, True, dst, dstb)
                else:
                    qk_post(ps[PQ][0:HW, :], TG, kg_t if isB else None, False, KP_ap[0:HW, kv, :], [b_KP],
                            f32_out=(kf_ap[0:HW, :], b_kf) if isB else None)
                    if isB:
                        self.dma(ko_d.rearrange("p (kv n) -> p kv n", kv=KV)[:, kv, :], kf_ap[:, :], [b_kf], [])
        assert vnext == 20
        rg = [[2 * i, 2 * i + 1] for i in range(self.ncores // 2)]
        kxin_ap, kxout_ap = kxin.ap(), kxout.ap()
        if isB:
            self.dma(kxin_ap[:, 0:4096], ktmp_ap.rearrange("p kv n -> p (kv n)"), [b_ktmp], [b_kxin])
            self.dma(kxin_ap[:, 4096:8192], vtmp_ap.rearrange("p t c -> p (t c)"), [b_vtmp], [b_kxin])
        else:
            kx4 = kxin_ap[:, 0:512].rearrange("p (kv e s) -> p kv e s", kv=KV, e=2)
            self.dma(kx4[:, :, 0, :], Kown_ap[:, :, 128:256], [b_K], [b_kxin])
            self.dma(kx4[:, :, 1, :], Kown_ap[:, :, 16 * 128:17 * 128], [b_K], [b_kxin])
            self.dma(kxin_ap[:, 512:640].rearrange("p (kv d) -> p kv d", kv=KV), Vowna_ap[:, 1, :, 0:64], [b_V], [b_kxin])
            self.dma(kxin_ap[:, 640:768].rearrange("p (kv d) -> p kv d", kv=KV), Vowna_ap[:, 16, :, 0:64], [b_V], [b_kxin])
        if self.ncores >= 2:
            P.add("pool", lambda e, a=kxin, b=kxout, r=rg: e.collective_compute(
                "AllGather", ALU.bypass, replica_groups=r, ins=[a.ap().opt()], outs=[b.ap().opt()]),
                [b_kxin], [b_kxout], cc=True)
        if isB:
            for r in range(2):
                self.dma(Kall_ap[:, :, (4 + 16 * r) * 128:(4 + 16 * r) * 128 + 2048],
                         kxout_ap[r * 128:(r + 1) * 128, 0:4096].rearrange("p (kv n) -> p kv n", kv=KV),
                         [b_kxout], [b_K])
                self.dma(Vall_ap[:, 4 + 16 * r:4 + 16 * (r + 1), :],
                         kxout_ap[r * 128:(r + 1) * 128, 4096:8192].rearrange("p (t c) -> p t c", t=16),
                         [b_kxout], [b_V])
            self.dma(Kall_ap[:, :, 0:512], ck_d.rearrange("p (kv n) -> p kv n", kv=KV), [], [b_K], eng="pool")
            self.dma(Vall_ap[:, 0:4, :], cv_d.rearrange("p (t c) -> p t c", t=4), [], [b_V], eng="pool")
        else:
            ko4 = kxout_ap[:, 0:512].rearrange("p (kv e s) -> p kv e s", kv=KV, e=2)
            self.dma(Kown_ap[:, :, 0:128], ko4[0:128, :, 1, :], [b_kxout], [b_K])
            self.dma(Kown_ap[:, :, 17 * 128:18 * 128], ko4[128:256, :, 0, :], [b_kxout], [b_K])
            self.dma(Vowna_ap[:, 0, :, 0:64], kxout_ap[0:128, 640:768].rearrange("p (kv d) -> p kv d", kv=KV), [b_kxout], [b_V])
            self.dma(Vowna_ap[:, 17, :, 0:64], kxout_ap[128:256, 512:640].rearrange("p (kv d) -> p kv d", kv=KV), [b_kxout], [b_V])
            self.dma(Kctx_ap[:, :, :], ck_d.rearrange("p (kv n) -> p kv n", kv=KV), [], [b_Kc], eng="pool")
            self.dma(Vctxa_ap[:, :, :, 0:64], cv_d.rearrange("p (t kv d) -> p t kv d", t=4, kv=KV), [], [b_Vc], eng="pool")

        HPS = 4
        R0 = 0
        for t in [4, 0, 1, 2, 3]:
            ctx = self.ctx_of(t)
            if t < 4:
                load_rope(t)
            tl = []
            if t == 4:
                for sq_i in range(2):
                    for j in range(2):
                        kt = sq_i * 2 + j
                        tl.append((lambda kv, kt=kt: KP_ap[R0:R0 + hd, kv, kt * 128:(kt + 1) * 128],
                                   (lambda kv, kt=kt: VP_ap[:, kt, kv * hd:(kv + 1) * hd]) if isB
                                   else (lambda kv, kt=kt: VPa_ap[:, kt, kv, :]),
                                   [], b_KP, b_VP, sq_i * 256, 256))
            elif isB:
                for kt in range(36):
                    tl.append((lambda kv, kt=kt: Kall_ap[0:hd, kv, kt * 128:(kt + 1) * 128],
                               lambda kv, kt=kt: Vall_ap[:, kt, kv * hd:(kv + 1) * hd], [], b_K, b_V, 0, 512))
            else:
                for j in range(4):
                    tl.append((lambda kv, j=j: Kctx_ap[R0:R0 + hd, kv, j * 128:(j + 1) * 128],
                               lambda kv, j=j: Vctxa_ap[:, j, kv, :], [], b_Kc, b_Vc, 0, 512))
                for r in range(-1, 5):
                    tile = t * 4 + r + 1
                    mi = r + 1
                    if t == 0 and r == -1:
                        mi = 6
                    if t == 3 and r == 4:
                        mi = 7
                    c0 = max(0, r - 1) * 128
                    c1 = min(4, r + 2) * 128
                    masks = []
                    for blk in (r + 1, r - 1):
                        if 0 <= blk <= 3:
                            masks.append((mi, blk * 128 - c0, blk * 128))
                    tl.append((lambda kv, tile=tile: Kown_ap[R0:R0 + hd, kv, tile * 128:(tile + 1) * 128],
                               lambda kv, tile=tile: Vowna_ap[:, tile, kv, :], masks, b_K, b_V, c0, c1 - c0))
            sit = 0

            def prep_steps(u):
                n = TG
                rope = t < 4
                out_ap, out_b = Q_ap[u % 2][:, :], [b_Q[u % 2]]
                src = ps[PQ][:, :]
                st = []

                def mmk(k):
                    nonlocal qw, sq_slot
                    if k == 0 and u % HPS == 0:
                        sq_slot = WSL[(u // HPS) % 2]
                        qw = self.wslot_ap[sq_slot][:, 0:HPS * NCH * HW].rearrange("p (h k c) -> p h k c", h=HPS, k=NCH)
                        self.dma(qw, qw_d[u:u + HPS].rearrange("h p (k c) -> p h k c", k=NCH), [], [self.wslot_b[sq_slot]], eng="pool")
                    self.mm(src, qw[:, u % HPS, k, :], h_t[:, k, t * TG:(t + 1) * TG], k == 0, k == NCH - 1,
                            [self.wslot_b[sq_slot], self.hb[k][t]], [psb[PQ]])
                for k in range(NCH):
                    st.append(lambda k=k: mmk(k))
                if isB:
                    st.append(lambda: self.act(raw_ap[0:HW, 0:n], src, AF.Identity, [b_g], [psb[PQ], b_raw], scale=qg_t[0:HW, 0:1]))
                    st.append(lambda: self.act(sq_ap[0:HW, 0:n], src, AF.Square, [], [psb[PQ], b_sq]))
                    st.append(lambda: self.mm(ps[PS2][0:HW, 0:n], self.ones_t[0:HW, 0:HW], sq_ap[0:HW, 0:n], True, True,
                                              [self.b_ones, b_sq], [psb[PS2]]))
                    st.append(lambda: self.act(rstd_ap[0:HW, 0:n], ps[PS2][0:HW, 0:n], AF.Ln, [self.b_eps], [psb[PS2], b_rstd],
                                               scale=1.0 / hd, bias=self.eps_t[0:HW, 0:1]))
                    st.append(lambda: self.act(rstd_ap[0:HW, 0:n], rstd_ap[0:HW, 0:n], AF.Exp, [], [b_rstd], scale=-0.5))
                elif not rope:
                    st.append(lambda: self.act(out_ap, src, AF.Copy, [], [psb[PQ]] + out_b))
                    return st
                else:
                    st.append(lambda: self.act(raw_ap[0:HW, 0:n], src, AF.Copy, [], [psb[PQ], b_raw]))
                if rope:
                    st.append(lambda: self.mm(ps[PS2][0:HW, 0:n], pm_t[0:HW, 0:HW], raw_ap[0:HW, 0:n], True, True,
                                              [b_pm, b_raw], [psb[PS2]]))
                    st.append(lambda: self.tt(t1_ap[0:HW, 0:n], raw_ap[0:HW, 0:n], rC_ap[0:HW, 0:n], ALU.mult, [b_raw, b_rC], [b_t1]))
                    st.append(lambda: self.tt(t2_ap[0:HW, 0:n], ps[PS2][0:HW, 0:n], rS_ap[0:HW, 0:n], ALU.mult, [b_rS], [psb[PS2], b_t2]))
                    if isB:
                        st.append(lambda: self.tt(t1_ap[0:HW, 0:n], t1_ap[0:HW, 0:n], t2_ap[0:HW, 0:n], ALU.add, [b_t2], [b_t1]))
                        st.append(lambda: self.tt(out_ap, t1_ap[0:HW, 0:n], rstd_ap[0:HW, 0:n], ALU.mult, [b_t1, b_rstd], out_b))
                    else:
                        st.append(lambda: self.tt(out_ap, t1_ap[0:HW, 0:n], t2_ap[0:HW, 0:n], ALU.add, [b_t1, b_t2], out_b))
                else:
                    st.append(lambda: self.tt(out_ap, raw_ap[0:HW, 0:n], rstd_ap[0:HW, 0:n], ALU.mult, [b_raw, b_rstd], out_b))
                return st

            def prep_q(u):
                for f_ in prep_steps(u):
                    f_()

            qw, sq_slot = None, None
            steps = []
            prep_q(0)
            MO = hd if isB else 128
            for h in range(HQ):
                kv = h // G
                if isB:
                    u, R0 = h, 0
                    if h + 1 < HQ:
                        steps = prep_steps(h + 1)
                else:
                    u, R0 = h // 2, (h % 2) * 64
                    if h % 2 == 0 and u + 1 < NQ:
                        steps = prep_steps(u + 1)
                qi = u % 2
                pO, pZ = 2 + (h % 2), 4 + (h % 2)

                def emit_qk(ti):
                    Kf, Vf, masks, bK, bV, c0, n = tl[ti]
                    sb_i = (sit + ti) % 2
                    self.mm(ps[sb_i][:, 0:n], Kf(kv), Q_ap[qi][R0:R0 + hd, c0:c0 + n], True, len(masks) == 0,
                            [bK, b_Q[qi]], [psb[sb_i]])
                    for mk, (mi, off, mcol) in enumerate(masks):
                        self.mm(ps[sb_i][:, off:off + 128], ident_t[:, :], mask_ap[:, mi, mcol:mcol + 128], False,
                                mk == len(masks) - 1, [b_id, b_mask], [psb[sb_i]])
                seen_cols = set()
                emit_qk(0)
                for ti, (Kf, Vf, masks, bK, bV, c0, n) in enumerate(tl):
                    sb_i = (sit + ti) % 2
                    last = ti == len(tl) - 1
                    if not last:
                        emit_qk(ti + 1)
                    if steps and ti >= 1:
                        steps.pop(0)()
                    self.act(PT_ap[sb_i][:, 0:n], ps[sb_i][:, 0:n], AF.Exp, [], [psb[sb_i], b_PT[sb_i]], scale=scale)
                    if t == 4:
                        first = c0 not in seen_cols
                        seen_cols.add(c0)
                    else:
                        first = ti == 0
                    stopf = last or (t == 4 and ti == 1)
                    self.mm(ps[pO][0:MO, c0:c0 + n], Vf(kv), PT_ap[sb_i][:, 0:n], first, stopf,
                            [bV, b_PT[sb_i]], [psb[pO]])
                    if isB and t == 4:
                        self.mm(ps[pZ][0:hd, c0:c0 + n], self.ones_t[:, 0:hd], PT_ap[sb_i][:, 0:n], first, stopf,
                                [self.b_ones, b_PT[sb_i]], [psb[pZ]])
                    elif isB:
                        self.mm(ps[pZ][0:hd, :], self.ones_t[:, 0:hd], PT_ap[sb_i][:, :], first, stopf,
                                [self.b_ones, b_PT[sb_i]], [psb[pZ]])
                sit += len(tl)
                if isB or h % 2 == 1:
                    while steps:
                        steps.pop(0)()
                if isB:
                    self.recip(rz_ap[0:hd, :], ps[pZ][0:hd, :], [], [psb[pZ], b_rz])
                    self.tt(O_ap[:, h, :], ps[pO][0:hd, :], rz_ap[0:hd, :], ALU.mult, [b_rz], [psb[pO], b_O[h % 8]])
                else:
                    self.ts(zz_ap[64:128, :], ps[pO][64:128, :], esink_t[64:128, h:h + 1], None, ALU.add, None,
                            [b_esk], [psb[pO], b_zz])
                    self.recip(rz_ap[64:128, :], zz_ap[64:128, :], [b_zz], [b_rz])
                    p0 = (h % 2) * 64
                    self.tt(O_ap[p0:p0 + 64, h // 2, :], ps[pO][0:64, :], rz_ap[64:128, :], ALU.mult,
                            [b_rz], [psb[pO], b_O[h // 2]])

            for d in range(NCH):
                if d % 4 == 0:
                    so = WSL[(d // 4) % 2]
                    ow = self.wslot_ap[so].rearrange("p (d h c) -> p d h c", d=4, h=8)
                    self.dma(ow, ow_d[d:d + 4].rearrange("d p (h c) -> p d h c", h=8), [], [self.wslot_b[so]], eng="pool")
                pob = PQ if d % 2 == 0 else PS2
                for u in range(8):
                    self.mm(ps[pob][:], ow[:, d % 4, u, :], O_ap[:, u, :], u == 0, u == 7,
                            [self.wslot_b[so], b_O[u]], [psb[pob]])
                self.stt(x_t[:, d, t * TG:(t + 1) * TG], ps[pob][:], self.G_t[:, 1, d, ctx:ctx + 1],
                         x_t[:, d, t * TG:(t + 1) * TG], ALU.mult, ALU.add,
                         [self.b_G], [psb[pob], self.xb[d][t]])


def _prep_shared(inputs, need):
    f = np.float32
    sh = {}

    def want(n):
        return n in need

    w_mod = np.asarray(inputs["w_mod"], f)
    b_mod = np.asarray(inputs["b_mod"], f)
    norm_g = np.asarray(inputs["norm_g"], f)
    for l in range(DEPTH):
        if want(f"wm{l}"):
            sh[f"wm{l}"] = np.ascontiguousarray(
                w_mod[l].reshape(NCH, 128, 72, 128).transpose(2, 1, 0, 3)).reshape(72, 128, NCH * 128)
            sh[f"bm{l}"] = np.ascontiguousarray(b_mod[l].reshape(72, 128).T)
            sh[f"gn{l}"] = np.ascontiguousarray(norm_g[l].reshape(3, NCH, 128).transpose(2, 0, 1)).reshape(128, 3 * NCH)
            wg = np.asarray(inputs["ffn_w_gate"][l], f).reshape(2, NCH, 128, NFF, 128)
            wu = np.asarray(inputs["ffn_w_up"][l], f).reshape(2, NCH, 128, NFF, 128)
            wgu = np.stack([wg, wu], axis=1)
            sh[f"wgu{l}"] = np.ascontiguousarray(wgu.transpose(0, 4, 3, 1, 2, 5)).reshape(2, NFF, 128, 2 * NCH * 128)
            sh[f"wd{l}"] = np.ascontiguousarray(np.asarray(inputs["ffn_w_down"][l], f).reshape(2, NFF, 128, D))
    for idx in range(2):
        if want(f"awuv{idx}"):
            w_in = np.asarray(inputs["a_w_in"][idx], f).reshape(NCH, 128, 2, 8, 256)
            sh[f"awuv{idx}"] = np.ascontiguousarray(w_in.transpose(3, 1, 2, 0, 4)).reshape(8, 128, 2 * NCH * 256)
            w_out = np.asarray(inputs["a_w_out"][idx], f).reshape(8, 2, 128, D)
            sh[f"awout{idx}"] = np.ascontiguousarray(w_out.transpose(0, 2, 1, 3)).reshape(8, 128, 2 * D)
            w_s = np.asarray(inputs["a_w_s"][idx], f)
            sh[f"awsT{idx}"] = np.ascontiguousarray(w_s.transpose(2, 0, 1)).reshape(128, 8 * 128)
            sh[f"abs{idx}"] = np.ascontiguousarray(np.asarray(inputs["a_b_s"][idx], f).reshape(1, 8 * 128))
            sh[f"ang{idx}"] = np.ascontiguousarray(np.broadcast_to(np.asarray(inputs["a_norm_g"][idx], f)[None, :], (128, 2048)))
    sh["fg"] = np.ascontiguousarray(np.asarray(inputs["final_g"], f).reshape(NCH, 128).T)
    _prep_attn_shared(inputs, need, sh)
    return sh


def _rope_tables(hd, rank):
    quarter = hd // 4
    t = np.arange(2048) + rank * 2048
    row = (t // 64).astype(np.float32)
    col = (t % 64).astype(np.float32)
    inv = (10000.0 ** (-np.arange(quarter, dtype=np.float32) / quarter)).astype(np.float32)
    tab = np.zeros((2, hd, 2048), np.float32)
    pm = np.zeros((hd, hd), np.float32)
    for d in range(hd):
        region = d // quarter
        i = d % quarter
        pos = row if region < 2 else col
        ang = (pos * inv[i]).astype(np.float32)
        tab[0, d] = np.cos(ang)
        tab[1, d] = np.sin(ang) * (-1.0 if region % 2 == 0 else 1.0)
        partner = d + quarter if region % 2 == 0 else d - quarter
        pm[partner, d] = 1.0
    return tab, pm


def _c_masks(rank):
    s_ = np.arange(128)[:, None]
    q = np.arange(512)[None, :]
    qi, ql = q // 128, q % 128
    m = np.full((8, 128, 512), NEG, np.float32)
    for r in range(-1, 5):
        vis = ((r == qi - 1) & (s_ >= ql)) | (r == qi) | ((r == qi + 1) & (s_ <= ql))
        m[r + 1] = np.where(vis, 0.0, NEG)
    if rank == 1:
        m[6] = m[0]
    if rank == 0:
        m[7] = m[5]
    return np.ascontiguousarray(m.transpose(1, 0, 2)).reshape(128, 8 * 512)


def _prep_attn_shared(inputs, need, sh):
    f = np.float32
    if "b_qw" in need:
        w = np.asarray(inputs["b_w_qkv"][0], f)
        sh["b_qw"] = np.ascontiguousarray(w[:, :1024].reshape(8, 128, 8, 128).transpose(2, 1, 0, 3)).reshape(8, 128, 1024)
        sh["b_kw"] = np.ascontiguousarray(w[:, 1024:1280].reshape(8, 128, 2, 128).transpose(1, 2, 0, 3)).reshape(128, 2048)
        sh["b_tw"] = np.ascontiguousarray(w[:, 1280:1536].reshape(8, 128, 256).transpose(1, 0, 2)).reshape(128, 2048)
        wo = np.asarray(inputs["b_w_out"][0], f)
        sh["b_ow"] = np.ascontiguousarray(wo.reshape(8, 128, 8, 128).transpose(2, 1, 0, 3)).reshape(8, 128, 1024)
        sh["b_qg"] = np.ascontiguousarray(np.asarray(inputs["b_q_g"][0], f).reshape(128, 1))
        sh["b_kg"] = np.ascontiguousarray(np.asarray(inputs["b_k_g"][0], f).reshape(128, 1))
        sh["b_pm"] = _rope_tables(128, 0)[1]
    if "c_qw" in need:
        w = np.asarray(inputs["c_w_qkv"][0], f)
        sh["c_qw"] = np.ascontiguousarray(w[:, :1024].reshape(8, 128, 8, 128).transpose(2, 1, 0, 3)).reshape(8, 128, 1024)
        wk = w[:, 1024:1152].reshape(8, 128, 2, 1, 64)
        wk = np.broadcast_to(wk, (8, 128, 2, 2, 64))
        sh["c_kw"] = np.ascontiguousarray(wk.transpose(1, 2, 0, 3, 4)).reshape(128, 2048)
        sh["c_tw"] = np.ascontiguousarray(w[:, 1024:1280].reshape(8, 128, 256).transpose(1, 0, 2)).reshape(128, 2048)
        wo = np.asarray(inputs["c_w_out"][0], f)
        sh["c_ow"] = np.ascontiguousarray(wo.reshape(8, 128, 8, 128).transpose(2, 1, 0, 3)).reshape(8, 128, 1024)
        pm64 = _rope_tables(64, 0)[1]
        pm = np.zeros((128, 128), f)
        pm[:64, :64] = pm64
        pm[64:, 64:] = pm64
        sh["c_pm"] = pm
        sh["c_ident"] = np.eye(128, dtype=f)
        sh["c_sink"] = np.ascontiguousarray(np.broadcast_to(np.asarray(inputs["c_sink"][0], f)[None, :], (128, 16)))


def _prep_core(inputs, c, need=()):
    f = np.float32
    b, r = c // 2, c % 2
    xs = np.asarray(inputs["x_sample"], f)[b, r * 2048:(r + 1) * 2048]
    xp = np.asarray(inputs["x_prompt"], f)[2 * c:2 * c + 2].reshape(512, D)
    xx = np.concatenate([xs, xp], 0)
    m = {}
    m["xT"] = np.ascontiguousarray(xx.reshape(NTOK, NCH, 128).transpose(2, 1, 0)).reshape(128, NCH * NTOK)
    cc = np.stack([np.asarray(inputs["c"], f)[b], np.asarray(inputs["c_ctx"], f)], 0)
    m["cT"] = np.ascontiguousarray(cc.reshape(2, NCH, 128).transpose(2, 1, 0)).reshape(128, NCH * 2)
    if "b_ck" in need:
        ck = np.asarray(inputs["cache_b_k"], f)[b, 0]
        m["b_ck"] = np.ascontiguousarray(ck.transpose(2, 1, 0)).reshape(128, 1024)
        cv = np.asarray(inputs["cache_b_v"], f)[b, 0].reshape(4, 128, 256)
        m["b_cv"] = np.ascontiguousarray(cv.transpose(1, 0, 2)).reshape(128, 1024)
        m["b_rope"] = _rope_tables(128, r)[0]
    if "c_ck" in need:
        ck = np.asarray(inputs["cache_c_k"], f)[b, 0]
        ckT = np.ascontiguousarray(ck.transpose(2, 1, 0)).reshape(64, 1024)
        m["c_ck"] = np.ascontiguousarray(np.concatenate([ckT, ckT], 0))
        cv = np.asarray(inputs["cache_c_v"], f)[b, 0].reshape(4, 128, 128)
        m["c_cv"] = np.ascontiguousarray(cv.transpose(1, 0, 2)).reshape(128, 512)
        rt = _rope_tables(64, r)[0]
        m["c_rope"] = np.ascontiguousarray(np.concatenate([rt, rt], 1))
        m["c_mask"] = _c_masks(r)
    return m


def run(inputs, stage=99, trace=False, ncores=8):
    bld = Builder(stage, ncores)
    nc = bld.build()
    need = set(bld.din.keys())
    sh = _prep_shared(inputs, need)
    in_maps = []
    for c in range(ncores):
        m = {k: v for k, v in sh.items() if k in need}
        pc = _prep_core(inputs, c, need)
        m.update({k: v for k, v in pc.items() if k in need})
        assert set(m.keys()) == need, (need - set(m.keys()), set(m.keys()) - need)
        in_maps.append(m)
    res = run_bass_kernel_spmd(nc, in_maps, core_ids=list(range(ncores)), trace=trace)
    return res


def kernel(**inputs):
    res = run(inputs)
    f = np.float32
    y_prompt = np.zeros((16, 256, D), f)
    y_sample = np.zeros((4, 4096, D), f)
    nbk = np.zeros((16, 1, 256, 2, 128), f)
    nbv = np.zeros((16, 1, 256, 2, 128), f)
    nck = np.zeros((16, 1, 256, 2, 64), f)
    ncv = np.zeros((16, 1, 256, 2, 64), f)
    for c in range(8):
        o = res.results[c]
        b, r = c // 2, c % 2
        y = np.asarray(o["yT"], f).reshape(128, NCH, NTOK).transpose(2, 1, 0).reshape(NTOK, D)
        y_sample[b, r * 2048:(r + 1) * 2048] = y[:2048]
        y_prompt[2 * c:2 * c + 2] = y[2048:].reshape(2, 256, D)
        ko = np.asarray(o["b_ko"], f).reshape(128, 2, 2, 256)
        nbk[2 * c:2 * c + 2, 0] = ko.transpose(2, 3, 1, 0)
        vo = np.asarray(o["b_vo"], f).reshape(128, 4, 256).transpose(1, 0, 2).reshape(2, 256, 2, 128)
        nbv[2 * c:2 * c + 2, 0] = vo
        kvo = np.asarray(o["c_kvo"], f).reshape(128, 4, 256).transpose(1, 0, 2).reshape(2, 256, 256)
        nck[2 * c:2 * c + 2, 0] = kvo[:, :, :128].reshape(2, 256, 2, 64)
        ncv[2 * c:2 * c + 2, 0] = kvo[:, :, 128:].reshape(2, 256, 2, 64)
    return (y_prompt, y_sample, nbk, nbv, nck, ncv)
```

```python
import numpy as np
import ml_dtypes
import concourse.bass as bass
import concourse.mybir as mybir
from concourse.bass_utils import run_bass_kernel_spmd
from contextlib import ExitStack

F32 = mybir.dt.float32
BF16 = mybir.dt.bfloat16
AF = mybir.ActivationFunctionType
ALU = mybir.AluOpType

D = 1024
NCH = 8
NTOK = 2560
NTG = 5
TG = 512
DFF = 2816
NFF = 22
EPS = 1e-6
DEPTH = 4
FB = 4
NEG = -30000.0


class Buf:
    __slots__ = ("name", "lo", "hi", "w", "r", "aliases")

    def __init__(self, name, lo=None, hi=None):
        self.name = name
        self.lo = lo
        self.hi = hi
        self.w = None
        self.r = {}
        self.aliases = []


class Op:
    __slots__ = ("eng", "fn", "deps", "is_dma", "needs_sig", "sigsem", "sigval")

    def __init__(self, eng, fn, is_dma):
        self.eng = eng
        self.fn = fn
        self.is_dma = is_dma
        self.deps = []
        self.needs_sig = False
        self.sigsem = None
        self.sigval = 0


ENGS = ("pe", "act", "dve", "pool", "sp")
NPOOLSEM = 8


class Prog:
    def __init__(self):
        self.ops = {e: [] for e in ENGS}
        self.arena = []
        self.acache = {}
        self.dma_cnt = {"sp": 0, "pool": 0, "act": 0}
        self.dma_last = {}
        self.cc_cnt = 0

    def abuf(self, name, lo, hi):
        key = (name, lo, hi)
        if key in self.acache:
            return self.acache[key]
        b = Buf(name, lo, hi)
        self.acache[key] = b
        for o in self.arena:
            if o.lo < hi and lo < o.hi:
                o.aliases.append(b)
                b.aliases.append(o)
        self.arena.append(b)
        return b

    def add(self, eng, fn, reads=(), writes=(), dma=False, cc=False, force=()):
        op = Op(eng, fn, dma)
        for d in force:
            d.needs_sig = True
            op.deps.append(d)
        if cc:
            self.cc_cnt += 1
            op.sigsem = ("cc",)
            op.sigval = self.cc_cnt
            op.needs_sig = True
        deps = set()
        for b in reads:
            for s in [b] + b.aliases:
                if s.w is not None:
                    deps.add(s.w)
        for b in writes:
            for s in [b] + b.aliases:
                if s.w is not None:
                    deps.add(s.w)
                deps.update(s.r.values())
        for b in reads:
            b.r[id(op) if dma else eng] = op
        for b in writes:
            b.w = op
            b.r = {}
        if dma:
            i = self.dma_cnt[eng]
            self.dma_cnt[eng] = i + 1
            slot = i % NPOOLSEM
            op.sigsem = ("dma", eng, slot)
            op.sigval = 16 * (i // NPOOLSEM + 1)
            prev = self.dma_last.get((eng, slot))
            if prev is not None:
                deps.add(prev)
            self.dma_last[(eng, slot)] = op
        for d in deps:
            if d is op:
                continue
            if (not d.is_dma) and (not dma) and d.eng == "pe" and eng == "pe":
                continue
            if d.sigsem == ("cc",):
                op.deps.append(d)
                continue
            d.needs_sig = True
            op.deps.append(d)
        self.ops[eng].append(op)
        return op

    def finalize(self):
        for e in ENGS:
            c = 0
            for op in self.ops[e]:
                if not op.is_dma and op.needs_sig and op.sigsem != ("cc",):
                    c += 1
                    op.sigsem = ("eng", e)
                    op.sigval = c

    def emit(self, eng, handle, sems):
        known = {}
        for op in self.ops[eng]:
            need = {}
            for d in op.deps:
                if need.get(d.sigsem, 0) < d.sigval:
                    need[d.sigsem] = d.sigval
            for k, v in need.items():
                if known.get(k, 0) < v:
                    handle.wait_ge(sems[k], v)
                    known[k] = v
            ins = op.fn(handle)
            if op.is_dma:
                ins.then_inc(sems[op.sigsem], 16)
            elif op.sigsem == ("cc",):
                ins.then_inc(sems[op.sigsem])
            elif op.needs_sig:
                ins.then_inc(sems[op.sigsem], 1)
        if eng in self.dma_cnt:
            n = self.dma_cnt[eng]
            for slot in range(min(n, NPOOLSEM)):
                cnt = (n - slot + NPOOLSEM - 1) // NPOOLSEM
                k = ("dma", eng, slot)
                if known.get(k, 0) < 16 * cnt:
                    handle.wait_ge(sems[k], 16 * cnt)


class Builder:
    def __init__(self, stage=99, ncores=8):
        self.stage = stage
        self.ncores = ncores
        self.nc = bass.Bass("TRN2", target_bir_lowering=False)
        self.P = Prog()
        self.es = ExitStack()
        self.din = {}
        self.dout = {}

    def inp(self, name, shape, dt=F32):
        t = self.nc.dram_tensor(name, list(shape), dt, kind="ExternalInput")
        self.din[name] = t
        return t.ap()

    def outp(self, name, shape, dt=F32):
        t = self.nc.dram_tensor(name, list(shape), dt, kind="ExternalOutput")
        self.dout[name] = t
        return t.ap()

    def sb(self, name, shape, dt):
        return self.es.enter_context(self.nc.sbuf_tensor(name, list(shape), dt))

    def mm(self, out, lhsT, rhs, start, stop, reads, writes, force=()):
        return self.P.add("pe", lambda e, o=out, l=lhsT, r=rhs, s=start, t=stop:
                          e.matmul(o, l, r, start=s, stop=t), reads, writes, force=force)

    def act(self, out, in_, func, reads, writes, scale=None, bias=None):
        kw = {}
        if scale is not None:
            kw["scale"] = scale
        if bias is not None:
            kw["bias"] = bias
        self.P.add("act", lambda e, o=out, i=in_, f=func, kw=kw:
                   e.activation(o, i, f, **kw), reads, writes)

    def tt(self, out, in0, in1, op, reads, writes, eng="dve"):
        self.P.add(eng, lambda e, o=out, a=in0, b=in1, p=op:
                   e.tensor_tensor(o, a, b, p), reads, writes)

    def stt(self, out, in0, scalar, in1, op0, op1, reads, writes):
        self.P.add("dve", lambda e, o=out, a=in0, s=scalar, b=in1, p0=op0, p1=op1:
                   e.scalar_tensor_tensor(o, a, s, b, p0, p1), reads, writes)

    def ts(self, out, in0, s1, s2, op0, op1, reads, writes, eng="dve"):
        if op1 is None:
            self.P.add(eng, lambda e, o=out, a=in0, x=s1, p0=op0:
                       e.tensor_scalar(o, a, x, None, p0), reads, writes)
        else:
            self.P.add(eng, lambda e, o=out, a=in0, x=s1, y=s2, p0=op0, p1=op1:
                       e.tensor_scalar(o, a, x, y, p0, p1), reads, writes)

    def copy(self, out, in_, reads, writes, eng="dve"):
        self.P.add(eng, lambda e, o=out, i=in_: e.tensor_copy(o, i), reads, writes)

    def recip(self, out, in_, reads, writes):
        self.P.add("dve", lambda e, o=out, i=in_: e.reciprocal(o, i), reads, writes)

    def memset(self, ap, val, writes, eng="dve"):
        self.P.add(eng, lambda e, a=ap, v=val: e.memset(a, v), (), writes)

    def dma(self, out, in_, reads, writes, eng="sp"):
        self.P.add(eng, lambda e, o=out, i=in_: e.dma_start(out=o, in_=i),
                   reads, writes, dma=True)

    def build(self):
        nc = self.nc
        P = self.P
        xT_d = self.inp("xT", [128, NCH * NTOK])
        cT_d = self.inp("cT", [128, NCH * 2])
        fg_d = self.inp("fg", [128, NCH])
        yT_d = self.outp("yT", [128, NCH * NTOK])

        x_t = self.sb("x", [128, NCH, NTOK], F32)
        h_t = self.sb("h", [128, NCH, NTOK], BF16)
        UBYTES = 84992
        U_t = self.sb("U", [128, UBYTES // 2], BF16)
        ones_t = self.sb("ones", [128, 128], BF16)
        cT_t = self.sb("cTs", [128, NCH * 2], F32)
        sc_t = self.sb("scs", [128, NCH, 2], BF16)
        self.modT2 = [self.sb(f"modT{i}", [128, 72, 2], F32) for i in range(2)]
        self.bm2 = [self.sb(f"bms{i}", [128, 72], F32) for i in range(2)]
        self.gn2 = [self.sb(f"gns{i}", [128, 3, NCH], F32) for i in range(2)]
        fg_t = self.sb("fgs", [128, NCH], F32)
        self.A2 = [self.sb(f"As{i}", [128, 3, NCH, 2], F32) for i in range(2)]
        self.G2 = [self.sb(f"Gs{i}", [128, 3, NCH, 2], F32) for i in range(2)]
        self.bm_b2 = [Buf(f"bm{i}") for i in range(2)]
        self.gn_b2 = [Buf(f"gn{i}") for i in range(2)]
        self.modT_b2 = [Buf(f"modT{i}") for i in range(2)]
        self.A_b2 = [Buf(f"A{i}") for i in range(2)]
        self.G_b2 = [Buf(f"G{i}") for i in range(2)]
        self.sc_t = sc_t
        eps_t = self.sb("epss", [128, 1], F32)
        self.onef_t = self.sb("onef", [128, 1], F32)
        self.b_onef = Buf("onef")
        self.x_t, self.h_t, self.U_t, self.ones_t = x_t, h_t, U_t, ones_t
        self.eps_t = eps_t

        self.ps = [self.es.enter_context(nc.psum_tensor(f"ps{i}", [128, 512], F32)) for i in range(8)]
        self.psb = [Buf(f"psb{i}") for i in range(8)]

        self.xb = [[Buf(f"x{c}_{t}") for t in range(NTG)] for c in range(NCH)]
        self.hb = [[Buf(f"h{c}_{t}") for t in range(NTG)] for c in range(NCH)]
        b_ones = Buf("ones")
        b_cT, b_sc, b_fg = (Buf(n) for n in ("cT", "sc", "fg"))
        b_eps = Buf("eps")
        self.b_ones, self.b_eps, self.b_sc = b_ones, b_eps, b_sc
        self.mod_items = []

        def U(lo, nbytes, dt, shape_rest):
            es = 2 if dt == BF16 else 4
            n = nbytes // es
            if dt == BF16:
                ap = U_t[:, lo // 2: lo // 2 + n]
            else:
                ap = U_t[:, lo // 2: lo // 2 + 2 * n].bitcast(F32)
            return ap
        self.U = U

        self.memset(ones_t[:], 1.0, [b_ones])
        self.memset(eps_t[:], EPS, [b_eps])
        self.memset(self.onef_t[:], 1.0, [self.b_onef])
        for t in range(NTG):
            self.dma(x_t[:, :, t * TG:(t + 1) * TG],
                     xT_d.rearrange("p (c n) -> p c n", c=NCH)[:, :, t * TG:(t + 1) * TG],
                     [], [self.xb[c][t] for c in range(NCH)])
        self.dma(cT_t[:], cT_d, [], [b_cT])
        self.dma(fg_t[:], fg_d, [], [b_fg])
        self.act(sc_t[:].rearrange("p k c -> p (k c)"), cT_t[:], AF.Silu, [b_cT], [b_sc])

        WS = 8192
        WBASE = 40960
        self.wslot_ap = [U(WBASE + i * WS, WS, BF16, None) for i in range(4)]
        self.wslot_b = [P.abuf(f"wslot{i}", WBASE + i * WS, WBASE + (i + 1) * WS) for i in range(4)]
        self.wctr = 0

        for l in range(DEPTH):
            if l >= self.stage_layers():
                break
            self.inp(f"wgu{l}", [2, NFF, 128, 2 * NCH * 128])
            self.inp(f"wd{l}", [2, NFF, 128, D])
            if l == 0:
                self.mods_schedule(0)
                self.mod_pump(1 + 24 + 1)
            else:
                self.mod_flush()
            self.set_layer(l)
            self.norm_mod(s=0)
            self.ffn(l, 0)
            if self.stage_partial(l, 1):
                break
            self.norm_mod(s=1)
            kind = l % 3
            if kind == 0:
                self.mixer_a(l, l // 3)
            else:
                self.attn_layer(l, "B" if kind == 1 else "C")
            if self.stage_partial(l, 2):
                break
            self.norm_mod(s=2)
            if l + 1 < self.stage_layers():
                self.mods_schedule(l + 1)
            self.ffn(l, 1)

        if self.stage >= 99:
            self.final_norm(fg_t, b_fg)
        for t in range(NTG):
            self.dma(yT_d.rearrange("p (c n) -> p c n", c=NCH)[:, :, t * TG:(t + 1) * TG],
                     x_t[:, :, t * TG:(t + 1) * TG],
                     [self.xb[c][t] for c in range(NCH)], [])

        P.finalize()
        sems = {}
        for e in ENGS:
            sems[("eng", e)] = self.es.enter_context(nc.semaphore(f"s_{e}"))
        for e in ("sp", "pool"):
            for i in range(NPOOLSEM):
                sems[("dma", e, i)] = self.es.enter_context(nc.semaphore(f"d_{e}{i}"))
        sems[("cc",)] = self.es.enter_context(nc.semaphore("s_cc"))
        with nc.Block() as block:
            @block.tensor
            def _(t):
                P.emit("pe", t, sems)

            @block.scalar
            def _(a):
                P.emit("act", a, sems)

            @block.vector
            def _(v):
                P.emit("dve", v, sems)

            @block.gpsimd
            def _(g):
                P.emit("pool", g, sems)

            @block.sync
            def _(s):
                P.emit("sp", s, sems)
        self.es.close()
        return nc

    def set_layer(self, l):
        i = l % 2
        self.modT_t, self.A_t, self.G_t = self.modT2[i], self.A2[i], self.G2[i]
        self.b_modT, self.b_A, self.b_G = self.modT_b2[i], self.A_b2[i], self.G_b2[i]

    def mods_schedule(self, l):
        P, U = self.P, self.U
        i = l % 2
        wm_d = self.inp(f"wm{l}", [72, 128, NCH * 128])
        bm_d = self.inp(f"bm{l}", [128, 72])
        gn_d = self.inp(f"gn{l}", [128, 3 * NCH])
        modT_t, bm_t, gn_t, A_t, G_t = self.modT2[i], self.bm2[i], self.gn2[i], self.A2[i], self.G2[i]
        b_modT, b_bm, b_gn, b_A, b_G = self.modT_b2[i], self.bm_b2[i], self.gn_b2[i], self.A_b2[i], self.G_b2[i]
        pm = self.ps[7]
        mslot_ap = [U(77824 + j * 2048, 2048, BF16, None).rearrange("p (k c) -> p k c", k=NCH) for j in range(2)]
        mslot_b = [P.abuf(f"mslot{j}", 77824 + j * 2048, 77824 + (j + 1) * 2048) for j in range(2)]
        items = []

        def first():
            self.dma(bm_t[:], bm_d, [], [b_bm])
            self.dma(gn_t[:].rearrange("p s c -> p (s c)"), gn_d, [], [b_gn])
            load(0)
            load(1)

        def load(fc):
            j = fc % 2
            self.dma(mslot_ap[j], wm_d[fc].rearrange("p (k c) -> p k c", k=NCH), [], [mslot_b[j]], eng="pool")

        def work(fc):
            j = fc % 2
            for k in range(NCH):
                self.mm(pm[:, fc * 2:fc * 2 + 2], mslot_ap[j][:, k, :], self.sc_t[:, k, :],
                        k == 0, k == NCH - 1, [mslot_b[j], self.b_sc], [self.psb[7]])
            if fc + 2 < 72:
                load(fc + 2)

        def final(s_):
            c0, c1 = 24 * s_, 24 * (s_ + 1)
            for ctx in range(2):
                self.tt(modT_t[:, c0:c1, ctx], pm[:, 2 * c0:2 * c1].rearrange("p (f c) -> p f c", c=2)[:, :, ctx],
                        bm_t[:, c0:c1], ALU.add, [b_bm], [self.psb[7], b_modT])
            for ctx in range(2):
                self.stt(A_t[:, s_, :, ctx], modT_t[:, (3 * s_ + 1) * 8:(3 * s_ + 2) * 8, ctx], 1.0,
                         gn_t[:, s_, :], ALU.add, ALU.mult, [b_modT, b_gn], [b_A])
                self.ts(G_t[:, s_, :, ctx], modT_t[:, (3 * s_ + 2) * 8:(3 * s_ + 3) * 8, ctx],
                        0.5 if s_ != 1 else 1.0, None, ALU.mult, None, [b_modT], [b_G])
        items.append(first)
        for fc in range(72):
            items.append(lambda fc=fc: work(fc))
            if fc % 24 == 23:
                items.append(lambda s_=fc // 24: final(s_))
        self.mod_items = items

    def mod_pump(self, n=1):
        for _ in range(n):
            if self.mod_items:
                self.mod_items.pop(0)()

    def mod_flush(self):
        while self.mod_items:
            self.mod_items.pop(0)()

    def stage_layers(self):
        return {1: 1, 2: 1, 3: 2, 4: 2, 5: 3, 6: 3}.get(self.stage, DEPTH)

    def stage_partial(self, l, point):
        if self.stage == 1:
            return l == 0 and point == 1
        if self.stage == 2:
            return l == 0 and point == 2
        if self.stage == 4:
            return l == 1 and point == 2
        if self.stage == 6:
            return l == 2 and point == 2
        return False

    def final_norm(self, fg_t, b_fg):
        P, U = self.P, self.U
        x_t = self.x_t
        sq_ap = [U(i * 8192, 8192, BF16, None).rearrange("p (c n) -> p c n", c=NCH) for i in range(2)]
        sq_b = [P.abuf(f"sq{i}", i * 8192, (i + 1) * 8192) for i in range(2)]
        sd_ap = U(16384, 10240, F32, None)
        sd_b = P.abuf("sd", 16384, 26624)
        rs_ap = U(26624, 10240, F32, None)
        rs_b = P.abuf("rstd", 26624, 36864)
        for t in range(NTG):
            i = t % 2
            pb = 4 + (t % 4)
            self.act(sq_ap[i], x_t[:, :, t * TG:(t + 1) * TG], AF.Square,
                     [self.xb[c][t] for c in range(NCH)], [sq_b[i]])
            for c in range(NCH):
                self.mm(self.ps[pb][:], self.ones_t[:], sq_ap[i][:, c, :], c == 0, c == NCH - 1,
                        [self.b_ones, sq_b[i]], [self.psb[pb]])
            self.act(sd_ap[:, t * TG:(t + 1) * TG], self.ps[pb][:], AF.Sqrt,
                     [self.b_eps], [self.psb[pb], sd_b], scale=1.0 / D, bias=self.eps_t[:, 0:1])
        self.recip(rs_ap, sd_ap, [sd_b], [rs_b])
        for t in range(NTG):
            for c in range(NCH):
                self.stt(x_t[:, c, t * TG:(t + 1) * TG], x_t[:, c, t * TG:(t + 1) * TG], fg_t[:, c:c + 1],
                         rs_ap[:, t * TG:(t + 1) * TG], ALU.mult, ALU.mult,
                         [b_fg, rs_b], [self.xb[c][t]])

    def ctx_of(self, tg):
        return 0 if tg < 4 else 1

    def norm_mod(self, s):
        P, U = self.P, self.U
        x_t, h_t = self.x_t, self.h_t
        sq_ap = [U(i * 8192, 8192, BF16, None).rearrange("p (c n) -> p c n", c=NCH) for i in range(2)]
        sq_b = [P.abuf(f"sq{i}", i * 8192, (i + 1) * 8192) for i in range(2)]
        sd_ap = [U(16384 + t * 2048, 2048, F32, None) for t in range(NTG)]
        sd_b = [P.abuf(f"sd{t}", 16384 + t * 2048, 16384 + (t + 1) * 2048) for t in range(NTG)]
        rs_ap = [U(26624 + t * 2048, 2048, F32, None) for t in range(NTG)]
        rs_b = [P.abuf(f"rstd{t}", 26624 + t * 2048, 26624 + (t + 1) * 2048) for t in range(NTG)]
        tmp_ap = [U(36864 + i * 2048, 2048, F32, None) for i in range(2)]
        tmp_b = [P.abuf(f"tmp{i}", 36864 + i * 2048, 36864 + (i + 1) * 2048) for i in range(2)]
        kk = [0]

        def sq_(t):
            self.act(sq_ap[t % 2], x_t[:, :, t * TG:(t + 1) * TG], AF.Square,
                     [self.xb[c][t] for c in range(NCH)], [sq_b[t % 2]])

        def mm_(t):
            pb = 4 + (t % 4)
            for c in range(NCH):
                self.mm(self.ps[pb][:], self.ones_t[:], sq_ap[t % 2][:, c, :], c == 0, c == NCH - 1,
                        [self.b_ones, sq_b[t % 2]], [self.psb[pb]])

        def rs_(t):
            pb = 4 + (t % 4)
            self.act(sd_ap[t], self.ps[pb][:], AF.Sqrt,
                     [self.b_eps], [self.psb[pb], sd_b[t]], scale=1.0 / D, bias=self.eps_t[:, 0:1])
            self.recip(rs_ap[t], sd_ap[t], [sd_b[t]], [rs_b[t]])

        def stage_b(t):
            ctx = self.ctx_of(t)
            for c in range(NCH):
                i = kk[0] % 2
                kk[0] += 1
                self.stt(tmp_ap[i], x_t[:, c, t * TG:(t + 1) * TG], self.A_t[:, s, c, ctx:ctx + 1],
                         rs_ap[t], ALU.mult, ALU.mult,
                         [self.xb[c][t], self.b_A, rs_b[t]], [tmp_b[i]])
                self.act(h_t[:, c, t * TG:(t + 1) * TG], tmp_ap[i], AF.Identity,
                         [tmp_b[i], self.b_modT], [self.hb[c][t]],
                         bias=self.modT_t[:, (3 * s) * 8 + c, ctx:ctx + 1])

        sq_(0)
        mm_(0)
        for t in range(NTG):
            if t + 1 < NTG:
                sq_(t + 1)
            rs_(t)
            if t + 1 < NTG:
                mm_(t + 1)
        for t in range(NTG):
            stage_b(t)

    def ffn(self, l, si):
        P, U = self.P, self.U
        s = 0 if si == 0 else 2
        wgu_d, wd_d = self.din[f"wgu{l}"].ap(), self.din[f"wd{l}"].ap()
        a_ap = U(0, FB * NTOK * 2, BF16, None).rearrange("p (j n) -> p j n", j=FB)
        a_b = [[P.abuf(f"a{j}_{t}", (j * NTOK + t * TG) * 2, (j * NTOK + (t + 1) * TG) * 2)
                for t in range(NTG)] for j in range(FB)]
        sg_ap = [U(73728 + i * 2048, 2048, F32, None) for i in range(2)]
        sg_b = [P.abuf(f"sg{i}", 73728 + i * 2048, 73728 + (i + 1) * 2048) for i in range(2)]
        fs_ap = list(self.wslot_ap) + [U(20480 + i * 8192, 8192, BF16, None) for i in range(2)]
        fs_b = list(self.wslot_b) + [P.abuf(f"fslot{i}", 20480 + i * 8192, 20480 + (i + 1) * 8192) for i in range(2)]
        bsz = [NFF % FB] + [FB] * (NFF // FB) if NFF % FB else [FB] * (NFF // FB)
        bstart = [sum(bsz[:i]) for i in range(len(bsz))]
        nblk = len(bsz)
        mods_pending = len(self.mod_items) > 0
        ndb = 3 if mods_pending else 4
        it = 0
        fctr = [0]

        def load_block(b):
            j0 = bstart[b]
            nj = bsz[b]
            slots = {}
            for jj in range(0, nj, 2):
                sidx = fctr[0] % 6
                fctr[0] += 1
                n2 = min(2, nj - jj)
                wap = fs_ap[sidx].rearrange("p (j g k c) -> p j g k c", j=2, g=2, k=NCH)
                self.dma(wap[:, 0:n2], wgu_d[si, j0 + jj:j0 + jj + n2].rearrange("j p (g k c) -> p j g k c", g=2, k=NCH),
                         [], [fs_b[sidx]], eng="pool")
                for q in range(n2):
                    slots[jj + q] = (sidx, wap, q)
            sidx_d = fctr[0] % 6
            fctr[0] += 1
            wdap = fs_ap[sidx_d].rearrange("p (j d) -> p j d", j=FB)
            self.dma(wdap[:, 0:nj], wd_d[si, j0:j0 + nj].rearrange("j p d -> p j d"),
                     [], [fs_b[sidx_d]], eng="pool")
            return (j0, nj, slots, sidx_d, wdap)

        nxt = load_block(0)
        for b in range(nblk):
            j0, nj, slots, sidx_d, wdap = nxt
            if b + 1 < nblk:
                nxt = load_block(b + 1)
            for jj in range(nj):
                sidx, wap, q = slots[jj]
                for t in range(NTG):
                    pg = (it % 2) * 2
                    pu = pg + 1
                    i2 = it % 2
                    it += 1
                    for k in range(NCH):
                        self.mm(self.ps[pg][:], wap[:, q, 0, k, :], self.h_t[:, k, t * TG:(t + 1) * TG],
                                k == 0, k == NCH - 1, [fs_b[sidx], self.hb[k][t]], [self.psb[pg]])
                    for k in range(NCH):
                        self.mm(self.ps[pu][:], wap[:, q, 1, k, :], self.h_t[:, k, t * TG:(t + 1) * TG],
                                k == 0, k == NCH - 1, [fs_b[sidx], self.hb[k][t]], [self.psb[pu]])
                    self.act(sg_ap[i2], self.ps[pg][:], AF.Silu, [], [self.psb[pg], sg_b[i2]])
                    self.tt(a_ap[:, jj, t * TG:(t + 1) * TG], self.ps[pu][:], sg_ap[i2], ALU.mult,
                            [sg_b[i2]], [self.psb[pu], a_b[jj][t]])
                self.mod_pump()
            for d in range(NCH):
                for t in range(NTG):
                    ctx = self.ctx_of(t)
                    pd = 4 + (it % ndb)
                    it += 1
                    for jj in range(nj):
                        self.mm(self.ps[pd][:], wdap[:, jj, d * 128:(d + 1) * 128], a_ap[:, jj, t * TG:(t + 1) * TG],
                                jj == 0, jj == nj - 1, [fs_b[sidx_d], a_b[jj][t]], [self.psb[pd]])
                    self.stt(self.x_t[:, d, t * TG:(t + 1) * TG], self.ps[pd][:], self.G_t[:, s, d, ctx:ctx + 1],
                             self.x_t[:, d, t * TG:(t + 1) * TG], ALU.mult, ALU.add,
                             [self.b_G], [self.psb[pd], self.xb[d][t]])
                self.mod_pump()
        self.mod_flush()

    def next_slot(self):
        si = self.wctr % 4
        self.wctr += 1
        return si

    def mixer_a(self, l, idx):
        P, U = self.P, self.U
        x_t, h_t = self.x_t, self.h_t
        awuv_d = self.inp(f"awuv{idx}", [8, 128, 2 * NCH * 256])
        awout_d = self.inp(f"awout{idx}", [8, 128, 2 * D])
        awsT_d = self.inp(f"awsT{idx}", [128, 8 * 128])
        abs_d = self.inp(f"abs{idx}", [1, 8 * 128])
        ang_d = self.inp(f"ang{idx}", [128, 2048])
        sqa_ap = [U(i * 1024, 1024, BF16, None) for i in range(2)]
        sqa_b = [P.abuf(f"sqa{i}", i * 1024, (i + 1) * 1024) for i in range(2)]
        sd_ap = U(2048, 10240, F32, None)
        sd_b = P.abuf("a_sd", 2048, 12288)
        rs_ap = U(12288, 10240, F32, None)
        rs_b = P.abuf("a_rs", 12288, 22528)
        rT_ap = U(22528, 128, F32, None)
        rT_b = P.abuf("a_rT", 22528, 22656)
        ga_ap = U(22656, 8192, F32, None)
        ga_b = P.abuf("a_ga", 22656, 30848)
        ws_ap = U(30848, 2048, BF16, None).rearrange("p (g q) -> p g q", g=8)
        ws_b = P.abuf("a_ws", 30848, 32896)
        bs_ap = U(32896, 2048, BF16, None)
        bs_b = P.abuf("a_bs", 32896, 34944)
        vn_ap = U(2048, 10240, BF16, None).rearrange("p (t c) -> p t c", t=20)
        vn_b = [P.abuf(f"a_vn{t}", 2048 + t * 512, 2048 + (t + 1) * 512) for t in range(20)]
        u_ap = U(12288, 10240, BF16, None).rearrange("p (f n) -> p f n", f=2)
        u_b = [[P.abuf(f"a_u{f}_{t}", 12288 + (f * NTOK + t * TG) * 2, 12288 + (f * NTOK + (t + 1) * TG) * 2)
                for t in range(NTG)] for f in range(2)]
        self.dma(ga_ap, ang_d, [], [ga_b])
        self.dma(ws_ap.rearrange("p g q -> p (g q)"), awsT_d, [], [ws_b], eng="pool")
        self.dma(bs_ap[0:1, :], abs_d, [], [bs_b], eng="pool")
        it = 0
        pend = None
        for g in range(8):
            si = self.next_slot()
            wv = self.wslot_ap[si][:, 0:NCH * 256].rearrange("p (k c) -> p k c", k=NCH)
            self.dma(wv, awuv_d[g].rearrange("p (u k c) -> p u k c", u=2, k=NCH)[:, 1], [], [self.wslot_b[si]], eng="pool")
            for fc in range(2):
                for t in range(NTG):
                    pb = it % 3
                    i2 = it % 2
                    it += 1
                    for k in range(NCH):
                        self.mm(self.ps[pb][:], wv[:, k, fc * 128:(fc + 1) * 128], h_t[:, k, t * TG:(t + 1) * TG],
                                k == 0, k == NCH - 1, [self.wslot_b[si], self.hb[k][t]], [self.psb[pb]])
                    if pend is not None:
                        pend()
                    self.act(sqa_ap[i2], self.ps[pb][:], AF.Square, [], [self.psb[pb], sqa_b[i2]])
                    pend = (lambda t=t, i2=i2, first=(g == 0 and fc == 0), lastf=(g == 7 and fc == 1):
                            self.mm(self.ps[3 + t][:], self.ones_t[:], sqa_ap[i2], first, lastf,
                                    [self.b_ones, sqa_b[i2]], [self.psb[3 + t]]))
        pend()
        for t in range(NTG):
            self.act(sd_ap[:, t * TG:(t + 1) * TG], self.ps[3 + t][:], AF.Sqrt,
                     [self.b_eps], [self.psb[3 + t], sd_b], scale=1.0 / 2048, bias=self.eps_t[:, 0:1])
        for t in range(NTG):
            self.recip(rs_ap[:, t * TG:(t + 1) * TG], sd_ap[:, t * TG:(t + 1) * TG], [sd_b], [rs_b])
        for tile in range(20):
            self.mm(self.ps[0][:, tile:tile + 1], rs_ap[0:1, tile * 128:(tile + 1) * 128], self.onef_t[0:1, 0:1],
                    True, True, [rs_b, self.b_onef], [self.psb[0]])
        self.copy(rT_ap[:, 0:20], self.ps[0][:, 0:20], [], [self.psb[0], rT_b])
        def load_g(g):
            sa = self.next_slot()
            wuv = self.wslot_ap[sa].rearrange("p (u k c) -> p u k c", u=2, k=NCH)
            self.dma(wuv, awuv_d[g].rearrange("p (u k c) -> p u k c", u=2, k=NCH), [], [self.wslot_b[sa]], eng="pool")
            sb_ = self.next_slot()
            wo = self.wslot_ap[sb_][:, 0:2 * D].rearrange("p (f d) -> p f d", f=2)
            self.dma(wo, awout_d[g].rearrange("p (f d) -> p f d", f=2), [], [self.wslot_b[sb_]], eng="pool")
            return (g, sa, wuv, sb_, wo)

        vctr = [0]

        def v_tile(G_, tile):
            g, sa, wuv, sb_, wo = G_
            pb = 2 + vctr[0] % 2
            vctr[0] += 1
            t = tile // 4
            for k in range(NCH):
                self.mm(self.ps[pb][:, 0:256], h_t[:, k, tile * 128:(tile + 1) * 128], wuv[:, 1, k, :],
                        k == 0, k == NCH - 1, [self.wslot_b[sa], self.hb[k][t]], [self.psb[pb]])
            self.stt(vn_ap[:, tile, :], self.ps[pb][:, 0:256], rT_ap[:, tile:tile + 1],
                     ga_ap[:, g * 256:(g + 1) * 256], ALU.mult, ALU.mult,
                     [rT_b, ga_b], [self.psb[pb], vn_b[tile]])

        cur = load_g(0)
        for tile in range(20):
            v_tile(cur, tile)
        for g in range(8):
            _, sa, wuv, sb_, wo = cur
            for fc in range(2):
                for t in range(NTG):
                    pb = it % 2
                    it += 1
                    for k in range(NCH):
                        self.mm(self.ps[pb][:], wuv[:, 0, k, fc * 128:(fc + 1) * 128], h_t[:, k, t * TG:(t + 1) * TG],
                                k == 0, k == NCH - 1, [self.wslot_b[sa], self.hb[k][t]], [self.psb[pb]])
                    self.act(u_ap[:, fc, t * TG:(t + 1) * TG], self.ps[pb][:], AF.Copy, [], [self.psb[pb], u_b[fc][t]])
            for fc in range(2):
                for t in range(NTG):
                    pb = 4 + it % 2
                    it += 1
                    for n in range(4):
                        tile = t * 4 + n
                        self.mm(self.ps[pb][:, n * 128:(n + 1) * 128], vn_ap[:, tile, fc * 128:(fc + 1) * 128],
                                ws_ap[:, g, :], True, False, [vn_b[tile], ws_b], [self.psb[pb]])
                        self.mm(self.ps[pb][:, n * 128:(n + 1) * 128], self.ones_t[0:1, :],
                                bs_ap[0:1, g * 128:(g + 1) * 128], False, True, [self.b_ones, bs_b], [self.psb[pb]])
                    self.tt(u_ap[:, fc, t * TG:(t + 1) * TG], self.ps[pb][:], u_ap[:, fc, t * TG:(t + 1) * TG],
                            ALU.mult, [], [self.psb[pb], u_b[fc][t]])
            nxt = load_g(g + 1) if g + 1 < 8 else None
            oi = 0
            for d in range(NCH):
                for t in range(NTG):
                    ctx = self.ctx_of(t)
                    pb = 6 + it % 2
                    it += 1
                    for fc in range(2):
                        self.mm(self.ps[pb][:], wo[:, fc, d * 128:(d + 1) * 128], u_ap[:, fc, t * TG:(t + 1) * TG],
                                fc == 0, fc == 1, [self.wslot_b[sb_], u_b[fc][t]], [self.psb[pb]])
                    self.stt(x_t[:, d, t * TG:(t + 1) * TG], self.ps[pb][:], self.G_t[:, 1, d, ctx:ctx + 1],
                             x_t[:, d, t * TG:(t + 1) * TG], ALU.mult, ALU.add,
                             [self.b_G], [self.psb[pb], self.xb[d][t]])
                    if nxt is not None and oi % 2 == 1:
                        v_tile(nxt, oi // 2)
                    oi += 1
            cur = nxt

    def attn_layer(self, l, kind):
        P, U = self.P, self.U
        nc = self.nc
        x_t, h_t = self.x_t, self.h_t
        isB = kind == "B"
        hd = 128 if isB else 64
        HQ = 8 if isB else 16
        KV = 2
        G = HQ // KV
        pre = "b_" if isB else "c_"
        scale = float(hd) ** -0.5
        voff = 0 if isB else 128
        VW = KV * hd
        HW = 128
        NQ = HQ if isB else HQ // 2
        qw_d = self.inp(pre + "qw", [NQ, 128, NCH * HW])
        kw_d = self.inp(pre + "kw", [128, KV * NCH * HW])
        tw_d = self.inp(pre + "tw", [128, NCH * 256])
        ow_d = self.inp(pre + "ow", [NCH, 128, 8 * 128])
        ck_d = self.inp(pre + "ck", [HW, KV * 512])
        cv_d = self.inp(pre + "cv", [128, 4 * VW])
        rope_d = self.inp(pre + "rope", [2, HW, 2048])
        pm_d = self.inp(pre + "pm", [HW, HW])
        if isB:
            qg_d = self.inp("b_qg", [128, 1])
            kg_d = self.inp("b_kg", [128, 1])
            ko_d = self.outp("b_ko", [128, KV * 512])
            vo_d = self.outp("b_vo", [128, 4 * 256])
            XW = 8192
        else:
            ident_d = self.inp("c_ident", [128, 128])
            mask_d = self.inp("c_mask", [128, 8 * 512])
            sink_d = self.inp("c_sink", [128, 16])
            vo_d = self.outp("c_kvo", [128, 4 * 256])
            XW = 768
        kxin = nc.dram_tensor(pre + "kxin", [128, XW], BF16)
        kxout = nc.dram_tensor(pre + "kxout", [256, XW], BF16)
        b_kxin, b_kxout = Buf(pre + "kxin"), Buf(pre + "kxout")
        pm_t = self.sb(pre + "pm_s", [128, 128], BF16)
        b_pm = Buf(pre + "pm")
        self.dma(pm_t[:, :], pm_d, [], [b_pm], eng="pool")
        if isB:
            onesf_t = self.sb("onesf_s", [128, 128], F32)
            b_onesf = Buf("onesf")
            self.memset(onesf_t[:], 1.0, [b_onesf])
            qg_t = self.sb("qg_s", [128, 1], F32)
            kg_t = self.sb("kg_s", [128, 1], F32)
            b_g = Buf("qkg")
            self.dma(qg_t[:], qg_d, [], [b_g])
            self.dma(kg_t[:], kg_d, [], [b_g])
        else:
            ident_t = self.sb("ident_s", [128, 128], BF16)
            sink_t = self.sb("sink_s", [128, 16], F32)
            esink_t = self.sb("esink_s", [128, 16], F32)
            b_id, b_sk, b_esk = Buf("ident"), Buf("sink"), Buf("esink")
            self.dma(ident_t[:], ident_d, [], [b_id], eng="pool")
            self.dma(sink_t[:], sink_d, [], [b_sk])
            self.act(esink_t[:], sink_t[:], AF.Exp, [b_sk], [b_esk])
        if isB:
            Kall_ap = U(0, 18432, BF16, None).rearrange("p (kv n) -> p kv n", kv=KV)
            b_K = P.abuf("B_Kall", 0, 18432)
            Vall_ap = U(18432, 18432, BF16, None).rearrange("p (t c) -> p t c", t=36)
            b_V = P.abuf("B_Vall", 18432, 36864)
            ktmp_ap = U(0, 8192, BF16, None).rearrange("p (kv n) -> p kv n", kv=KV)
            b_ktmp = P.abuf("B_ktmp", 0, 8192)
            vtmp_ap = U(18432, 8192, BF16, None).rearrange("p (t c) -> p t c", t=16)
            b_vtmp = P.abuf("B_vtmp", 18432, 26624)
            KP0, VP0 = 36864, 38912
            O0 = 76800
            RC0, RS0 = 68608, 69632
        else:
            Kown_ap = U(0, 9216, BF16, None).rearrange("p (kv n) -> p kv n", kv=KV)
            b_K = P.abuf("C_Kown", 0, 9216)
            Kctx_ap = U(9216, 2048, BF16, None).rearrange("p (kv n) -> p kv n", kv=KV)
            b_Kc = P.abuf("C_Kctx", 9216, 11264)
            Vowna_ap = U(11264, 9216, BF16, None).rearrange("p (t kv c) -> p t kv c", t=18, kv=KV)
            b_V = P.abuf("C_Vown", 11264, 20480)
            Vctxa_ap = U(20480, 2048, BF16, None).rearrange("p (t kv c) -> p t kv c", t=4, kv=KV)
            b_Vc = P.abuf("C_Vctx", 20480, 22528)
            mask_ap = U(22528, 8192, BF16, None).rearrange("p (m n) -> p m n", m=8)
            b_mask = P.abuf("C_mask", 22528, 30720)
            KP0, VP0 = 78848, 80896
            O0 = 30720
            RC0, RS0 = 76800, 77824
        KP_ap = U(KP0, 2048, BF16, None).rearrange("p (kv n) -> p kv n", kv=KV)
        b_KP = P.abuf(pre + "KP", KP0, KP0 + 2048)
        if isB:
            VP_ap = U(VP0, 4 * VW * 2, BF16, None).rearrange("p (t c) -> p t c", t=4)
            b_VP = P.abuf(pre + "VP", VP0, VP0 + 4 * VW * 2)
        else:
            VPa_ap = U(VP0, 2048, BF16, None).rearrange("p (t kv c) -> p t kv c", t=4, kv=KV)
            b_VP = P.abuf(pre + "VP", VP0, VP0 + 2048)
            self.memset(Vowna_ap[:, :, :, 64:128], 1.0, [b_V])
            self.memset(Vctxa_ap[:, :, :, 64:128], 1.0, [b_Vc])
            self.memset(VPa_ap[:, :, :, 64:128], 1.0, [b_VP])
        O_ap = U(O0, 8192, BF16, None).rearrange("p (h n) -> p h n", h=8)
        b_O = [P.abuf(pre + f"O{i}", O0 + i * 1024, O0 + (i + 1) * 1024) for i in range(8)]

        def f32buf(name, lo, nb=2048):
            return U(lo, nb, F32, None), P.abuf(pre + name, lo, lo + nb)
        rstd_ap, b_rstd = f32buf("rstd", 57344)
        t1_ap, b_t1 = f32buf("t1", 59392)
        t2_ap, b_t2 = f32buf("t2", 61440)
        rz_ap, b_rz = f32buf("rz", 63488)
        kf_ap, b_kf = f32buf("kf", 65536)
        vf_ap, b_vf = f32buf("vf", 67584, 1024)
        if not isB:
            zz_ap, b_zz = f32buf("zz", 68608)

        def bfbuf(name, lo, nb=1024):
            return U(lo, nb, BF16, None), P.abuf(pre + name, lo, lo + nb)
        Q_ap, b_Q = [None, None], [None, None]
        Q_ap[0], b_Q[0] = bfbuf("Q0", 70656)
        Q_ap[1], b_Q[1] = bfbuf("Q1", 71680)
        raw_ap, b_raw = bfbuf("raw", 72704)
        sq_ap, b_sq = bfbuf("sqh", 73728)
        PT_ap, b_PT = [None, None], [None, None]
        PT_ap[0], b_PT[0] = bfbuf("PT0", 74752)
        PT_ap[1], b_PT[1] = bfbuf("PT1", 75776)
        rC_ap, b_rC = bfbuf("rC", RC0)
        rS_ap, b_rS = bfbuf("rS", RS0)
        ps = self.ps
        psb = self.psb
        WSL = [0, 1]
        PQ, PS2 = 6, 7

        def load_rope(t):
            self.dma(rC_ap[:, :], rope_d[0, :, t * TG:(t + 1) * TG], [], [b_rC], eng="pool")
            self.dma(rS_ap[:, :], rope_d[1, :, t * TG:(t + 1) * TG], [], [b_rS], eng="pool")

        def qk_post_a(src, n, g_t, rope, out_ap, out_b):
            if isB:
                self.act(raw_ap[0:HW, 0:n], src, AF.Identity, [b_g], [psb[PQ], b_raw], scale=g_t[0:HW, 0:1])
                self.act(sq_ap[0:HW, 0:n], src, AF.Square, [], [psb[PQ], b_sq])
            elif not rope:
                self.act(out_ap, src, AF.Copy, [], [psb[PQ]] + out_b)
            else:
                self.act(raw_ap[0:HW, 0:n], src, AF.Copy, [], [psb[PQ], b_raw])

        def qk_post_b(n, rope, out_ap, out_b, f32_out=None):
            if isB:
                self.mm(ps[PS2][0:HW, 0:n], self.ones_t[0:HW, 0:HW], sq_ap[0:HW, 0:n], True, True,
                        [self.b_ones, b_sq], [psb[PS2]])
                self.act(rstd_ap[0:HW, 0:n], ps[PS2][0:HW, 0:n], AF.Ln, [self.b_eps], [psb[PS2], b_rstd],
                         scale=1.0 / hd, bias=self.eps_t[0:HW, 0:1])
                self.act(rstd_ap[0:HW, 0:n], rstd_ap[0:HW, 0:n], AF.Exp, [], [b_rstd], scale=-0.5)
            elif not rope:
                return
            if rope:
                self.mm(ps[PS2][0:HW, 0:n], pm_t[0:HW, 0:HW], raw_ap[0:HW, 0:n], True, True, [b_pm, b_raw], [psb[PS2]])
                self.tt(t1_ap[0:HW, 0:n], raw_ap[0:HW, 0:n], rC_ap[0:HW, 0:n], ALU.mult, [b_raw, b_rC], [b_t1])
                self.tt(t2_ap[0:HW, 0:n], ps[PS2][0:HW, 0:n], rS_ap[0:HW, 0:n], ALU.mult, [b_rS], [psb[PS2], b_t2])
                if isB:
                    self.tt(t1_ap[0:HW, 0:n], t1_ap[0:HW, 0:n], t2_ap[0:HW, 0:n], ALU.add, [b_t2], [b_t1])
                    self.tt(out_ap, t1_ap[0:HW, 0:n], rstd_ap[0:HW, 0:n], ALU.mult, [b_t1, b_rstd], out_b)
                else:
                    self.tt(out_ap, t1_ap[0:HW, 0:n], t2_ap[0:HW, 0:n], ALU.add, [b_t1, b_t2], out_b)
            else:
                if f32_out is not None:
                    self.tt(f32_out[0], raw_ap[0:HW, 0:n], rstd_ap[0:HW, 0:n], ALU.mult, [b_raw, b_rstd], [f32_out[1]])
                    self.act(out_ap, f32_out[0], AF.Copy, [f32_out[1]], out_b)
                else:
                    self.tt(out_ap, raw_ap[0:HW, 0:n], rstd_ap[0:HW, 0:n], ALU.mult, [b_raw, b_rstd], out_b)

        def qk_post(src, n, g_t, rope, out_ap, out_b, f32_out=None):
            qk_post_a(src, n, g_t, rope, out_ap, out_b)
            qk_post_b(n, rope, out_ap, out_b, f32_out)

        sk = WSL[0]
        kw = self.wslot_ap[sk][:, 0:KV * NCH * HW].rearrange("p (kv k c) -> p kv k c", kv=KV, k=NCH)
        self.dma(kw, kw_d.rearrange("p (kv k c) -> p kv k c", kv=KV, k=NCH), [], [self.wslot_b[sk]], eng="pool")
        st = WSL[1]
        tw = self.wslot_ap[st][:, 0:NCH * 256].rearrange("p (k c) -> p k c", k=NCH)
        self.dma(tw, tw_d.rearrange("p (k c) -> p k c", k=NCH), [], [self.wslot_b[st]], eng="pool")
        if not isB:
            self.dma(mask_ap.rearrange("p m n -> p (m n)"), mask_d, [], [b_mask], eng="pool")
        vit = [0]

        def v_tile(tile):
            pb = vit[0] % 2
            vit[0] += 1
            tt_ = tile // 4
            for k in range(NCH):
                self.mm(ps[pb][:, 0:256], h_t[:, k, tile * 128:(tile + 1) * 128], tw[:, k, :], k == 0, k == NCH - 1,
                        [self.wslot_b[st], self.hb[k][tt_]], [psb[pb]])
            if tile < 16:
                if isB:
                    self.act(vtmp_ap[:, tile, :], ps[pb][:, 0:256], AF.Copy, [], [psb[pb], b_vtmp])
                else:
                    self.act(Vowna_ap[:, tile + 1, :, 0:64], ps[pb][:, 128:256].rearrange("p (kv d) -> p kv d", kv=KV),
                             AF.Copy, [], [psb[pb], b_V])
            else:
                if isB:
                    self.act(VP_ap[:, tile - 16, :], ps[pb][:, voff:voff + VW], AF.Copy, [], [psb[pb], b_VP])
                else:
                    self.act(VPa_ap[:, tile - 16, :, 0:64], ps[pb][:, 128:256].rearrange("p (kv d) -> p kv d", kv=KV),
                             AF.Copy, [], [psb[pb], b_VP])
                self.copy(vf_ap[:, :], ps[pb][:, 0:256], [], [psb[pb], b_vf])
                self.dma(vo_d.rearrange("p (t c) -> p t c", t=4)[:, tile - 16, :], vf_ap[:, :], [b_vf], [])

        vnext = 0
        for t in range(NTG):
            if t < 4:
                load_rope(t)
            for kv in range(KV):
                for k in range(NCH):
                    self.mm(ps[PQ][0:HW, :], kw[:, kv, k, :], h_t[:, k, t * TG:(t + 1) * TG], k == 0, k == NCH - 1,
                            [self.wslot_b[sk], self.hb[k][t]], [psb[PQ]])
                for _ in range(2):
                    v_tile(vnext)
                    vnext += 1
                if t < 4:
                    if isB:
                        dst, dstb = ktmp_ap[0:HW, kv, t * TG:(t + 1) * TG], [b_ktmp]
                    else:
                        dst, dstb = Kown_ap[0:HW, kv, 128 + t * TG:128 + (t + 1) * TG], [b_K]
                    qk_post(ps[PQ][0:HW, :], TG, kg_t if isB else None, True, dst, dstb)
                else:
                    qk_post(ps[PQ][0:HW, :], TG, kg_t if isB else None, False, KP_ap[0:HW, kv, :], [b_KP],
                            f32_out=(kf_ap[0:HW, :], b_kf) if isB else None)
                    if isB:
                        self.dma(ko_d.rearrange("p (kv n) -> p kv n", kv=KV)[:, kv, :], kf_ap[:, :], [b_kf], [])
        assert vnext == 20
        rg = [[2 * i, 2 * i + 1] for i in range(self.ncores // 2)]
        kxin_ap, kxout_ap = kxin.ap(), kxout.ap()
        if isB:
            self.dma(kxin_ap[:, 0:4096], ktmp_ap.rearrange("p kv n -> p (kv n)"), [b_ktmp], [b_kxin])
            self.dma(kxin_ap[:, 4096:8192], vtmp_ap.rearrange("p t c -> p (t c)"), [b_vtmp], [b_kxin])
        else:
            kx4 = kxin_ap[:, 0:512].rearrange("p (kv e s) -> p kv e s", kv=KV, e=2)
            self.dma(kx4[:, :, 0, :], Kown_ap[:, :, 128:256], [b_K], [b_kxin])
            self.dma(kx4[:, :, 1, :], Kown_ap[:, :, 16 * 128:17 * 128], [b_K], [b_kxin])
            self.dma(kxin_ap[:, 512:640].rearrange("p (kv d) -> p kv d", kv=KV), Vowna_ap[:, 1, :, 0:64], [b_V], [b_kxin])
            self.dma(kxin_ap[:, 640:768].rearrange("p (kv d) -> p kv d", kv=KV), Vowna_ap[:, 16, :, 0:64], [b_V], [b_kxin])
        if self.ncores >= 2:
            P.add("pool", lambda e, a=kxin, b=kxout, r=rg: e.collective_compute(
                "AllGather", ALU.bypass, replica_groups=r, ins=[a.ap().opt()], outs=[b.ap().opt()]),
                [b_kxin], [b_kxout], cc=True)
        if isB:
            for r in range(2):
                self.dma(Kall_ap[:, :, (4 + 16 * r) * 128:(4 + 16 * r) * 128 + 2048],
                         kxout_ap[r * 128:(r + 1) * 128, 0:4096].rearrange("p (kv n) -> p kv n", kv=KV),
                         [b_kxout], [b_K])
                self.dma(Vall_ap[:, 4 + 16 * r:4 + 16 * (r + 1), :],
                         kxout_ap[r * 128:(r + 1) * 128, 4096:8192].rearrange("p (t c) -> p t c", t=16),
                         [b_kxout], [b_V])
            self.dma(Kall_ap[:, :, 0:512], ck_d.rearrange("p (kv n) -> p kv n", kv=KV), [], [b_K], eng="pool")
            self.dma(Vall_ap[:, 0:4, :], cv_d.rearrange("p (t c) -> p t c", t=4), [], [b_V], eng="pool")
        else:
            ko4 = kxout_ap[:, 0:512].rearrange("p (kv e s) -> p kv e s", kv=KV, e=2)
            self.dma(Kown_ap[:, :, 0:128], ko4[0:128, :, 1, :], [b_kxout], [b_K])
            self.dma(Kown_ap[:, :, 17 * 128:18 * 128], ko4[128:256, :, 0, :], [b_kxout], [b_K])
            self.dma(Vowna_ap[:, 0, :, 0:64], kxout_ap[0:128, 640:768].rearrange("p (kv d) -> p kv d", kv=KV), [b_kxout], [b_V])
            self.dma(Vowna_ap[:, 17, :, 0:64], kxout_ap[128:256, 512:640].rearrange("p (kv d) -> p kv d", kv=KV), [b_kxout], [b_V])
            self.dma(Kctx_ap[:, :, :], ck_d.rearrange("p (kv n) -> p kv n", kv=KV), [], [b_Kc], eng="pool")
            self.dma(Vctxa_ap[:, :, :, 0:64], cv_d.rearrange("p (t kv d) -> p t kv d", t=4, kv=KV), [], [b_Vc], eng="pool")

        HPS = 4
        R0 = 0
        for t in [4, 0, 1, 2, 3]:
            ctx = self.ctx_of(t)
            if t < 4:
                load_rope(t)
            tl = []
            if t == 4:
                for sq_i in range(2):
                    for j in range(2):
                        kt = sq_i * 2 + j
                        tl.append((lambda kv, kt=kt: KP_ap[R0:R0 + hd, kv, kt * 128:(kt + 1) * 128],
                                   (lambda kv, kt=kt: VP_ap[:, kt, kv * hd:(kv + 1) * hd]) if isB
                                   else (lambda kv, kt=kt: VPa_ap[:, kt, kv, :]),
                                   [], b_KP, b_VP, sq_i * 256, 256))
            elif isB:
                for kt in range(36):
                    tl.append((lambda kv, kt=kt: Kall_ap[0:hd, kv, kt * 128:(kt + 1) * 128],
                               lambda kv, kt=kt: Vall_ap[:, kt, kv * hd:(kv + 1) * hd], [], b_K, b_V, 0, 512))
            else:
                for j in range(4):
                    tl.append((lambda kv, j=j: Kctx_ap[R0:R0 + hd, kv, j * 128:(j + 1) * 128],
                               lambda kv, j=j: Vctxa_ap[:, j, kv, :], [], b_Kc, b_Vc, 0, 512))
                for r in range(-1, 5):
                    tile = t * 4 + r + 1
                    mi = r + 1
                    if t == 0 and r == -1:
                        mi = 6
                    if t == 3 and r == 4:
                        mi = 7
                    c0 = max(0, r - 1) * 128
                    c1 = min(4, r + 2) * 128
                    masks = []
                    for blk in (r + 1, r - 1):
                        if 0 <= blk <= 3:
                            masks.append((mi, blk * 128 - c0, blk * 128))
                    tl.append((lambda kv, tile=tile: Kown_ap[R0:R0 + hd, kv, tile * 128:(tile + 1) * 128],
                               lambda kv, tile=tile: Vowna_ap[:, tile, kv, :], masks, b_K, b_V, c0, c1 - c0))
            sit = 0

            def prep_steps(u):
                n = TG
                rope = t < 4
                out_ap, out_b = Q_ap[u % 2][:, :], [b_Q[u % 2]]
                src = ps[PQ][:, :]
                st = []

                def mmk(k):
                    nonlocal qw, sq_slot
                    if k == 0 and u % HPS == 0:
                        sq_slot = WSL[(u // HPS) % 2]
                        qw = self.wslot_ap[sq_slot][:, 0:HPS * NCH * HW].rearrange("p (h k c) -> p h k c", h=HPS, k=NCH)
                        self.dma(qw, qw_d[u:u + HPS].rearrange("h p (k c) -> p h k c", k=NCH), [], [self.wslot_b[sq_slot]], eng="pool")
                    self.mm(src, qw[:, u % HPS, k, :], h_t[:, k, t * TG:(t + 1) * TG], k == 0, k == NCH - 1,
                            [self.wslot_b[sq_slot], self.hb[k][t]], [psb[PQ]])
                for k in range(NCH):
                    st.append(lambda k=k: mmk(k))
                if isB:
                    st.append(lambda: self.act(raw_ap[0:HW, 0:n], src, AF.Identity, [b_g], [psb[PQ], b_raw], scale=qg_t[0:HW, 0:1]))
                    st.append(lambda: self.act(sq_ap[0:HW, 0:n], src, AF.Square, [], [psb[PQ], b_sq]))
                    st.append(lambda: self.mm(ps[PS2][0:HW, 0:n], self.ones_t[0:HW, 0:HW], sq_ap[0:HW, 0:n], True, True,
                                              [self.b_ones, b_sq], [psb[PS2]]))
                    st.append(lambda: self.act(rstd_ap[0:HW, 0:n], ps[PS2][0:HW, 0:n], AF.Ln, [self.b_eps], [psb[PS2], b_rstd],
                                               scale=1.0 / hd, bias=self.eps_t[0:HW, 0:1]))
                    st.append(lambda: self.act(rstd_ap[0:HW, 0:n], rstd_ap[0:HW, 0:n], AF.Exp, [], [b_rstd], scale=-0.5))
                elif not rope:
                    st.append(lambda: self.act(out_ap, src, AF.Copy, [], [psb[PQ]] + out_b))
                    return st
                else:
                    st.append(lambda: self.act(raw_ap[0:HW, 0:n], src, AF.Copy, [], [psb[PQ], b_raw]))
                if rope:
                    st.append(lambda: self.mm(ps[PS2][0:HW, 0:n], pm_t[0:HW, 0:HW], raw_ap[0:HW, 0:n], True, True,
                                              [b_pm, b_raw], [psb[PS2]]))
                    st.append(lambda: self.tt(t1_ap[0:HW, 0:n], raw_ap[0:HW, 0:n], rC_ap[0:HW, 0:n], ALU.mult, [b_raw, b_rC], [b_t1]))
                    st.append(lambda: self.tt(t2_ap[0:HW, 0:n], ps[PS2][0:HW, 0:n], rS_ap[0:HW, 0:n], ALU.mult, [b_rS], [psb[PS2], b_t2]))
                    if isB:
                        st.append(lambda: self.tt(t1_ap[0:HW, 0:n], t1_ap[0:HW, 0:n], t2_ap[0:HW, 0:n], ALU.add, [b_t2], [b_t1]))
                        st.append(lambda: self.tt(out_ap, t1_ap[0:HW, 0:n], rstd_ap[0:HW, 0:n], ALU.mult, [b_t1, b_rstd], out_b))
                    else:
                        st.append(lambda: self.tt(out_ap, t1_ap[0:HW, 0:n], t2_ap[0:HW, 0:n], ALU.add, [b_t1, b_t2], out_b))
                else:
                    st.append(lambda: self.tt(out_ap, raw_ap[0:HW, 0:n], rstd_ap[0:HW, 0:n], ALU.mult, [b_raw, b_rstd], out_b))
                return st

            def prep_q(u):
                for f_ in prep_steps(u):
                    f_()

            qw, sq_slot = None, None
            steps = []
            prep_q(0)
            MO = hd if isB else 128
            for h in range(HQ):
                kv = h // G
                if isB:
                    u, R0 = h, 0
                    if h + 1 < HQ:
                        steps = prep_steps(h + 1)
                else:
                    u, R0 = h // 2, (h % 2) * 64
                    if h % 2 == 0 and u + 1 < NQ:
                        steps = prep_steps(u + 1)
                qi = u % 2
                pO, pZ = 2 + (h % 2), 4 + (h % 2)

                def emit_qk(ti):
                    Kf, Vf, masks, bK, bV, c0, n = tl[ti]
                    sb_i = (sit + ti) % 2
                    self.mm(ps[sb_i][:, 0:n], Kf(kv), Q_ap[qi][R0:R0 + hd, c0:c0 + n], True, len(masks) == 0,
                            [bK, b_Q[qi]], [psb[sb_i]])
                    for mk, (mi, off, mcol) in enumerate(masks):
                        self.mm(ps[sb_i][:, off:off + 128], ident_t[:, :], mask_ap[:, mi, mcol:mcol + 128], False,
                                mk == len(masks) - 1, [b_id, b_mask], [psb[sb_i]])
                seen_cols = set()
                emit_qk(0)
                for ti, (Kf, Vf, masks, bK, bV, c0, n) in enumerate(tl):
                    sb_i = (sit + ti) % 2
                    last = ti == len(tl) - 1
                    if not last:
                        emit_qk(ti + 1)
                    if steps and ti >= 1:
                        steps.pop(0)()
                    self.act(PT_ap[sb_i][:, 0:n], ps[sb_i][:, 0:n], AF.Exp, [], [psb[sb_i], b_PT[sb_i]], scale=scale)
                    if t == 4:
                        first = c0 not in seen_cols
                        seen_cols.add(c0)
                    else:
                        first = ti == 0
                    stopf = last or (t == 4 and ti == 1)
                    self.mm(ps[pO][0:MO, c0:c0 + n], Vf(kv), PT_ap[sb_i][:, 0:n], first, stopf,
                            [bV, b_PT[sb_i]], [psb[pO]])
                    if isB and t == 4:
                        self.mm(ps[pZ][0:hd, c0:c0 + n], self.ones_t[:, 0:hd], PT_ap[sb_i][:, 0:n], first, stopf,
                                [self.b_ones, b_PT[sb_i]], [psb[pZ]])
                    elif isB:
                        self.mm(ps[pZ][0:hd, :], self.ones_t[:, 0:hd], PT_ap[sb_i][:, :], first, stopf,
                                [self.b_ones, b_PT[sb_i]], [psb[pZ]])
                sit += len(tl)
                if isB or h % 2 == 1:
                    while steps:
                        steps.pop(0)()
                if isB:
                    self.recip(rz_ap[0:hd, :], ps[pZ][0:hd, :], [], [psb[pZ], b_rz])
                    self.tt(O_ap[:, h, :], ps[pO][0:hd, :], rz_ap[0:hd, :], ALU.mult, [b_rz], [psb[pO], b_O[h % 8]])
                else:
                    self.ts(zz_ap[64:128, :], ps[pO][64:128, :], esink_t[64:128, h:h + 1], None, ALU.add, None,
                            [b_esk], [psb[pO], b_zz])
                    self.recip(rz_ap[64:128, :], zz_ap[64:128, :], [b_zz], [b_rz])
                    p0 = (h % 2) * 64
                    self.tt(O_ap[p0:p0 + 64, h // 2, :], ps[pO][0:64, :], rz_ap[64:128, :], ALU.mult,
                            [b_rz], [psb[pO], b_O[h // 2]])

            for d in range(NCH):
                if d % 4 == 0:
                    so = WSL[(d // 4) % 2]
                    ow = self.wslot_ap[so].rearrange("p (d h c) -> p d h c", d=4, h=8)
                    self.dma(ow, ow_d[d:d + 4].rearrange("d p (h c) -> p d h c", h=8), [], [self.wslot_b[so]], eng="pool")
                pob = PQ if d % 2 == 0 else PS2
                for u in range(8):
                    self.mm(ps[pob][:], ow[:, d % 4, u, :], O_ap[:, u, :], u == 0, u == 7,
                            [self.wslot_b[so], b_O[u]], [psb[pob]])
                self.stt(x_t[:, d, t * TG:(t + 1) * TG], ps[pob][:], self.G_t[:, 1, d, ctx:ctx + 1],
                         x_t[:, d, t * TG:(t + 1) * TG], ALU.mult, ALU.add,
                         [self.b_G], [psb[pob], self.xb[d][t]])


def _prep_shared(inputs, need):
    f = np.float32
    sh = {}

    def want(n):
        return n in need

    w_mod = np.asarray(inputs["w_mod"], f)
    b_mod = np.asarray(inputs["b_mod"], f)
    norm_g = np.asarray(inputs["norm_g"], f)
    for l in range(DEPTH):
        if want(f"wm{l}"):
            sh[f"wm{l}"] = np.ascontiguousarray(
                w_mod[l].reshape(NCH, 128, 72, 128).transpose(2, 1, 0, 3)).reshape(72, 128, NCH * 128)
            sh[f"bm{l}"] = np.ascontiguousarray(b_mod[l].reshape(72, 128).T)
            sh[f"gn{l}"] = np.ascontiguousarray(norm_g[l].reshape(3, NCH, 128).transpose(2, 0, 1)).reshape(128, 3 * NCH)
            wg = np.asarray(inputs["ffn_w_gate"][l], f).reshape(2, NCH, 128, NFF, 128)
            wu = np.asarray(inputs["ffn_w_up"][l], f).reshape(2, NCH, 128, NFF, 128)
            wgu = np.stack([wg, wu], axis=1)
            sh[f"wgu{l}"] = np.ascontiguousarray(wgu.transpose(0, 4, 3, 1, 2, 5)).reshape(2, NFF, 128, 2 * NCH * 128)
            sh[f"wd{l}"] = np.ascontiguousarray(np.asarray(inputs["ffn_w_down"][l], f).reshape(2, NFF, 128, D))
    for idx in range(2):
        if want(f"awuv{idx}"):
            w_in = np.asarray(inputs["a_w_in"][idx], f).reshape(NCH, 128, 2, 8, 256)
            sh[f"awuv{idx}"] = np.ascontiguousarray(w_in.transpose(3, 1, 2, 0, 4)).reshape(8, 128, 2 * NCH * 256)
            w_out = np.asarray(inputs["a_w_out"][idx], f).reshape(8, 2, 128, D)
            sh[f"awout{idx}"] = np.ascontiguousarray(w_out.transpose(0, 2, 1, 3)).reshape(8, 128, 2 * D)
            w_s = np.asarray(inputs["a_w_s"][idx], f)
            sh[f"awsT{idx}"] = np.ascontiguousarray(w_s.transpose(2, 0, 1)).reshape(128, 8 * 128)
            sh[f"abs{idx}"] = np.ascontiguousarray(np.asarray(inputs["a_b_s"][idx], f).reshape(1, 8 * 128))
            sh[f"ang{idx}"] = np.ascontiguousarray(np.broadcast_to(np.asarray(inputs["a_norm_g"][idx], f)[None, :], (128, 2048)))
    sh["fg"] = np.ascontiguousarray(np.asarray(inputs["final_g"], f).reshape(NCH, 128).T)
    _prep_attn_shared(inputs, need, sh)
    return sh


def _rope_tables(hd, rank):
    quarter = hd // 4
    t = np.arange(2048) + rank * 2048
    row = (t // 64).astype(np.float32)
    col = (t % 64).astype(np.float32)
    inv = (10000.0 ** (-np.arange(quarter, dtype=np.float32) / quarter)).astype(np.float32)
    tab = np.zeros((2, hd, 2048), np.float32)
    pm = np.zeros((hd, hd), np.float32)
    for d in range(hd):
        region = d // quarter
        i = d % quarter
        pos = row if region < 2 else col
        ang = (pos * inv[i]).astype(np.float32)
        tab[0, d] = np.cos(ang)
        tab[1, d] = np.sin(ang) * (-1.0 if region % 2 == 0 else 1.0)
        partner = d + quarter if region % 2 == 0 else d - quarter
        pm[partner, d] = 1.0
    return tab, pm


def _c_masks(rank):
    s_ = np.arange(128)[:, None]
    q = np.arange(512)[None, :]
    qi, ql = q // 128, q % 128
    m = np.full((8, 128, 512), NEG, np.float32)
    for r in range(-1, 5):
        vis = ((r == qi - 1) & (s_ >= ql)) | (r == qi) | ((r == qi + 1) & (s_ <= ql))
        m[r + 1] = np.where(vis, 0.0, NEG)
    if rank == 1:
        m[6] = m[0]
    if rank == 0:
        m[7] = m[5]
    return np.ascontiguousarray(m.transpose(1, 0, 2)).reshape(128, 8 * 512)


def _prep_attn_shared(inputs, need, sh):
    f = np.float32
    if "b_qw" in need:
        w = np.asarray(inputs["b_w_qkv"][0], f)
        sh["b_qw"] = np.ascontiguousarray(w[:, :1024].reshape(8, 128, 8, 128).transpose(2, 1, 0, 3)).reshape(8, 128, 1024)
        sh["b_kw"] = np.ascontiguousarray(w[:, 1024:1280].reshape(8, 128, 2, 128).transpose(1, 2, 0, 3)).reshape(128, 2048)
        sh["b_tw"] = np.ascontiguousarray(w[:, 1280:1536].reshape(8, 128, 256).transpose(1, 0, 2)).reshape(128, 2048)
        wo = np.asarray(inputs["b_w_out"][0], f)
        sh["b_ow"] = np.ascontiguousarray(wo.reshape(8, 128, 8, 128).transpose(2, 1, 0, 3)).reshape(8, 128, 1024)
        sh["b_qg"] = np.ascontiguousarray(np.asarray(inputs["b_q_g"][0], f).reshape(128, 1))
        sh["b_kg"] = np.ascontiguousarray(np.asarray(inputs["b_k_g"][0], f).reshape(128, 1))
        sh["b_pm"] = _rope_tables(128, 0)[1]
    if "c_qw" in need:
        w = np.asarray(inputs["c_w_qkv"][0], f)
        sh["c_qw"] = np.ascontiguousarray(w[:, :1024].reshape(8, 128, 8, 128).transpose(2, 1, 0, 3)).reshape(8, 128, 1024)
        wk = w[:, 1024:1152].reshape(8, 128, 2, 1, 64)
        wk = np.broadcast_to(wk, (8, 128, 2, 2, 64))
        sh["c_kw"] = np.ascontiguousarray(wk.transpose(1, 2, 0, 3, 4)).reshape(128, 2048)
        sh["c_tw"] = np.ascontiguousarray(w[:, 1024:1280].reshape(8, 128, 256).transpose(1, 0, 2)).reshape(128, 2048)
        wo = np.asarray(inputs["c_w_out"][0], f)
        sh["c_ow"] = np.ascontiguousarray(wo.reshape(8, 128, 8, 128).transpose(2, 1, 0, 3)).reshape(8, 128, 1024)
        pm64 = _rope_tables(64, 0)[1]
        pm = np.zeros((128, 128), f)
        pm[:64, :64] = pm64
        pm[64:, 64:] = pm64
        sh["c_pm"] = pm
        sh["c_ident"] = np.eye(128, dtype=f)
        sh["c_sink"] = np.ascontiguousarray(np.broadcast_to(np.asarray(inputs["c_sink"][0], f)[None, :], (128, 16)))


def _prep_core(inputs, c, need=()):
    f = np.float32
    b, r = c // 2, c % 2
    xs = np.asarray(inputs["x_sample"], f)[b, r * 2048:(r + 1) * 2048]
    xp = np.asarray(inputs["x_prompt"], f)[2 * c:2 * c + 2].reshape(512, D)
    xx = np.concatenate([xs, xp], 0)
    m = {}
    m["xT"] = np.ascontiguousarray(xx.reshape(NTOK, NCH, 128).transpose(2, 1, 0)).reshape(128, NCH * NTOK)
    cc = np.stack([np.asarray(inputs["c"], f)[b], np.asarray(inputs["c_ctx"], f)], 0)
    m["cT"] = np.ascontiguousarray(cc.reshape(2, NCH, 128).transpose(2, 1, 0)).reshape(128, NCH * 2)
    if "b_ck" in need:
        ck = np.asarray(inputs["cache_b_k"], f)[b, 0]
        m["b_ck"] = np.ascontiguousarray(ck.transpose(2, 1, 0)).reshape(128, 1024)
        cv = np.asarray(inputs["cache_b_v"], f)[b, 0].reshape(4, 128, 256)
        m["b_cv"] = np.ascontiguousarray(cv.transpose(1, 0, 2)).reshape(128, 1024)
        m["b_rope"] = _rope_tables(128, r)[0]
    if "c_ck" in need:
        ck = np.asarray(inputs["cache_c_k"], f)[b, 0]
        ckT = np.ascontiguousarray(ck.transpose(2, 1, 0)).reshape(64, 1024)
        m["c_ck"] = np.ascontiguousarray(np.concatenate([ckT, ckT], 0))
        cv = np.asarray(inputs["cache_c_v"], f)[b, 0].reshape(4, 128, 128)
        m["c_cv"] = np.ascontiguousarray(cv.transpose(1, 0, 2)).reshape(128, 512)
        rt = _rope_tables(64, r)[0]
        m["c_rope"] = np.ascontiguousarray(np.concatenate([rt, rt], 1))
        m["c_mask"] = _c_masks(r)
    return m


def run(inputs, stage=99, trace=False, ncores=8):
    bld = Builder(stage, ncores)
    nc = bld.build()
    need = set(bld.din.keys())
    sh = _prep_shared(inputs, need)
    in_maps = []
    for c in range(ncores):
        m = {k: v for k, v in sh.items() if k in need}
        pc = _prep_core(inputs, c, need)
        m.update({k: v for k, v in pc.items() if k in need})
        assert set(m.keys()) == need, (need - set(m.keys()), set(m.keys()) - need)
        in_maps.append(m)
    res = run_bass_kernel_spmd(nc, in_maps, core_ids=list(range(ncores)), trace=trace)
    return res


def kernel(**inputs):
    res = run(inputs)
    f = np.float32
    y_prompt = np.zeros((16, 256, D), f)
    y_sample = np.zeros((4, 4096, D), f)
    nbk = np.zeros((16, 1, 256, 2, 128), f)
    nbv = np.zeros((16, 1, 256, 2, 128), f)
    nck = np.zeros((16, 1, 256, 2, 64), f)
    ncv = np.zeros((16, 1, 256, 2, 64), f)
    for c in range(8):
        o = res.results[c]
        b, r = c // 2, c % 2
        y = np.asarray(o["yT"], f).reshape(128, NCH, NTOK).transpose(2, 1, 0).reshape(NTOK, D)
        y_sample[b, r * 2048:(r + 1) * 2048] = y[:2048]
        y_prompt[2 * c:2 * c + 2] = y[2048:].reshape(2, 256, D)
        ko = np.asarray(o["b_ko"], f).reshape(128, 2, 2, 256)
        nbk[2 * c:2 * c + 2, 0] = ko.transpose(2, 3, 1, 0)
        vo = np.asarray(o["b_vo"], f).reshape(128, 4, 256).transpose(1, 0, 2).reshape(2, 256, 2, 128)
        nbv[2 * c:2 * c + 2, 0] = vo
        kvo = np.asarray(o["c_kvo"], f).reshape(128, 4, 256).transpose(1, 0, 2).reshape(2, 256, 256)
        nck[2 * c:2 * c + 2, 0] = kvo[:, :, :128].reshape(2, 256, 2, 64)
        ncv[2 * c:2 * c + 2, 0] = kvo[:, :, 128:].reshape(2, 256, 2, 64)
    return (y_prompt, y_sample, nbk, nbv, nck, ncv)
```
